# Optimizing a Trainium2 kernel written in Bass

```python
import math
import jax, jax.numpy as jnp
from jax import lax
import numpy as np

D_MODEL = 1024
BATCH = 8
SEQ = 4096
DEPTH = 4

HEAD_DIM = 64
N_MIXERS = 3
SB_HEADS = D_MODEL // HEAD_DIM
FOX_HEADS = D_MODEL // HEAD_DIM
SWA_Q_HEADS = D_MODEL // HEAD_DIM
SWA_KV_HEADS = 4
WINDOW = 128
Q_BLOCK = 128
REL_BUCKETS = 32
REL_MAX_DIST = 128
D_FF = 4 * D_MODEL
PLE_DIM = 256
EPS = 1e-6

kernel_name = "interleaved_sb_fox_swa_hybrid"


def _n_layers_of(kind):
    return len(range(kind, DEPTH, N_MIXERS))


def rms_norm(x, g):
    xf = x.astype(jnp.float32)
    y = xf * lax.rsqrt(jnp.mean(xf * xf, axis=-1, keepdims=True) + EPS)
    return (y * g.astype(jnp.float32)).astype(x.dtype)


def _heads(t, n_heads):
    return t.reshape(t.shape[0], t.shape[1], n_heads, HEAD_DIM)


def stick_breaking_attention(q, k, v):
    B, S, H, Dh = q.shape
    nb = S // Q_BLOCK
    scale = Dh ** -0.5
    qb = q.reshape(B, nb, Q_BLOCK, H, Dh).transpose(1, 0, 2, 3, 4)
    key_pos = jnp.arange(S)

    def block(args):
        q_blk, start = args
        z = jnp.einsum('bqhd,bkhd->bhqk', q_blk, k,
                       preferred_element_type=jnp.float32) * scale
        q_pos = start + jnp.arange(Q_BLOCK)
        strict = key_pos[None, :] < q_pos[:, None]
        log_keep = jnp.where(strict, jax.nn.log_sigmoid(-z), 0.0)
        after = lax.cumsum(log_keep, axis=3, reverse=True) - log_keep
        w = jnp.where(strict, jnp.exp(jax.nn.log_sigmoid(z) + after), 0.0)
        return jnp.einsum('bhqk,bkhd->bqhd', w.astype(v.dtype), v)

    out = lax.map(block, (qb, jnp.arange(nb, dtype=jnp.int32) * Q_BLOCK))
    return out.transpose(1, 0, 2, 3, 4).reshape(B, S, H, Dh)


def forgetting_attention(q, k, v, log_f):
    B, S, H, Dh = q.shape
    nb = S // Q_BLOCK
    scale = Dh ** -0.5
    c = jnp.cumsum(log_f, axis=1).transpose(0, 2, 1)
    cb = c.reshape(B, H, nb, Q_BLOCK).transpose(2, 0, 1, 3)
    qb = q.reshape(B, nb, Q_BLOCK, H, Dh).transpose(1, 0, 2, 3, 4)
    key_pos = jnp.arange(S)

    def block(args):
        q_blk, c_blk, start = args
        s = jnp.einsum('bqhd,bkhd->bhqk', q_blk, k,
                       preferred_element_type=jnp.float32) * scale
        s = s + (c_blk[..., :, None] - c[:, :, None, :])
        q_pos = start + jnp.arange(Q_BLOCK)
        causal = key_pos[None, :] <= q_pos[:, None]
        w = jax.nn.softmax(jnp.where(causal, s, -jnp.inf), axis=-1)
        return jnp.einsum('bhqk,bkhd->bqhd', w.astype(v.dtype), v)

    out = lax.map(block, (qb, cb, jnp.arange(nb, dtype=jnp.int32) * Q_BLOCK))
    return out.transpose(1, 0, 2, 3, 4).reshape(B, S, H, Dh)


def t5_bucket(dist):
    max_exact = REL_BUCKETS // 2
    d = jnp.maximum(dist, 1).astype(jnp.float32)
    large = max_exact + (jnp.log(d / max_exact) / math.log(REL_MAX_DIST / max_exact)
                         * (REL_BUCKETS - max_exact)).astype(jnp.int32)
    large = jnp.minimum(large, REL_BUCKETS - 1)
    return jnp.where(dist < max_exact, dist, large)


def sliding_window_attention(q, k, v, sinks, rel_bias):
    B, S, Hq, Dh = q.shape
    Hkv = k.shape[2]
    G = Hq // Hkv
    nb = S // Q_BLOCK
    scale = Dh ** -0.5
    pad = ((0, 0), (Q_BLOCK, 0), (0, 0), (0, 0))
    kp = jnp.pad(k, pad).reshape(B, nb + 1, Q_BLOCK, Hkv, Dh)
    vp = jnp.pad(v, pad).reshape(B, nb + 1, Q_BLOCK, Hkv, Dh)
    kband = jnp.concatenate([kp[:, :-1], kp[:, 1:]], axis=2).transpose(1, 0, 2, 3, 4)
    vband = jnp.concatenate([vp[:, :-1], vp[:, 1:]], axis=2).transpose(1, 0, 2, 3, 4)
    qb = q.reshape(B, nb, Q_BLOCK, Hkv, G, Dh).transpose(1, 0, 2, 3, 4, 5)
    kpos = (jnp.arange(nb) * Q_BLOCK - Q_BLOCK)[:, None] + jnp.arange(2 * Q_BLOCK)[None, :]

    dist = jnp.arange(Q_BLOCK)[:, None] + Q_BLOCK - jnp.arange(2 * Q_BLOCK)[None, :]
    in_window = (dist >= 0) & (dist < WINDOW)
    bias = rel_bias.astype(jnp.float32)[t5_bucket(jnp.maximum(dist, 0))]
    bias = bias.transpose(2, 0, 1).reshape(Hkv, G, Q_BLOCK, 2 * Q_BLOCK)
    sink_logit = sinks.astype(jnp.float32).reshape(Hkv, G, 1, 1)

    def block(args):
        q_blk, k_blk, v_blk, kp_blk = args
        s = jnp.einsum('bqhgd,bkhd->bhgqk', q_blk, k_blk,
                       preferred_element_type=jnp.float32) * scale + bias
        valid = in_window & (kp_blk >= 0)[None, :]
        s = jnp.where(valid, s, -jnp.inf)
        sink = jnp.broadcast_to(sink_logit, s.shape[:-1] + (1,))
        w = jax.nn.softmax(jnp.concatenate([s, sink], axis=-1), axis=-1)[..., :-1]
        return jnp.einsum('bhgqk,bkhd->bqhgd', w.astype(v_blk.dtype), v_blk)

    out = lax.map(block, (qb, kband, vband, kpos))
    return out.transpose(1, 0, 2, 3, 4, 5).reshape(B, S, Hq, Dh)


def squared_relu_mlp(u, w_up, w_down):
    a = jax.nn.relu(u @ w_up)
    return (a * a) @ w_down


def setup_inputs(seed: int = 0) -> dict:
    key = jax.random.key(seed)
    ks = jax.random.split(key, 20)
    f32 = jnp.float32
    n_sb, n_fox, n_swa = _n_layers_of(0), _n_layers_of(1), _n_layers_of(2)
    d_in = D_MODEL ** -0.5
    attn_w = (SB_HEADS * HEAD_DIM) ** -0.5

    def nrm(k, shape, s):
        return jax.random.normal(k, shape, f32) * s

    fox_cols = 3 * FOX_HEADS * HEAD_DIM + FOX_HEADS
    swa_cols = (SWA_Q_HEADS + 2 * SWA_KV_HEADS) * HEAD_DIM
    return {
        "x": jax.random.normal(ks[0], (BATCH, SEQ, D_MODEL), f32),
        "p": jax.random.normal(ks[1], (DEPTH, BATCH, SEQ, PLE_DIM), f32),
        "attn_norm": 1.0 + nrm(ks[2], (DEPTH, D_MODEL), 0.02),
        "mlp_norm": 1.0 + nrm(ks[3], (DEPTH, D_MODEL), 0.02),
        "ple_norm": 1.0 + nrm(ks[4], (DEPTH, D_MODEL), 0.02),
        "final_norm": 1.0 + nrm(ks[5], (D_MODEL,), 0.02),
        "w_in_sb": nrm(ks[6], (n_sb, D_MODEL, 3 * SB_HEADS * HEAD_DIM), d_in),
        "w_out_sb": nrm(ks[7], (n_sb, SB_HEADS * HEAD_DIM, D_MODEL), attn_w),
        "w_in_fox": nrm(ks[8], (n_fox, D_MODEL, fox_cols), d_in),
        "b_forget": jax.random.uniform(ks[9], (n_fox, FOX_HEADS), f32, 1.0, 6.0),
        "w_out_fox": nrm(ks[10], (n_fox, FOX_HEADS * HEAD_DIM, D_MODEL), attn_w),
        "w_in_swa": nrm(ks[11], (n_swa, D_MODEL, swa_cols), d_in),
        "sinks": nrm(ks[12], (n_swa, SWA_Q_HEADS), 0.5),
        "w_out_swa": nrm(ks[13], (n_swa, SWA_Q_HEADS * HEAD_DIM, D_MODEL), attn_w),
        "rel_bias": nrm(ks[14], (REL_BUCKETS, SWA_Q_HEADS), 0.5),
        "w_up": nrm(ks[15], (DEPTH, D_MODEL, D_FF), d_in),
        "w_down": nrm(ks[16], (DEPTH, D_FF, D_MODEL), D_FF ** -0.5),
        "w_ple": nrm(ks[17], (DEPTH, PLE_DIM, D_MODEL), PLE_DIM ** -0.5),
        "w_ple_gate": nrm(ks[18], (DEPTH, D_MODEL, D_MODEL), d_in),
    }


def reference(x, p, attn_norm, mlp_norm, ple_norm, final_norm, w_in_sb, w_out_sb,
              w_in_fox, b_forget, w_out_fox, w_in_swa, sinks, w_out_swa, rel_bias,
              w_up, w_down, w_ple, w_ple_gate):
    qkv_w = SB_HEADS * HEAD_DIM
    h = x
    for i in range(DEPTH):
        kind = i % N_MIXERS
        j = i // N_MIXERS
        u = rms_norm(h, attn_norm[i])
        if kind == 0:
            proj = u @ w_in_sb[j]
            q = _heads(proj[..., :qkv_w], SB_HEADS)
            k = _heads(proj[..., qkv_w:2 * qkv_w], SB_HEADS)
            v = _heads(proj[..., 2 * qkv_w:], SB_HEADS)
            o = stick_breaking_attention(q, k, v)
            o = o.reshape(o.shape[0], o.shape[1], -1) @ w_out_sb[j]
        elif kind == 1:
            proj = u @ w_in_fox[j]
            fw = FOX_HEADS * HEAD_DIM
            q = _heads(proj[..., :fw], FOX_HEADS)
            k = _heads(proj[..., fw:2 * fw], FOX_HEADS)
            v = _heads(proj[..., 2 * fw:3 * fw], FOX_HEADS)
            f_logit = proj[..., 3 * fw:].astype(jnp.float32) + b_forget[j].astype(jnp.float32)
            o = forgetting_attention(q, k, v, jax.nn.log_sigmoid(f_logit))
            o = o.reshape(o.shape[0], o.shape[1], -1) @ w_out_fox[j]
        else:
            proj = u @ w_in_swa[j]
            qw = SWA_Q_HEADS * HEAD_DIM
            kw = SWA_KV_HEADS * HEAD_DIM
            q = _heads(proj[..., :qw], SWA_Q_HEADS)
            k = _heads(proj[..., qw:qw + kw], SWA_KV_HEADS)
            v = _heads(proj[..., qw + kw:], SWA_KV_HEADS)
            o = sliding_window_attention(q, k, v, sinks[j], rel_bias)
            o = o.reshape(o.shape[0], o.shape[1], -1) @ w_out_swa[j]
        h = h + o
        h = h + squared_relu_mlp(rms_norm(h, mlp_norm[i]), w_up[i], w_down[i])
        gate = jax.nn.sigmoid(rms_norm(h, ple_norm[i]) @ w_ple_gate[i])
        h = h + (p[i] @ w_ple[i]) * gate
    return rms_norm(h, final_norm)
```

```python
from contextlib import ExitStack

import ml_dtypes
import numpy as np

import concourse.bass as bass
import concourse.mybir as mybir
from concourse.bass_utils import run_bass_kernel_spmd

F32 = mybir.dt.float32
BF16 = mybir.dt.bfloat16
AF = mybir.ActivationFunctionType
ALU = mybir.AluOpType

D = 1024
DFF = 4096
PLE = 256
NH = 16
HD = 64
EPS = 1e-6
NEG = -30000.0

ENGS = ("sync", "scalar", "vector", "gpsimd", "tensor")


class Buf:
    __slots__ = ("w", "r")

    def __init__(self):
        self.w = {}
        self.r = {}


class Sched:
    def __init__(self, nc, ctx):
        self.nc = nc
        self.ctx = ctx
        self.ops = {e: [] for e in ENGS}
        self.cur = {}
        self.nsem = 0
        self.dma_pool = {}
        self.dma_rr = {}
        self.n_ops = 0
        self.fence_toks = []

    def fence(self):
        toks = {}
        for s in self.cur.values():
            if s[1] > 0:
                toks[id(s[0])] = (s[0], s[1])
        for pool in self.dma_pool.values():
            for slot in pool:
                if slot[2] is not None:
                    toks[id(slot[0])] = slot[2]
        for t in self.fence_toks:
            k = id(t[0])
            if k not in toks:
                toks[k] = t
        self.fence_toks = list(toks.values())

    def _newsem(self):
        self.nsem += 1
        return self.ctx.enter_context(self.nc.semaphore(f"s{self.nsem}"))

    def _dma_slot(self, eng):
        if eng not in self.dma_pool:
            n = 20 if eng == "sync" else 8
            self.dma_pool[eng] = [[self._newsem(), 0, None] for _ in range(n)]
            self.dma_rr[eng] = 0
        pool = self.dma_pool[eng]
        i = self.dma_rr[eng]
        self.dma_rr[eng] = (i + 1) % len(pool)
        slot = pool[i]
        if slot[1] >= 30000:
            slot[0] = self._newsem()
            slot[1] = 0
            slot[2] = None
        return slot

    def emit(self, eng, fn, reads=(), writes=(), dma=False, disjoint=False):
        waits = {}

        def add(tok):
            if tok is None:
                return
            k = id(tok[0])
            if k not in waits or waits[k][1] < tok[1]:
                waits[k] = tok

        for t in self.fence_toks:
            add(t)
        for b in reads:
            for t in b.w.values():
                add(t)
        for b in writes:
            if not disjoint:
                for t in b.w.values():
                    add(t)
            for t in b.r.values():
                add(t)
        if dma:
            slot = self._dma_slot(eng)
            add(slot[2])
            slot[1] += 16
            tok = (slot[0], slot[1])
            slot[2] = tok
            amt = 16
        else:
            s = self.cur.get(eng)
            if s is None or s[1] >= 30000:
                s = [self._newsem(), 0]
                self.cur[eng] = s
            s[1] += 1
            tok = (s[0], s[1])
            amt = 1
        self.ops[eng].append((fn, list(waits.values()), tok, amt))
        for b in reads:
            k = id(tok[0])
            if k not in b.r or b.r[k][1] < tok[1]:
                b.r[k] = tok
        for b in writes:
            if disjoint and not b.r:
                k = id(tok[0])
                if k not in b.w or b.w[k][1] < tok[1]:
                    b.w[k] = tok
            else:
                b.w = {id(tok[0]): tok}
            b.r = {}
        self.n_ops += 1
        return tok

    def run(self):
        nc = self.nc
        with nc.Block() as block:
            def replay(name):
                def body(e):
                    waited = {}
                    for fn, waits, tok, amt in self.ops[name]:
                        for (sem, val) in waits:
                            k = id(sem)
                            if waited.get(k, 0) >= val:
                                continue
                            e.wait_ge(sem, val)
                            waited[k] = val
                        ins = fn(e)
                        ins.then_inc(tok[0], amt)
                    if self.ops[name]:
                        seen = {}
                        for fn, waits, tok, amt in self.ops[name]:
                            seen[id(tok[0])] = tok
                        for (sem, val) in seen.values():
                            e.wait_ge(sem, val)
                return body

            block.sync(replay("sync"))
            block.scalar(replay("scalar"))
            block.vector(replay("vector"))
            block.gpsimd(replay("gpsimd"))
            block.tensor(replay("tensor"))


class Arena:
    LO = 16640
    HI = 229376 - 256

    def __init__(self, nc):
        self.nc = nc
        self.top = Arena.LO
        self.n = 0

    def mark(self):
        return self.top

    def reset(self, m):
        self.top = m

    def alloc(self, shape, dtype, name="t"):
        esz = 4 if dtype == F32 else 2
        free = 1
        for s in shape[1:]:
            free *= s
        nbytes = (free * esz + 63) // 64 * 64
        off = self.top
        assert off + nbytes <= Arena.HI, f"SBUF overflow allocating {name} {shape}: {off}+{nbytes}"
        self.top += nbytes
        self.n += 1
        return self.nc.alloc_sbuf_tensor_at(f"{name}_{self.n}", list(shape), dtype, offset=off)


class T:
    def __init__(self, t):
        self.t = t
        self.b = Buf()


def host_consts():
    bf = ml_dtypes.bfloat16
    c = {}
    c["ident"] = np.eye(128, dtype=np.float32).astype(bf)
    j = np.arange(128)[:, None]
    s = np.arange(128)[None, :]
    c["tri"] = (-(j >= s).astype(np.float32)).astype(bf)
    e2 = np.zeros((128, 32, 128), np.float32)
    for kb in range(32):
        e2[:, kb, 64 + kb] = 1.0
    c["e2"] = e2.astype(bf)
    ksel = np.zeros((32, 32, 128), np.float32)
    for kp in range(32):
        for kb in range(32):
            if kp > kb:
                ksel[kp, kb, :] = -1.0
    c["ksel"] = ksel.astype(bf)
    sel = np.zeros((128, 32, 128), np.float32)
    for kp in range(64):
        for kb in range(32):
            if (kp % 32) > kb:
                sel[kp, kb, :] = -1.0
    c["selall"] = sel.astype(bf)
    sp = np.arange(128)[:, None, None] + 128 * np.arange(4)[None, :, None]
    tt = np.arange(512)[None, None, :]
    c["mask_lt"] = (sp < tt).astype(np.float32).astype(bf)
    c["mask_le"] = (sp <= tt).astype(np.float32).astype(bf)
    L = 384
    selb = np.zeros((32, L), np.float32)
    maskv = np.zeros((1, L), np.float32)
    for i in range(L):
        dist = i - 127
        if 0 <= dist < 128:
            if dist < 16:
                bkt = dist
            else:
                dd = np.float32(max(dist, 1))
                val = np.log(dd / np.float32(16)) / np.float32(np.log(128 / 16)) * np.float32(16)
                bkt = min(16 + int(np.float32(val)), 31)
            selb[bkt, i] = 1.0
        else:
            maskv[0, i] = NEG
    c["selb"] = selb
    c["maskv"] = maskv
    return c


CONST_SPECS = {
    "ident": ([128, 128], BF16), "tri": ([128, 128], BF16), "e2": ([128, 32, 128], BF16),
    "selall": ([128, 32, 128], BF16), "ksel": ([32, 32, 128], BF16), "mask_lt": ([128, 4, 512], BF16), "mask_le": ([128, 4, 512], BF16),
    "selb": ([32, 384], F32), "maskv": ([1, 384], F32),
}

WEIGHT_SPECS = {
    "attn_norm": [4, D], "mlp_norm": [4, D], "ple_norm": [4, D], "final_norm": [1, D],
    "w_in_sb": [2, D, 3072], "w_out_sb": [2, D, D], "w_in_fox": [1, D, 3088], "b_forget": [16, 1],
    "w_out_fox": [1, D, D], "w_in_swa": [1, D, 1536], "sinks": [1, 16], "w_out_swa": [1, D, D],
    "rel_bias": [32, 16], "w_up": [4, D, DFF], "w_down": [4, DFF, D], "w_ple": [4, PLE, D],
    "w_ple_gate": [4, D, D],
}


def build(layers, S=4096, dbg=False):
    nc = bass.Bass("TRN2", target_bir_lowering=False)
    NG = S // 512
    NKB = S // 128
    last_is_final = (layers[-1] == 3)

    x_in = nc.dram_tensor("x", [S, D], F32, kind="ExternalInput").ap()
    p_in = nc.dram_tensor("p", [4, S, PLE], F32, kind="ExternalInput").ap()
    W = {k: nc.dram_tensor(k, shp, F32, kind="ExternalInput").ap() for k, shp in WEIGHT_SPECS.items()}
    C = {k: nc.dram_tensor(k, shp, dt, kind="ExternalInput").ap() for k, (shp, dt) in CONST_SPECS.items()}
    out = nc.dram_tensor("out", [S, D], F32, kind="ExternalOutput").ap()
    skind = "ExternalOutput" if dbg else "Internal"
    hbuf = nc.dram_tensor("hbuf", [S, D], F32, kind=skind).ap()
    QT = nc.dram_tensor("QT", [D, S], BF16, kind=skind).ap()
    KT = nc.dram_tensor("KT", [D, S], BF16, kind=skind).ap()
    Vd = nc.dram_tensor("Vd", [S, D], BF16, kind=skind).ap()
    OT = nc.dram_tensor("OT", [D, S], BF16, kind=skind).ap()
    AT = nc.dram_tensor("AT", [32, 128, S], BF16, kind=skind).ap()
    A3 = nc.dram_tensor("A3", [16, 3, S], BF16, kind=skind).ap()
    NA3 = nc.dram_tensor("NA3", [16, 3, S], BF16, kind=skind).ap()
    RPD = nc.dram_tensor("RPD", [16, 128, 384], F32, kind=skind).ap()
    dQT, dKT, dV, dOT, dAT, dA3, dRP, dOUT = (Buf() for _ in range(8))
    dH = [Buf() for _ in range(S // 128)]
    dX = [Buf() for _ in range(S // 128)]

    with ExitStack() as ctx:
        ctx.enter_context(nc.allow_low_precision("bf16 matmul operands, fp32 accumulation"))
        sc = Sched(nc, ctx)
        ar = Arena(nc)
        em = sc.emit

        class TV:
            def __init__(self, ap):
                self.t = ap
                self.b = Buf()

        PSW = [T(nc.alloc_psum_tensor(f"psw{i}", [128, 1024], F32)) for i in range(2)]
        PS = [TV(PSW[i // 2].t[:, (i % 2) * 512:(i % 2 + 1) * 512]) for i in range(4)]
        PS += [T(nc.alloc_psum_tensor(f"ps{i}", [128, 512], F32)) for i in range(4, 7)]
        PST = T(nc.alloc_psum_tensor("pst", [128, 1024], BF16))

        def cload(name, shape, dt):
            t = T(ar.alloc(shape, dt, name))
            em("sync", lambda e, t=t, name=name: e.dma_start(out=t.t[:], in_=C[name]), writes=[t.b], dma=True)
            return t

        ident = cload("ident", [128, 128], BF16)
        tri = cload("tri", [128, 128], BF16)
        e2 = cload("e2", [128, 32, 128], BF16)
        mask_lt = cload("mask_lt", [128, 4, 512], BF16)
        mask_le = cload("mask_le", [128, 4, 512], BF16)
        ones_bf = T(ar.alloc([128, 128], BF16, "ones"))
        em("gpsimd", lambda e: e.memset(ones_bf.t[:], 1.0), writes=[ones_bf.b])
        neghalf = T(ar.alloc([128, 4], F32, "neghalf"))
        em("gpsimd", lambda e: e.memset(neghalf.t[:], -0.5), writes=[neghalf.b])
        persist_mark = ar.mark()

        def bcast_rows(ap_row, n):
            return bass.AP(ap_row.tensor, ap_row.offset, [[0, n]] + [list(x) for x in ap_row.ap[1:]])

        class WT:
            def __init__(self, t, ncols, cw):
                self.t = t
                self.cw = cw
                self.bufs = [Buf() for _ in range((ncols + cw - 1) // cw)]

            def rb(self, c0, c1):
                return self.bufs[c0 // self.cw:(c1 - 1) // self.cw + 1]

        def load_w_bf16(wap, ncols, name, k_chunks, cw=512):
            t = WT(ar.alloc([128, k_chunks, ncols], BF16, name), ncols, cw)
            src = wap.rearrange("(kc p) n -> p kc n", p=128)
            kstep = max(1, min(k_chunks, 4096 // cw))
            for ci, c0 in enumerate(range(0, ncols, cw)):
                c1 = min(ncols, c0 + cw)
                for k0 in range(0, k_chunks, kstep):
                    k1 = min(k_chunks, k0 + kstep)
                    em("gpsimd", lambda e, k0=k0, k1=k1, c0=c0, c1=c1: e.dma_start(
                        out=t.t[:, k0:k1, c0:c1], in_=src[:, k0:k1, c0:c1]),
                       writes=[t.bufs[ci]], dma=True, disjoint=True)
            return t

        def rmsnorm_tile(h_ap, hbufs, gam, ubf, ss, ms, rstd, junk, col):
            em("scalar", lambda e: e.activation(out=junk.t[:], in_=h_ap, func=AF.Square,
                                                 accum_out=ss.t[:, col:col + 1]),
               reads=hbufs, writes=[junk.b, ss.b])
            em("gpsimd", lambda e: e.tensor_scalar(out=ms.t[:, col:col + 1], in0=ss.t[:, col:col + 1],
                                                    scalar1=1.0 / D, scalar2=EPS, op0=ALU.mult, op1=ALU.add),
               reads=[ss.b], writes=[ms.b])
            em("gpsimd", lambda e: e.tensor_tensor(out=rstd.t[:, col:col + 1], in0=ms.t[:, col:col + 1],
                                                    in1=neghalf.t[:, 0:1], op=ALU.pow),
               reads=[ms.b, neghalf.b], writes=[rstd.b])
            em("vector", lambda e: e.scalar_tensor_tensor(out=ubf.t[:], in0=h_ap, scalar=rstd.t[:, col:col + 1],
                                                          in1=gam.t[:], op0=ALU.mult, op1=ALU.mult),
               reads=hbufs + [rstd.b, gam.b], writes=[ubf.b])

        def transpose_to(ubf, nchunk, dstT, dst_cols):
            def pe(e):
                ins = None
                for c in range(nchunk):
                    ins = e.transpose(out=PST.t[:, c * 128:(c + 1) * 128], in_=ubf.t[:, c * 128:(c + 1) * 128],
                                      identity=ident.t[:])
                return ins
            em("tensor", pe, reads=[ubf.b, ident.b], writes=[PST.b])
            em("scalar", lambda e: e.activation(
                out=dstT.t[:, 0:nchunk, dst_cols],
                in_=PST.t[:, 0:nchunk * 128].rearrange("p (c t) -> p c t", t=128), func=AF.Copy),
               reads=[PST.b], writes=[dstT.b])

        evac_rr = [0]

        def evac(dst_ap, dst_bufs, ps, scale=None, eng=None):
            if eng is None:
                eng = "scalar" if evac_rr[0] % 2 == 0 else "vector"
                evac_rr[0] += 1
            if eng == "scalar":
                if scale is None:
                    em("scalar", lambda e: e.activation(out=dst_ap, in_=ps.t[:], func=AF.Copy),
                       reads=[ps.b], writes=dst_bufs)
                else:
                    em("scalar", lambda e: e.activation(out=dst_ap, in_=ps.t[:], func=AF.Copy, scale=scale),
                       reads=[ps.b], writes=dst_bufs)
            else:
                if scale is None:
                    em("vector", lambda e: e.tensor_copy(out=dst_ap, in_=ps.t[:]), reads=[ps.b], writes=dst_bufs)
                else:
                    em("vector", lambda e: e.tensor_scalar(out=dst_ap, in0=ps.t[:], scalar1=scale, scalar2=None,
                                                            op0=ALU.mult),
                       reads=[ps.b], writes=dst_bufs)

        def stage_A(li, hsrc, hsrc_bufs):
            kind = li % 3
            j = li // 3
            ar.reset(persist_mark)
            sc.fence()
            if kind == 0:
                w_ap, ncols, nq, nk, vcols = W["w_in_sb"][j], 3072, 8, 8, 1024
            elif kind == 1:
                w_ap, ncols, nq, nk, vcols = W["w_in_fox"][j], 3088, 8, 8, 1024
            else:
                w_ap, ncols, nq, nk, vcols = W["w_in_swa"][j], 1536, 8, 2, 256
            voff = (nq + nk) * 128
            if kind == 1:
                negb = T(ar.alloc([16, 1], F32, "negb"))
                em("sync", lambda e: e.dma_start(out=negb.t[:], in_=W["b_forget"]), writes=[negb.b], dma=True)
                em("gpsimd", lambda e: e.tensor_scalar(out=negb.t[:], in0=negb.t[:], scalar1=-1.0, scalar2=None,
                                                        op0=ALU.mult), reads=[negb.b], writes=[negb.b])
                aT = T(ar.alloc([16, S], F32, "aT"))
                spt = [T(ar.alloc([16, 512], F32, "spt")) for _ in range(2)]
                ones16 = T(ar.alloc([16, 512], F32, "ones16"))
                em("gpsimd", lambda e: e.memset(ones16.t[:], 1.0), writes=[ones16.b])
                tail_mark = ar.mark()
            wsb = load_w_bf16(w_ap, ncols, "w_in", 8)
            gam = T(ar.alloc([128, D], F32, "gam"))
            em("sync", lambda e: e.dma_start(out=gam.t[:], in_=bcast_rows(W["attn_norm"][li:li + 1, :], 128)),
               writes=[gam.b], dma=True)
            ht = [T(ar.alloc([128, 4, D], F32, "ht")) for _ in range(2)]
            ubf = [T(ar.alloc([128, D], BF16, "ubf")) for _ in range(2)]
            junk = T(ar.alloc([128, D], BF16, "junk"))
            uT = [T(ar.alloc([128, 8, 512], BF16, "uT")) for _ in range(2)]
            qko = [T(ar.alloc([128, 512], BF16, "qko")) for _ in range(3)]
            vo = [T(ar.alloc([128, vcols], BF16, "vo")) for _ in range(2)]
            ss = [T(ar.alloc([128, 4], F32, "ss")) for _ in range(2)]
            ms = [T(ar.alloc([128, 4], F32, "ms")) for _ in range(2)]
            rstd = [T(ar.alloc([128, 4], F32, "rstd")) for _ in range(2)]
            st = {"nqo": 0, "pi": 0}

            def load(g):
                hb = ht[g % 2]
                em("sync", lambda e: e.dma_start(
                    out=hb.t[:], in_=hsrc[g * 512:(g + 1) * 512, :].rearrange("(t p) d -> p t d", p=128)),
                   reads=hsrc_bufs[4 * g:4 * g + 4], writes=[hb.b], dma=True)

            def norm(g, t):
                hb = ht[g % 2]
                rmsnorm_tile(hb.t[:, t, :], [hb.b], gam, ubf[t % 2], ss[g % 2], ms[g % 2], rstd[g % 2], junk, t)

            def trans(g, t):
                transpose_to(ubf[t % 2], 8, uT[g % 2], slice(t * 128, (t + 1) * 128))

            def qk_block(g, fb):
                u_t = uT[g % 2]
                ps = PS[st["pi"] % 4]
                st["pi"] += 1

                def pe(e):
                    ins = None
                    for kc in range(8):
                        ins = e.matmul(ps.t[:], wsb.t[:, kc, fb * 128:(fb + 1) * 128], u_t.t[:, kc, :],
                                       start=(kc == 0), stop=(kc == 7))
                    return ins
                em("tensor", pe, reads=wsb.rb(fb * 128, (fb + 1) * 128) + [u_t.b], writes=[ps.b])
                qo = qko[st["nqo"] % 3]
                st["nqo"] += 1
                evac(qo.t[:], [qo.b], ps, scale=(0.125 if fb < nq else None))
                if fb < nq:
                    dst, db = QT[fb * 128:(fb + 1) * 128, g * 512:(g + 1) * 512], dQT
                else:
                    dst, db = KT[(fb - nq) * 128:(fb - nq + 1) * 128, g * 512:(g + 1) * 512], dKT
                em("sync", lambda e: e.dma_start(out=dst, in_=qo.t[:]), reads=[qo.b], writes=[db], dma=True, disjoint=True)

            def v_block(g, t):
                u_t = uT[g % 2]
                vt = vo[t % 2]
                for c0 in range(0, vcols, 512):
                    cw = min(512, vcols - c0)
                    ps = PS[st["pi"] % 4]
                    st["pi"] += 1

                    def pe(e, c0=c0, cw=cw, ps=ps):
                        ins = None
                        for kc in range(8):
                            ins = e.matmul(ps.t[:, 0:cw], u_t.t[:, kc, t * 128:(t + 1) * 128],
                                           wsb.t[:, kc, voff + c0:voff + c0 + cw], start=(kc == 0), stop=(kc == 7))
                        return ins
                    em("tensor", pe, reads=wsb.rb(voff + c0, voff + c0 + cw) + [u_t.b], writes=[ps.b])
                    eng = "scalar" if evac_rr[0] % 2 == 0 else "vector"
                    evac_rr[0] += 1
                    if eng == "scalar":
                        em("scalar", lambda e, ps=ps, c0=c0, cw=cw: e.activation(
                            out=vt.t[:, c0:c0 + cw], in_=ps.t[:, 0:cw], func=AF.Copy), reads=[ps.b], writes=[vt.b])
                    else:
                        em("vector", lambda e, ps=ps, c0=c0, cw=cw: e.tensor_copy(
                            out=vt.t[:, c0:c0 + cw], in_=ps.t[:, 0:cw]), reads=[ps.b], writes=[vt.b])
                r0 = g * 512 + t * 128
                em("sync", lambda e: e.dma_start(out=Vd[r0:r0 + 128, 0:vcols], in_=vt.t[:]),
                   reads=[vt.b], writes=[dV], dma=True, disjoint=True)

            def fox_block(g):
                u_t = uT[g % 2]
                ps = PS[4]

                def pe(e):
                    ins = None
                    for kc in range(8):
                        ins = e.matmul(ps.t[0:16, :], wsb.t[:, kc, 3072:3088], u_t.t[:, kc, :],
                                       start=(kc == 0), stop=(kc == 7))
                    return ins
                em("tensor", pe, reads=wsb.rb(3072, 3088) + [u_t.b], writes=[ps.b])
                sp_ = spt[g % 2]
                em("scalar", lambda e: e.activation(out=sp_.t[:], in_=ps.t[0:16, :], func=AF.Softplus,
                                                     bias=negb.t[:], scale=-1.0),
                   reads=[ps.b, negb.b], writes=[sp_.b])
                init = 0.0 if g == 0 else aT.t[:, g * 512 - 1:g * 512]
                em("vector", lambda e: e.tensor_tensor_scan(
                    out=aT.t[:, g * 512:(g + 1) * 512], data0=ones16.t[:], data1=sp_.t[:], initial=init,
                    op0=ALU.mult, op1=ALU.add), reads=[sp_.b, ones16.b, aT.b], writes=[aT.b])

            load(0)
            for t in range(4):
                norm(0, t)
                trans(0, t)
            for g in range(NG):
                work = [("qk", fb) for fb in range(nq + nk)] + [("v", t) for t in range(4)]
                if kind == 1:
                    work.append(("f", 0))
                nxt = g + 1 < NG
                if nxt:
                    load(g + 1)
                per = (len(work) + 3) // 4
                for part in range(4):
                    if nxt:
                        norm(g + 1, part)
                    for kind_, a in work[part * per:(part + 1) * per]:
                        if kind_ == "qk":
                            qk_block(g, a)
                        elif kind_ == "v":
                            v_block(g, a)
                        else:
                            fox_block(g)
                    if nxt:
                        trans(g + 1, part)
            if kind == 1:
                sc.fence()
                ar.reset(tail_mark)
                a3 = T(ar.alloc([16, 3, S], BF16, "a3"))
                r1 = T(ar.alloc([16, S], F32, "r1"))
                v = "vector"
                em(v, lambda e: e.tensor_copy(out=a3.t[:, 0, :], in_=aT.t[:]), reads=[aT.b], writes=[a3.b])
                em(v, lambda e: e.tensor_tensor(out=r1.t[:], in0=aT.t[:], in1=a3.t[:, 0, :], op=ALU.subtract),
                   reads=[aT.b, a3.b], writes=[r1.b])
                em(v, lambda e: e.tensor_copy(out=a3.t[:, 1, :], in_=r1.t[:]), reads=[r1.b], writes=[a3.b])
                em(v, lambda e: e.tensor_tensor(out=r1.t[:], in0=r1.t[:], in1=a3.t[:, 1, :], op=ALU.subtract),
                   reads=[r1.b, a3.b], writes=[r1.b])
                em(v, lambda e: e.tensor_copy(out=a3.t[:, 2, :], in_=r1.t[:]), reads=[r1.b], writes=[a3.b])
                em("sync", lambda e: e.dma_start(out=A3, in_=a3.t[:]), reads=[a3.b], writes=[dA3], dma=True, disjoint=True)
                na3 = T(ar.alloc([16, 3, S], BF16, "na3"))
                em(v, lambda e: e.tensor_scalar(out=na3.t[:], in0=a3.t[:], scalar1=-1.0, scalar2=None, op0=ALU.mult),
                   reads=[a3.b], writes=[na3.b])
                em("sync", lambda e: e.dma_start(out=NA3, in_=na3.t[:]), reads=[na3.b], writes=[dA3], dma=True, disjoint=True)

        def stage_B_full(li):
            kind = li % 3
            fox = (kind == 1)
            ar.reset(persist_mark)
            sc.fence()
            kta = [[T(ar.alloc([128, S], BF16, "kta")) for _ in range(2)] for _ in range(2)]
            qta = [[T(ar.alloc([128, S], BF16, "qta")) for _ in range(2)] for _ in range(2)]
            vp = [T(ar.alloc([128, NKB, 128], BF16, "vp")) for _ in range(2)]
            for s_ in range(2):
                for hh in range(2):
                    for tt_ in (kta[s_][hh], qta[s_][hh]):
                        em("vector", lambda e, tt_=tt_: e.memset(tt_.t[64:128, :], 0.0), writes=[tt_.b])
                        if fox:
                            em("vector", lambda e, tt_=tt_: e.memset(tt_.t[64:70, :], 1.0), writes=[tt_.b])
            if not fox:
                Lt = ar.alloc([128, NKB, 512], BF16, "Lt")
                Ltb = [Buf() for _ in range(NKB)]
                T2 = T(ar.alloc([128, 512], BF16, "T2"))
                em("vector", lambda e: e.memset(T2.t[:], 0.0), writes=[T2.b])
            wt = [T(ar.alloc([128, 512], BF16, "wt")) for _ in range(6)]
            otile = [T(ar.alloc([128, 512], BF16, "otile")) for _ in range(3)]
            rec = [T(ar.alloc([128, 512], F32, "rec")) for _ in range(2)]
            pvq = []
            mask = mask_le if fox else mask_lt
            wti = [0]
            if fox:
                ps_s = [PS[0], PS[1], PS[2]]
                ps_o = [PS[3], PS[4]]
                ps_d = [PS[5], PS[6]]
            else:
                ps_z = [PS[0], PS[1]]
                ps_T = PS[2]
                ps_c = [PS[3], PS[4]]
                ps_o = [PS[5], PS[6]]
            cnt = {"z": 0, "c": 0, "o": 0, "s": 0}

            def load_pair(hp, sset):
                for hh in range(2):
                    h = 2 * hp + hh
                    k_t, q_t = kta[sset][hh], qta[sset][hh]
                    em("sync", lambda e, k_t=k_t, h=h: e.dma_start(out=k_t.t[0:64, :], in_=KT[h * 64:(h + 1) * 64, :]),
                       reads=[dKT], writes=[k_t.b], dma=True)
                    em("sync", lambda e, q_t=q_t, h=h: e.dma_start(out=q_t.t[0:64, :], in_=QT[h * 64:(h + 1) * 64, :]),
                       reads=[dQT], writes=[q_t.b], dma=True)
                    if fox:
                        em("sync", lambda e, k_t=k_t, h=h: e.dma_start(out=k_t.t[64:67, :], in_=A3[h]),
                           reads=[dA3], writes=[k_t.b], dma=True)
                        em("sync", lambda e, q_t=q_t, h=h: e.dma_start(out=q_t.t[67:70, :], in_=NA3[h]),
                           reads=[dA3], writes=[q_t.b], dma=True)
                v_t = vp[sset]
                em("sync", lambda e, v_t=v_t, hp=hp: e.dma_start(
                    out=v_t.t[:], in_=Vd[:, hp * 128:(hp + 1) * 128].rearrange("(kb p) c -> p kb c", p=128)),
                   reads=[dV], writes=[v_t.b], dma=True)

            def do_tile(hp, i, sset, v_t):
                if True:
                    nkb = 4 * i + 4
                    qs = slice(i * 512, (i + 1) * 512)
                    ot = otile[(hp * NG + i) % 3]
                    for hh in range(2):
                        do_head(hp, i, sset, v_t, nkb, qs, ot, hh)
                    pvq.append(lambda: em("sync", lambda e: e.dma_start(out=OT[hp * 128:(hp + 1) * 128, qs], in_=ot.t[:]),
                                          reads=[ot.b], writes=[dOT], dma=True, disjoint=True))
                    if not fox:
                        while pvq:
                            pvq.pop(0)()

            def do_head(hp, i, sset, v_t, nkb, qs, ot, hh):
                if True:
                    if True:
                        k_t, q_t = kta[sset][hh], qta[sset][hh]
                        rows = slice(hh * 64, (hh + 1) * 64)
                        pso = ps_o[cnt["o"] % 2]
                        cnt["o"] += 1
                        if fox:
                            psd = ps_d[(cnt["o"] - 1) % 2]
                            q0 = i * 512
                            for kb in range(nkb):
                                ks = slice(kb * 128, (kb + 1) * 128)
                                pss = ps_s[cnt["s"] % 3]
                                cnt["s"] += 1
                                jj = kb - 4 * i
                                c0 = max(jj, 0) * 128
                                em("tensor", lambda e, pss=pss, ks=ks, c0=c0: e.matmul(
                                    pss.t[:, c0:], k_t.t[:, ks], q_t.t[:, q0 + c0:q0 + 512], start=True, stop=True),
                                   reads=[k_t.b, q_t.b], writes=[pss.b])
                                w_ = wt[wti[0] % len(wt)]
                                wti[0] += 1
                                em("scalar", lambda e, w_=w_, pss=pss, c0=c0: e.activation(
                                    out=w_.t[:, c0:], in_=pss.t[:, c0:], func=AF.Exp),
                                   reads=[pss.b], writes=[w_.b])
                                if jj >= 0:
                                    em("vector", lambda e, w_=w_, jj=jj, c0=c0: e.scalar_tensor_tensor(
                                        out=w_.t[:, c0:], in0=w_.t[:, c0:], scalar=3.0e38, in1=mask.t[:, jj, c0:],
                                        op0=ALU.min, op1=ALU.mult),
                                       reads=[w_.b, mask.b], writes=[w_.b])

                                def pv(pk=kb, pw=w_, pc0=c0):
                                    def pe(e):
                                        e.matmul(pso.t[:, pc0:], v_t.t[:, pk, :], pw.t[:, pc0:], start=(pk == 0), stop=(pk == nkb - 1))
                                        return e.matmul(psd.t[:, pc0:], ones_bf.t[:], pw.t[:, pc0:], start=(pk == 0), stop=(pk == nkb - 1))
                                    em("tensor", pe, reads=[v_t.b, pw.b, ones_bf.b], writes=[pso.b, psd.b])
                                pvq.append(pv)
                                while len(pvq) > 2:
                                    pvq.pop(0)()
                            rc = rec[hh]

                            def epi():
                                em("vector", lambda e: e.reciprocal(out=rc.t[rows, :], in_=psd.t[rows, :]),
                                   reads=[psd.b], writes=[rc.b])
                                em("vector", lambda e: e.tensor_tensor(
                                    out=ot.t[rows, :], in0=pso.t[rows, :], in1=rc.t[rows, :], op=ALU.mult),
                                   reads=[pso.b, rc.b], writes=[ot.b])
                            pvq.append(epi)
                        else:
                            prev = None
                            for kb in range(nkb + 1):
                                cur = None
                                if kb < nkb:
                                    ks = slice(kb * 128, (kb + 1) * 128)
                                    psz = ps_z[cnt["z"] % 2]
                                    cnt["z"] += 1
                                    em("tensor", lambda e, psz=psz, k_t=k_t, q_t=q_t, ks=ks: e.matmul(
                                        psz.t[:], k_t.t[:, ks], q_t.t[:, qs], start=True, stop=True),
                                       reads=[k_t.b, q_t.b], writes=[psz.b])
                                    em("scalar", lambda e, psz=psz, kb=kb: e.activation(
                                        out=Lt[:, kb, :], in_=psz.t[:], func=AF.Softplus),
                                       reads=[psz.b], writes=[Ltb[kb]])
                                    jj = kb - 4 * i
                                    if jj >= 0:
                                        em("vector", lambda e, kb=kb, jj=jj: e.tensor_tensor(
                                            out=Lt[:, kb, :], in0=Lt[:, kb, :], in1=mask.t[:, jj, :], op=ALU.mult),
                                           reads=[Ltb[kb], mask.b], writes=[Ltb[kb]])
                                    cur = kb
                                if prev is not None:
                                    pk = prev
                                    em("tensor", lambda e, pk=pk: e.matmul(
                                        ps_T.t[:], e2.t[:, pk, :], Lt[:, pk, :], start=(pk == 0), stop=(pk == nkb - 1)),
                                       reads=[e2.b, Ltb[pk]], writes=[ps_T.b])
                                prev = cur
                            em("vector", lambda e: e.tensor_copy(out=T2.t[0:64, :], in_=ps_T.t[0:64, :]), reads=[ps_T.b], writes=[T2.b])
                            em("vector", lambda e: e.tensor_tensor(out=T2.t[32:64, :], in0=ps_T.t[32:64, :],
                                                                   in1=T2.t[32:64, :], op=ALU.subtract),
                               reads=[ps_T.b, T2.b], writes=[T2.b])
                            prev = None
                            for kb in range(nkb + 1):
                                cur = None
                                if kb < nkb:
                                    ks = slice(kb * 128, (kb + 1) * 128)
                                    psc = ps_c[cnt["c"] % 2]
                                    cnt["c"] += 1

                                    def pe(e, psc=psc, kb=kb, ks=ks, k_t=k_t, q_t=q_t):
                                        e.matmul(psc.t[:], tri.t[:], Lt[:, kb, :], start=True, stop=False)
                                        e.matmul(psc.t[:], selall.t[:, kb, :], T2.t[:], start=False, stop=False)
                                        return e.matmul(psc.t[:], k_t.t[:, ks], q_t.t[:, qs], start=False, stop=True)
                                    em("tensor", pe, reads=[tri.b, Ltb[kb], selall.b, T2.b, k_t.b, q_t.b], writes=[psc.b])
                                    w_ = wt[wti[0] % 3]
                                    wti[0] += 1
                                    em("scalar", lambda e, w_=w_, psc=psc: e.activation(
                                        out=w_.t[:], in_=psc.t[:], func=AF.Exp),
                                       reads=[psc.b], writes=[w_.b])
                                    jj = kb - 4 * i
                                    if jj >= 0:
                                        em("vector", lambda e, w_=w_, jj=jj: e.scalar_tensor_tensor(
                                            out=w_.t[:], in0=w_.t[:], scalar=3.0e38, in1=mask.t[:, jj, :],
                                            op0=ALU.min, op1=ALU.mult),
                                           reads=[w_.b, mask.b], writes=[w_.b])
                                    cur = (kb, w_)
                                if prev is not None:
                                    pk, pw = prev
                                    em("tensor", lambda e, pk=pk, pw=pw, pso=pso, v_t=v_t: e.matmul(
                                        pso.t[:], v_t.t[:, pk, :], pw.t[:], start=(pk == 0), stop=(pk == nkb - 1)),
                                       reads=[v_t.b, pw.b], writes=[pso.b])
                                prev = cur
                            em("vector", lambda e, pso=pso, ot=ot, rows=rows: e.tensor_copy(out=ot.t[rows, :], in_=pso.t[rows, :]),
                               reads=[pso.b], writes=[ot.b])

            load_pair(0, 0)
            for hp in range(NH // 2):
                sset = hp % 2
                if hp + 1 < NH // 2:
                    load_pair(hp + 1, 1 - sset)
                v_t = vp[sset]
                for i in range(NG):
                    do_tile(hp, i, sset, v_t)
                while pvq:
                    pvq.pop(0)()


        def stage_B_sb(li):
            ar.reset(persist_mark)
            sc.fence()
            kta = [[T(ar.alloc([128, S], BF16, "kta")) for _ in range(2)] for _ in range(2)]
            qta = [[T(ar.alloc([128, S], BF16, "qta")) for _ in range(2)] for _ in range(2)]
            vp = [T(ar.alloc([128, NKB, 128], BF16, "vp")) for _ in range(2)]
            for s_ in range(2):
                for hh in range(2):
                    k_ = kta[s_][hh]
                    em("vector", lambda e, k_=k_: e.memset(k_.t[64:128, :], 0.0), writes=[k_.b])
                    em("sync", lambda e, k_=k_: e.dma_start(
                        out=k_.t[64:96, :].rearrange("p (kb s) -> p kb s", s=128), in_=C["ksel"][:, 0:NKB, :]),
                       writes=[k_.b], dma=True)
            Lt = [ar.alloc([128, NKB, 512], BF16, "Lt") for _ in range(2)]
            Ltb = [[Buf() for _ in range(NKB)] for _ in range(2)]
            wt = [T(ar.alloc([128, 2, 512], BF16, "wt")) for _ in range(4)]
            otile = [T(ar.alloc([128, 512], BF16, "otile")) for _ in range(2)]
            ps_T = [PS[4], PS[4]]
            ps_o = [PS[5], PS[6]]
            cnt = {"zc": 0, "w": 0}

            def load_pair(hp, sset):
                for hh in range(2):
                    h = 2 * hp + hh
                    k_t, q_t = kta[sset][hh], qta[sset][hh]
                    em("sync", lambda e, k_t=k_t, h=h: e.dma_start(out=k_t.t[0:64, :], in_=KT[h * 64:(h + 1) * 64, :]),
                       reads=[dKT], writes=[k_t.b], dma=True)
                    em("vector", lambda e, q_t=q_t: e.memset(q_t.t[64:128, :], 0.0), writes=[q_t.b])
                    em("sync", lambda e, q_t=q_t, h=h: e.dma_start(out=q_t.t[0:64, :], in_=QT[h * 64:(h + 1) * 64, :]),
                       reads=[dQT], writes=[q_t.b], dma=True)
                v_t = vp[sset]
                em("sync", lambda e, v_t=v_t, hp=hp: e.dma_start(
                    out=v_t.t[:], in_=Vd[:, hp * 128:(hp + 1) * 128].rearrange("(kb p) c -> p kb c", p=128)),
                   reads=[dV], writes=[v_t.b], dma=True)


            def units(i):
                return [(kb, kb + 1) for kb in range(0, 4 * i, 2)] + [(kb,) for kb in range(4 * i, 4 * i + 4)]

            def phase_a(i, hh, k_t, q_t):
                nkb = 4 * i + 4
                q0 = i * 512
                L, Lb, pT = Lt[hh], Ltb[hh], ps_T[hh]
                prev = None
                for u in units(i) + [None]:
                    if u is not None:
                        W_ = PSW[cnt["zc"] % 2]
                        cnt["zc"] += 1
                        jj = u[0] - 4 * i
                        c0 = max(jj, 0) * 128

                        def pe(e, u=u, W_=W_, c0=c0):
                            ins = None
                            for n_, kb in enumerate(u):
                                ins = e.matmul(W_.t[:, n_ * 512 + c0:(n_ + 1) * 512], k_t.t[:, kb * 128:(kb + 1) * 128],
                                               q_t.t[:, q0 + c0:q0 + 512], start=True, stop=True)
                            return ins
                        em("tensor", pe, reads=[k_t.b, q_t.b], writes=[W_.b])
                        if len(u) == 2:
                            em("scalar", lambda e, u=u, W_=W_: e.activation(
                                out=L[:, u[0]:u[0] + 2, :], in_=W_.t[:].rearrange("p (a c) -> p a c", c=512), func=AF.Softplus),
                               reads=[W_.b], writes=[Lb[u[0]], Lb[u[1]]])
                        else:
                            kb = u[0]
                            em("scalar", lambda e, kb=kb, W_=W_, c0=c0: e.activation(
                                out=L[:, kb, c0:], in_=W_.t[:, c0:512], func=AF.Softplus), reads=[W_.b], writes=[Lb[kb]])
                            em("vector", lambda e, kb=kb, jj=jj, c0=c0: e.tensor_tensor(
                                out=L[:, kb, c0:], in0=L[:, kb, c0:], in1=mask_lt.t[:, jj, c0:], op=ALU.mult),
                               reads=[Lb[kb], mask_lt.b], writes=[Lb[kb]])
                    if prev is not None:
                        pu, pc0 = prev

                        def peT(e, pu=pu, pc0=pc0):
                            ins = None
                            for pk in pu:
                                ins = e.matmul(pT.t[:, pc0:], e2.t[:, pk, :], L[:, pk, pc0:], start=(pk == 0), stop=(pk == nkb - 1))
                            return ins
                        em("tensor", peT, reads=[e2.b] + [Lb[pk] for pk in pu], writes=[pT.b])
                    prev = (u, c0) if u is not None else None
                em("vector", lambda e: e.tensor_copy(out=q_t.t[64:96, q0:q0 + 512], in_=pT.t[64:96, :]),
                   reads=[pT.b], writes=[q_t.b])

            def phase_b(i, hh, k_t, q_t, v_t, ot):
                nkb = 4 * i + 4
                q0 = i * 512
                L, Lb, pso = Lt[hh], Ltb[hh], ps_o[hh]
                rows = slice(hh * 64, (hh + 1) * 64)
                prev = None
                for u in units(i) + [None]:
                    if u is not None:
                        W_ = PSW[cnt["zc"] % 2]
                        cnt["zc"] += 1
                        jj = u[0] - 4 * i
                        c0 = max(jj, 0) * 128

                        def pe(e, u=u, W_=W_, c0=c0):
                            ins = None
                            for n_, kb in enumerate(u):
                                o_ = W_.t[:, n_ * 512 + c0:(n_ + 1) * 512]
                                e.matmul(o_, tri.t[:], L[:, kb, c0:], start=True, stop=False)
                                ins = e.matmul(o_, k_t.t[:, kb * 128:(kb + 1) * 128], q_t.t[:, q0 + c0:q0 + 512],
                                               start=False, stop=True)
                            return ins
                        em("tensor", pe, reads=[tri.b, k_t.b, q_t.b] + [Lb[kb] for kb in u], writes=[W_.b])
                        w_ = wt[cnt["w"] % 4]
                        cnt["w"] += 1
                        if len(u) == 2:
                            em("scalar", lambda e, w_=w_, W_=W_: e.activation(
                                out=w_.t[:], in_=W_.t[:].rearrange("p (a c) -> p a c", c=512), func=AF.Exp),
                               reads=[W_.b], writes=[w_.b])
                        else:
                            em("scalar", lambda e, w_=w_, W_=W_, c0=c0: e.activation(
                                out=w_.t[:, 0, c0:], in_=W_.t[:, c0:512], func=AF.Exp), reads=[W_.b], writes=[w_.b])
                            em("vector", lambda e, w_=w_, jj=jj, c0=c0: e.scalar_tensor_tensor(
                                out=w_.t[:, 0, c0:], in0=w_.t[:, 0, c0:], scalar=3.0e38, in1=mask_lt.t[:, jj, c0:],
                                op0=ALU.min, op1=ALU.mult), reads=[w_.b, mask_lt.b], writes=[w_.b])
                    if prev is not None:
                        pu, pw, pc0 = prev

                        def pv(e, pu=pu, pw=pw, pc0=pc0):
                            ins = None
                            for n_, pk in enumerate(pu):
                                ins = e.matmul(pso.t[:, pc0:], v_t.t[:, pk, :], pw.t[:, n_, pc0:],
                                               start=(pk == 0), stop=(pk == nkb - 1))
                            return ins
                        em("tensor", pv, reads=[v_t.b, pw.b], writes=[pso.b])
                    prev = (u, w_, c0) if u is not None else None
                em("vector", lambda e: e.tensor_copy(out=ot.t[rows, :], in_=pso.t[rows, :]), reads=[pso.b], writes=[ot.b])

            def do_tile(hp, i, sset):
                v_t = vp[sset]
                ot = otile[(hp * NG + i) % 2]
                qs = slice(i * 512, (i + 1) * 512)
                for hh in range(2):
                    phase_a(i, hh, kta[sset][hh], qta[sset][hh])
                for hh in range(2):
                    phase_b(i, hh, kta[sset][hh], qta[sset][hh], v_t, ot)
                em("sync", lambda e: e.dma_start(out=OT[hp * 128:(hp + 1) * 128, qs], in_=ot.t[:]),
                   reads=[ot.b], writes=[dOT], dma=True, disjoint=True)

            load_pair(0, 0)
            for hp in range(NH // 2):
                sset = hp % 2
                if hp + 1 < NH // 2:
                    load_pair(hp + 1, 1 - sset)
                for i in range(NG):
                    do_tile(hp, i, sset)

        def stage_B_fox(li):
            ar.reset(persist_mark)
            sc.fence()
            kta = [[T(ar.alloc([128, S], BF16, "kta")) for _ in range(2)] for _ in range(2)]
            qta = [[T(ar.alloc([128, S], BF16, "qta")) for _ in range(2)] for _ in range(2)]
            vp = [[T(ar.alloc([128, NKB, 128], BF16, "vp")) for _ in range(2)] for _ in range(2)]
            for s_ in range(2):
                for hh in range(2):
                    for tt_ in (kta[s_][hh], qta[s_][hh]):
                        em("vector", lambda e, tt_=tt_: e.memset(tt_.t[64:128, :], 0.0), writes=[tt_.b])
                        em("vector", lambda e, tt_=tt_: e.memset(tt_.t[64:70, :], 1.0), writes=[tt_.b])
            wt = [T(ar.alloc([128, 2, 512], BF16, "wt")) for _ in range(5)]
            otile = [T(ar.alloc([128, 512], BF16, "otile")) for _ in range(3)]
            rec = [T(ar.alloc([128, 512], F32, "rec")) for _ in range(2)]
            rec2 = [T(ar.alloc([128, 512], F32, "rec2")) for _ in range(2)]
            pvq = []
            ps_o = [PS[4], PS[5], PS[6]]
            cnt = {"w": 0, "s": 0, "o": 0}

            def load_pair(hp, sset):
                for hh in range(2):
                    h = 2 * hp + hh
                    k_t, q_t, v_t = kta[sset][hh], qta[sset][hh], vp[sset][hh]
                    em("sync", lambda e, k_t=k_t, h=h: e.dma_start(out=k_t.t[0:64, :], in_=KT[h * 64:(h + 1) * 64, :]),
                       reads=[dKT], writes=[k_t.b], dma=True)
                    em("sync", lambda e, q_t=q_t, h=h: e.dma_start(out=q_t.t[0:64, :], in_=QT[h * 64:(h + 1) * 64, :]),
                       reads=[dQT], writes=[q_t.b], dma=True)
                    em("sync", lambda e, k_t=k_t, h=h: e.dma_start(out=k_t.t[64:67, :], in_=A3[h]),
                       reads=[dA3], writes=[k_t.b], dma=True)
                    em("sync", lambda e, q_t=q_t, h=h: e.dma_start(out=q_t.t[67:70, :], in_=NA3[h]),
                       reads=[dA3], writes=[q_t.b], dma=True)
                    em("sync", lambda e, v_t=v_t, hp=hp: e.dma_start(
                        out=v_t.t[:], in_=Vd[:, hp * 128:(hp + 1) * 128].rearrange("(kb p) c -> p kb c", p=128)),
                       reads=[dV], writes=[v_t.b], dma=True)
                    oc = slice(64, 128) if hh == 0 else slice(0, 64)
                    em("vector", lambda e, v_t=v_t, oc=oc: e.memset(v_t.t[:, :, oc], 1.0), writes=[v_t.b])

            def units(i):
                return [(kb, kb + 1) for kb in range(0, 4 * i, 2)] + [(kb,) for kb in range(4 * i, 4 * i + 4)]

            def do_head(hp, i, sset, hh, ot):
                nkb = 4 * i + 4
                q0 = i * 512
                k_t, q_t, v_t = kta[sset][hh], qta[sset][hh], vp[sset][hh]
                orow = slice(hh * 64, (hh + 1) * 64)
                drow = slice(64, 128) if hh == 0 else slice(0, 64)
                pso = ps_o[cnt["o"] % 3]
                cnt["o"] += 1
                for u in units(i):
                    W_ = PSW[cnt["s"] % 2]
                    cnt["s"] += 1
                    jj = u[0] - 4 * i
                    c0 = max(jj, 0) * 128

                    def pe(e, u=u, W_=W_, c0=c0):
                        ins = None
                        for n_, kb in enumerate(u):
                            ins = e.matmul(W_.t[:, n_ * 512 + c0:(n_ + 1) * 512], k_t.t[:, kb * 128:(kb + 1) * 128],
                                           q_t.t[:, q0 + c0:q0 + 512], start=True, stop=True)
                        return ins
                    em("tensor", pe, reads=[k_t.b, q_t.b], writes=[W_.b])
                    w_ = wt[cnt["w"] % len(wt)]
                    cnt["w"] += 1
                    if len(u) == 2:
                        em("scalar", lambda e, w_=w_, W_=W_: e.activation(
                            out=w_.t[:], in_=W_.t[:].rearrange("p (a c) -> p a c", c=512), func=AF.Exp),
                           reads=[W_.b], writes=[w_.b])
                    else:
                        em("scalar", lambda e, w_=w_, W_=W_, c0=c0: e.activation(
                            out=w_.t[:, 0, c0:], in_=W_.t[:, c0:512], func=AF.Exp), reads=[W_.b], writes=[w_.b])
                        em("vector", lambda e, w_=w_, jj=jj, c0=c0: e.scalar_tensor_tensor(
                            out=w_.t[:, 0, c0:], in0=w_.t[:, 0, c0:], scalar=3.0e38, in1=mask_le.t[:, jj, c0:],
                            op0=ALU.min, op1=ALU.mult), reads=[w_.b, mask_le.b], writes=[w_.b])

                    def pv(pu=u, pw=w_, pc0=c0):
                        def pe2(e):
                            ins = None
                            for n_, pk in enumerate(pu):
                                ins = e.matmul(pso.t[:, pc0:], v_t.t[:, pk, :], pw.t[:, n_, pc0:],
                                               start=(pk == 0), stop=(pk == nkb - 1))
                            return ins
                        em("tensor", pe2, reads=[v_t.b, pw.b], writes=[pso.b])
                    pvq.append(pv)
                    while len(pvq) > 2:
                        pvq.pop(0)()
                rc, rc2 = rec[hh], rec2[hh]

                def epi():
                    em("vector", lambda e: e.reciprocal(out=rc.t[drow, :], in_=pso.t[drow, :]), reads=[pso.b], writes=[rc.b])
                    em("vector", lambda e: e.tensor_copy(out=rc2.t[orow, :], in_=rc.t[drow, :]), reads=[rc.b], writes=[rc2.b])
                    em("vector", lambda e: e.tensor_tensor(out=ot.t[orow, :], in0=pso.t[orow, :], in1=rc2.t[orow, :],
                                                           op=ALU.mult), reads=[pso.b, rc2.b], writes=[ot.b])
                pvq.append(epi)

            load_pair(0, 0)
            for hp in range(NH // 2):
                sset = hp % 2
                if hp + 1 < NH // 2:
                    load_pair(hp + 1, 1 - sset)
                for i in range(NG):
                    ot = otile[(hp * NG + i) % 3]
                    qs = slice(i * 512, (i + 1) * 512)
                    for hh in range(2):
                        do_head(hp, i, sset, hh, ot)
                    pvq.append(lambda hp=hp, qs=qs, ot=ot: em(
                        "sync", lambda e: e.dma_start(out=OT[hp * 128:(hp + 1) * 128, qs], in_=ot.t[:]),
                        reads=[ot.b], writes=[dOT], dma=True, disjoint=True))
                while pvq:
                    pvq.pop(0)()

        def stage_B_swa(li):
            j = li // 3
            ar.reset(persist_mark)
            sc.fence()
            relb = T(ar.alloc([32, 16], F32, "relb"))
            selb = T(ar.alloc([32, 384], F32, "selb"))
            maskv = T(ar.alloc([1, 384], F32, "maskv"))
            ones1 = T(ar.alloc([1, 16], F32, "ones1"))
            rp = T(ar.alloc([16, 384], F32, "rp"))
            em("sync", lambda e: e.dma_start(out=relb.t[:], in_=W["rel_bias"]), writes=[relb.b], dma=True)
            em("sync", lambda e: e.dma_start(out=selb.t[:], in_=C["selb"]), writes=[selb.b], dma=True)
            em("sync", lambda e: e.dma_start(out=maskv.t[:], in_=C["maskv"]), writes=[maskv.b], dma=True)
            em("gpsimd", lambda e: e.memset(ones1.t[:], 1.0), writes=[ones1.b])

            def pe(e):
                e.matmul(PS[0].t[0:16, 0:384], relb.t[:], selb.t[:], start=True, stop=False)
                return e.matmul(PS[0].t[0:16, 0:384], ones1.t[:], maskv.t[:], start=False, stop=True)
            em("tensor", pe, reads=[relb.b, selb.b, ones1.b, maskv.b], writes=[PS[0].b])
            em("vector", lambda e: e.tensor_copy(out=rp.t[:], in_=PS[0].t[0:16, 0:384]), reads=[PS[0].b], writes=[rp.b])
            for r0 in range(0, 128, 32):
                src = bass.AP(rp.t[:].tensor, rp.t[:].offset, [list(rp.t[:].ap[0]), [0, 32], [1, 384]])
                em("sync", lambda e, r0=r0, src=src: e.dma_start(out=RPD[:, r0:r0 + 32, :], in_=src),
                   reads=[rp.b], writes=[dRP], dma=True, disjoint=True)
            biasT = T(ar.alloc([128, 16, 2, 128], F32, "biasT"))
            for h in range(16):
                src = bass.AP(RPD.tensor, RPD.offset + h * 128 * 384 + 127, [[383, 128], [128, 2], [1, 128]])
                em("sync", lambda e, h=h, src=src: e.dma_start(out=biasT.t[:, h, :, :], in_=src),
                   reads=[dRP], writes=[biasT.b], dma=True)
            bias_hi = T(ar.alloc([128, 16, 2, 128], BF16, "bias_hi"))
            bias_lo = T(ar.alloc([128, 16, 2, 128], BF16, "bias_lo"))
            em("vector", lambda e: e.tensor_copy(out=bias_hi.t[:], in_=biasT.t[:]), reads=[biasT.b], writes=[bias_hi.b])
            em("vector", lambda e: e.tensor_tensor(out=biasT.t[:], in0=biasT.t[:], in1=bias_hi.t[:], op=ALU.subtract),
               reads=[biasT.b, bias_hi.b], writes=[biasT.b])
            em("vector", lambda e: e.tensor_copy(out=bias_lo.t[:], in_=biasT.t[:]), reads=[biasT.b], writes=[bias_lo.b])
            esk = T(ar.alloc([64, 16], F32, "esk"))
            em("sync", lambda e: e.dma_start(out=esk.t[:], in_=bcast_rows(W["sinks"][j:j + 1, :], 64)),
               writes=[esk.b], dma=True)
            em("scalar", lambda e: e.activation(out=esk.t[:], in_=esk.t[:], func=AF.Exp), reads=[esk.b], writes=[esk.b])
            q4 = [T(ar.alloc([128, 4, S], BF16, "q4")) for _ in range(2)]
            kg = [T(ar.alloc([128, S], BF16, "kg")) for _ in range(2)]
            vg = [T(ar.alloc([128, NKB, 128], BF16, "vg")) for _ in range(2)]
            for s_ in range(2):
                em("vector", lambda e, s_=s_: e.memset(q4[s_].t[64:128, :, :], 0.0), writes=[q4[s_].b])
                em("vector", lambda e, s_=s_: e.memset(kg[s_].t[64:128, :], 0.0), writes=[kg[s_].b])
                em("vector", lambda e, s_=s_: e.memset(vg[s_].t[:], 0.0), writes=[vg[s_].b])
            ssb = [T(ar.alloc([128, 512], F32, "ssb")) for _ in range(4)]
            wt = [T(ar.alloc([128, 512], BF16, "wt")) for _ in range(5)]
            den = [T(ar.alloc([64, 512], F32, "den")) for _ in range(2)]
            ot4 = [T(ar.alloc([64, 4, 512], BF16, "ot4")) for _ in range(2)]
            ps_s = [PS[0], PS[1], PS[2]]
            ps_o = [PS[3], PS[4]]
            ps_d = [PS[5], PS[6]]
            ci = {"s": 0, "w": 0, "o": 0, "sb": 0}

            def load_g(g, sset):
                em("sync", lambda e: e.dma_start(
                    out=q4[sset].t[0:64, :, :], in_=QT[g * 256:(g + 1) * 256, :].rearrange("(j d) t -> d j t", d=64)),
                   reads=[dQT], writes=[q4[sset].b], dma=True)
                em("sync", lambda e: e.dma_start(out=kg[sset].t[0:64, :], in_=KT[g * 64:(g + 1) * 64, :]),
                   reads=[dKT], writes=[kg[sset].b], dma=True)
                em("sync", lambda e: e.dma_start(
                    out=vg[sset].t[:, :, 0:64], in_=Vd[:, g * 64:(g + 1) * 64].rearrange("(kb p) c -> p kb c", p=128)),
                   reads=[dV], writes=[vg[sset].b], dma=True)

            do_qblock = make_do_qblock(ot4, ps_o, ps_d, ps_s, ci, ssb, wt, den, (bias_hi, bias_lo), esk)
            load_g(0, 0)
            for g in range(4):
                sset = g % 2
                if g + 1 < 4:
                    load_g(g + 1, 1 - sset)
                q_t, k_t, v_t = q4[sset], kg[sset], vg[sset]
                do_qblock(g, q_t, k_t, v_t, NKB)

        def _unused():
            pass

        def make_do_qblock(ot4, ps_o, ps_d, ps_s, ci, ssb, wt, den, biasT, esk):
            def front(g, n, q_t, k_t, v_t):
                qcols = slice(n * 128, (n + 1) * 128)
                blocks = [(n, 0)] + ([(n - 1, 1)] if n > 0 else [])
                ws = []
                for bi, (kb, c) in enumerate(blocks):
                    ks = slice(kb * 128, (kb + 1) * 128)
                    pss = ps_s[ci["s"] % 3]
                    ci["s"] += 1
                    b_hi, b_lo = biasT

                    def pe(e, pss=pss, ks=ks, c=c):
                        o3 = pss.t[:].rearrange("p (j t) -> p j t", t=128)
                        e.matmul(o3, k_t.t[:, ks], q_t.t[:, :, qcols], start=True, stop=False)
                        e.matmul(o3, ident.t[:], b_hi.t[:, 4 * g:4 * g + 4, c, :], start=False, stop=False)
                        return e.matmul(o3, ident.t[:], b_lo.t[:, 4 * g:4 * g + 4, c, :], start=False, stop=True)
                    em("tensor", pe, reads=[k_t.b, q_t.b, ident.b, b_hi.b, b_lo.b], writes=[pss.b])
                    w_ = wt[ci["w"] % len(wt)]
                    ci["w"] += 1
                    em("scalar", lambda e, w_=w_, pss=pss: e.activation(out=w_.t[:], in_=pss.t[:], func=AF.Exp),
                       reads=[pss.b], writes=[w_.b])
                    ws.append((kb, w_))
                return ws

            def back(g, n, v_t, ws):
                o4 = ot4[(n // 4) % 2]
                pso = ps_o[ci["o"] % 2]
                psd = ps_d[ci["o"] % 2]
                ci["o"] += 1
                for bi, (kb, w_) in enumerate(ws):
                    first, lastb = (bi == 0), (bi == len(ws) - 1)

                    def pe(e, w_=w_, kb=kb, first=first, lastb=lastb):
                        e.matmul(pso.t[:], v_t.t[:, kb, :], w_.t[:], start=first, stop=lastb)
                        return e.matmul(psd.t[:], ones_bf.t[:], w_.t[:], start=first, stop=lastb)
                    em("tensor", pe, reads=[v_t.b, w_.b, ones_bf.b], writes=[pso.b, psd.b])
                dn = den[n % 2]
                esb = bass.AP(esk.t[:].tensor, esk.t[:, 4 * g:4 * g + 4].offset,
                              [list(esk.t[:].ap[0]), [1, 4], [0, 128]])
                em("vector", lambda e: e.tensor_tensor(
                    out=dn.t[:].rearrange("p (j t) -> p j t", t=128),
                    in0=psd.t[0:64, :].rearrange("p (j t) -> p j t", t=128), in1=esb, op=ALU.add),
                   reads=[psd.b, esk.b], writes=[dn.b])
                em("vector", lambda e: e.reciprocal(out=dn.t[:], in_=dn.t[:]), reads=[dn.b], writes=[dn.b])
                c4 = slice((n % 4) * 128, (n % 4 + 1) * 128)
                em("vector", lambda e: e.tensor_tensor(
                    out=o4.t[:, :, c4], in0=pso.t[0:64, :].rearrange("p (j t) -> p j t", t=128),
                    in1=dn.t[:].rearrange("p (j t) -> p j t", t=128), op=ALU.mult),
                   reads=[pso.b, dn.b], writes=[o4.b])
                if n % 4 == 3:
                    n0 = (n - 3) * 128
                    em("sync", lambda e: e.dma_start(
                        out=OT[g * 256:(g + 1) * 256, n0:n0 + 512].rearrange("(j d) t -> d j t", d=64),
                        in_=o4.t[:]), reads=[o4.b], writes=[dOT], dma=True, disjoint=True)

            def do_group(g, q_t, k_t, v_t, nkb):
                ws = front(g, 0, q_t, k_t, v_t)
                for n in range(nkb):
                    ws_next = front(g, n + 1, q_t, k_t, v_t) if n + 1 < nkb else None
                    back(g, n, v_t, ws)
                    ws = ws_next
            return do_group

        def stage_CU(li, hsrc, hsrc_bufs):
            kind = li % 3
            j = li // 3
            ar.reset(persist_mark)
            sc.fence()
            wo_ap = {0: W["w_out_sb"], 1: W["w_out_fox"], 2: W["w_out_swa"]}[kind][j]
            wout = load_w_bf16(wo_ap, D, "wout", 8)
            wup = load_w_bf16(W["w_up"][li], DFF, "wup", 8)
            gam = T(ar.alloc([128, D], F32, "gam"))
            em("sync", lambda e: e.dma_start(out=gam.t[:], in_=bcast_rows(W["mlp_norm"][li:li + 1, :], 128)),
               writes=[gam.b], dma=True)
            ott = [T(ar.alloc([128, 8, 512], BF16, "ott")) for _ in range(2)]
            ht = [T(ar.alloc([128, 4, D], F32, "ht")) for _ in range(2)]
            ubf = [T(ar.alloc([128, D], BF16, "ubf")) for _ in range(2)]
            junk = T(ar.alloc([128, D], BF16, "junk"))
            uT = [T(ar.alloc([128, 8, 512], BF16, "uT")) for _ in range(2)]
            rt = [T(ar.alloc([128, 512], F32, "rt")) for _ in range(2)]
            ao = [T(ar.alloc([128, 512], BF16, "ao")) for _ in range(3)]
            ss = [T(ar.alloc([128, 4], F32, "ss")) for _ in range(2)]
            ms = [T(ar.alloc([128, 4], F32, "ms")) for _ in range(2)]
            rstd = [T(ar.alloc([128, 4], F32, "rstd")) for _ in range(2)]
            st = {"pi": 0, "ai": 0}

            def load(g):
                hb, ob = ht[g % 2], ott[g % 2]
                em("sync", lambda e: e.dma_start(
                    out=hb.t[:], in_=hsrc[g * 512:(g + 1) * 512, :].rearrange("(t p) d -> p t d", p=128)),
                   reads=hsrc_bufs[4 * g:4 * g + 4], writes=[hb.b], dma=True)
                em("sync", lambda e: e.dma_start(
                    out=ob.t[:], in_=OT[:, g * 512:(g + 1) * 512].rearrange("(c p) t -> p c t", p=128)),
                   reads=[dOT], writes=[ob.b], dma=True)

            def head(g, t):
                hb, ob = ht[g % 2], ott[g % 2]
                for half in range(2):
                    ps = PS[st["pi"] % 4]
                    st["pi"] += 1

                    def pe(e, ps=ps, half=half):
                        ins = None
                        for c in range(8):
                            ins = e.matmul(ps.t[:], ob.t[:, c, t * 128:(t + 1) * 128],
                                           wout.t[:, c, half * 512:(half + 1) * 512], start=(c == 0), stop=(c == 7))
                        return ins
                    em("tensor", pe, reads=[ob.b] + wout.rb(half * 512, (half + 1) * 512), writes=[ps.b])
                    em("vector", lambda e, ps=ps, half=half: e.tensor_tensor(
                        out=hb.t[:, t, half * 512:(half + 1) * 512], in0=ps.t[:],
                        in1=hb.t[:, t, half * 512:(half + 1) * 512], op=ALU.add),
                       reads=[ps.b, hb.b], writes=[hb.b])
                rmsnorm_tile(hb.t[:, t, :], [hb.b], gam, ubf[t % 2], ss[g % 2], ms[g % 2], rstd[g % 2], junk, t)

            def trans(g, t):
                transpose_to(ubf[t % 2], 8, uT[g % 2], slice(t * 128, (t + 1) * 128))

            def store(g):
                hb = ht[g % 2]
                em("sync", lambda e: e.dma_start(
                    out=hbuf[g * 512:(g + 1) * 512, :].rearrange("(t p) d -> p t d", p=128), in_=hb.t[:]),
                   reads=[hb.b], writes=dH[4 * g:4 * g + 4], dma=True)

            def up(g, fc):
                u_t = uT[g % 2]
                ps = PS[4 + (st["pi"] % 3)]
                st["pi"] += 1

                def pe(e):
                    ins = None
                    for kc in range(8):
                        ins = e.matmul(ps.t[:], wup.t[:, kc, fc * 128:(fc + 1) * 128], u_t.t[:, kc, :],
                                       start=(kc == 0), stop=(kc == 7))
                    return ins
                em("tensor", pe, reads=wup.rb(fc * 128, (fc + 1) * 128) + [u_t.b], writes=[ps.b])
                r_ = rt[fc % 2]
                em("scalar", lambda e: e.activation(out=r_.t[:], in_=ps.t[:], func=AF.Relu), reads=[ps.b], writes=[r_.b])
                a_ = ao[st["ai"] % 3]
                st["ai"] += 1
                em("vector", lambda e: e.tensor_tensor(out=a_.t[:], in0=r_.t[:], in1=r_.t[:], op=ALU.mult),
                   reads=[r_.b], writes=[a_.b])
                em("sync", lambda e: e.dma_start(out=AT[fc, :, g * 512:(g + 1) * 512], in_=a_.t[:]),
                   reads=[a_.b], writes=[dAT], dma=True, disjoint=True)

            load(0)
            for t in range(4):
                head(0, t)
                trans(0, t)
            store(0)
            for g in range(NG):
                nxt = g + 1 < NG
                if nxt:
                    load(g + 1)
                for part in range(4):
                    if nxt:
                        head(g + 1, part)
                    for fc in range(part * 8, part * 8 + 8):
                        up(g, fc)
                    if nxt:
                        trans(g + 1, part)
                if nxt:
                    store(g + 1)

        def stage_CD(li, final):
            ar.reset(persist_mark)
            sc.fence()
            wdn = load_w_bf16(W["w_down"][li], D, "wdn", 32)
            wgt = load_w_bf16(W["w_ple_gate"][li], D, "wgt", 8)
            wpl = load_w_bf16(W["w_ple"][li], D, "wpl", 2)
            gam = T(ar.alloc([128, D], F32, "gam"))
            em("sync", lambda e: e.dma_start(out=gam.t[:], in_=bcast_rows(W["ple_norm"][li:li + 1, :], 128)),
               writes=[gam.b], dma=True)
            if final:
                gamf = T(ar.alloc([128, D], F32, "gamf"))
                em("sync", lambda e: e.dma_start(out=gamf.t[:], in_=bcast_rows(W["final_norm"][0:1, :], 128)),
                   writes=[gamf.b], dma=True)
            TG = 256
            TPG = TG // 128
            ntile = S // 128
            ain = [T(ar.alloc([128, 32, TG], BF16, "ain")) for _ in range(2)]
            ht = [T(ar.alloc([128, D], F32, "ht")) for _ in range(3)]
            pt = [T(ar.alloc([128, PLE], F32, "pt")) for _ in range(2)]
            pbf = [T(ar.alloc([128, PLE], BF16, "pbf")) for _ in range(2)]
            pT = [T(ar.alloc([128, 2, 128], BF16, "pT")) for _ in range(2)]
            ubf = [T(ar.alloc([128, D], BF16, "ubf")) for _ in range(2)]
            junk = T(ar.alloc([128, D], BF16, "junk"))
            uT = [T(ar.alloc([128, 8, 128], BF16, "uT")) for _ in range(2)]
            gate = [T(ar.alloc([128, D], F32, "gate")) for _ in range(2)]
            ofin = [T(ar.alloc([128, D], F32, "ofin")) for _ in range(2)]
            ss = [T(ar.alloc([128, 4], F32, "ss")) for _ in range(2)]
            ms = [T(ar.alloc([128, 4], F32, "ms")) for _ in range(2)]
            rstd = [T(ar.alloc([128, 4], F32, "rstd")) for _ in range(2)]
            st = {"pi": 0}

            def load_ain(g):
                a_in = ain[g % 2]
                em("sync", lambda e: e.dma_start(
                    out=a_in.t[:], in_=AT[:, :, g * TG:(g + 1) * TG].rearrange("c p t -> p c t")),
                   reads=[dAT], writes=[a_in.b], dma=True)

            def head(ti):
                g, t = ti // TPG, ti % TPG
                a_in = ain[g % 2]
                if t == 0 and (g + 1) * TG < S:
                    load_ain(g + 1)
                r0 = ti * 128
                hb, p_ = ht[ti % 3], pt[ti % 2]
                em("sync", lambda e: e.dma_start(out=hb.t[:], in_=hbuf[r0:r0 + 128, :]),
                   reads=[dH[ti]], writes=[hb.b], dma=True)
                em("sync", lambda e: e.dma_start(out=p_.t[:], in_=p_in[li, r0:r0 + 128, :]), writes=[p_.b], dma=True)
                for half in range(2):
                    ps = PS[st["pi"] % 4]
                    st["pi"] += 1

                    def pe(e, ps=ps, half=half):
                        ins = None
                        for fc in range(32):
                            ins = e.matmul(ps.t[:], a_in.t[:, fc, t * 128:(t + 1) * 128],
                                           wdn.t[:, fc, half * 512:(half + 1) * 512], start=(fc == 0), stop=(fc == 31))
                        return ins
                    em("tensor", pe, reads=[a_in.b] + wdn.rb(half * 512, (half + 1) * 512), writes=[ps.b])
                    em("vector", lambda e, ps=ps, half=half: e.tensor_tensor(
                        out=hb.t[:, half * 512:(half + 1) * 512], in0=ps.t[:],
                        in1=hb.t[:, half * 512:(half + 1) * 512], op=ALU.add),
                       reads=[ps.b, hb.b], writes=[hb.b])
                pb = pbf[ti % 2]
                em("gpsimd", lambda e: e.tensor_copy(out=pb.t[:], in_=p_.t[:]), reads=[p_.b], writes=[pb.b])

            def tail(ti):
                r0 = ti * 128
                hb = ht[ti % 3]
                ub = ubf[ti % 2]
                sidx = ti % 2
                rmsnorm_tile(hb.t[:], [hb.b], gam, ub, ss[sidx], ms[sidx], rstd[sidx], junk, 0)
                u_t = uT[ti % 2]
                transpose_to(ub, 8, u_t, slice(0, 128))
                p_T = pT[ti % 2]
                transpose_to(pbf[ti % 2], 2, p_T, slice(0, 128))
                gt = gate[ti % 2]
                for half in range(2):
                    hs = slice(half * 512, (half + 1) * 512)
                    ps = PS[4 + (st["pi"] % 3)]
                    st["pi"] += 1

                    def pe(e, ps=ps, hs=hs):
                        ins = None
                        for kc in range(8):
                            ins = e.matmul(ps.t[:], u_t.t[:, kc, :], wgt.t[:, kc, hs], start=(kc == 0), stop=(kc == 7))
                        return ins
                    em("tensor", pe, reads=[u_t.b] + wgt.rb(hs.start, hs.stop), writes=[ps.b])
                    em("scalar", lambda e, ps=ps, hs=hs: e.activation(out=gt.t[:, hs], in_=ps.t[:], func=AF.Sigmoid),
                       reads=[ps.b], writes=[gt.b])
                    ps2 = PS[4 + (st["pi"] % 3)]
                    st["pi"] += 1

                    def pe2(e, ps2=ps2, hs=hs):
                        e.matmul(ps2.t[:], p_T.t[:, 0, :], wpl.t[:, 0, hs], start=True, stop=False)
                        return e.matmul(ps2.t[:], p_T.t[:, 1, :], wpl.t[:, 1, hs], start=False, stop=True)
                    em("tensor", pe2, reads=[p_T.b] + wpl.rb(hs.start, hs.stop), writes=[ps2.b])
                    em("vector", lambda e, ps2=ps2, hs=hs: e.tensor_tensor(
                        out=gt.t[:, hs], in0=ps2.t[:], in1=gt.t[:, hs], op=ALU.mult),
                       reads=[ps2.b, gt.b], writes=[gt.b])
                em("gpsimd", lambda e: e.tensor_tensor(out=hb.t[:], in0=hb.t[:], in1=gt.t[:], op=ALU.add),
                   reads=[hb.b, gt.b], writes=[hb.b])
                if final:
                    of = ofin[ti % 2]
                    em("scalar", lambda e: e.activation(out=junk.t[:], in_=hb.t[:], func=AF.Square,
                                                         accum_out=ss[sidx].t[:, 1:2]),
                       reads=[hb.b], writes=[junk.b, ss[sidx].b])
                    em("gpsimd", lambda e: e.tensor_scalar(
                        out=ms[sidx].t[:, 1:2], in0=ss[sidx].t[:, 1:2], scalar1=1.0 / D, scalar2=EPS,
                        op0=ALU.mult, op1=ALU.add), reads=[ss[sidx].b], writes=[ms[sidx].b])
                    em("gpsimd", lambda e: e.tensor_tensor(
                        out=rstd[sidx].t[:, 1:2], in0=ms[sidx].t[:, 1:2], in1=neghalf.t[:, 0:1], op=ALU.pow),
                       reads=[ms[sidx].b, neghalf.b], writes=[rstd[sidx].b])
                    em("vector", lambda e: e.scalar_tensor_tensor(
                        out=of.t[:], in0=hb.t[:], scalar=rstd[sidx].t[:, 1:2], in1=gamf.t[:], op0=ALU.mult, op1=ALU.mult),
                       reads=[hb.b, rstd[sidx].b, gamf.b], writes=[of.b])
                    em("sync", lambda e: e.dma_start(out=out[r0:r0 + 128, :], in_=of.t[:]),
                       reads=[of.b], writes=[dOUT], dma=True, disjoint=True)
                else:
                    em("sync", lambda e: e.dma_start(out=hbuf[r0:r0 + 128, :], in_=hb.t[:]),
                       reads=[hb.b], writes=[dH[ti]], dma=True)

            load_ain(0)
            head(0)
            for ti in range(ntile):
                if ti + 1 < ntile:
                    head(ti + 1)
                tail(ti)

        hsrc, hsrc_bufs = x_in, dX
        for li in layers:
            stage_A(li, hsrc, hsrc_bufs)
            if li % 3 == 2:
                stage_B_swa(li)
            elif li % 3 == 0:
                stage_B_sb(li)
            else:
                stage_B_fox(li)
            stage_CU(li, hsrc, hsrc_bufs)
            final = (li == 3)
            stage_CD(li, final)
            hsrc, hsrc_bufs = hbuf, dH
        if not last_is_final:
            em("sync", lambda e: e.dma_start(out=out, in_=hbuf), reads=dH, writes=[dOUT], dma=True, disjoint=True)
        sc.run()
    return nc


_CONSTS = None
_NC_CACHE = {}


def _get_nc(layers, S=4096, dbg=False):
    key = (tuple(layers), S, dbg)
    if key not in _NC_CACHE:
        _NC_CACHE[key] = build(list(layers), S=S, dbg=dbg)
    return _NC_CACHE[key]


def make_in_maps(inputs, xs, S=4096):
    global _CONSTS
    if _CONSTS is None:
        _CONSTS = host_consts()
    shared = {}
    for k, shp in WEIGHT_SPECS.items():
        shared[k] = np.ascontiguousarray(np.asarray(inputs[k], dtype=np.float32).reshape(shp))
    shared.update(_CONSTS)
    p = np.asarray(inputs["p"], dtype=np.float32)
    maps = []
    for c in range(len(xs)):
        m = dict(shared)
        m["x"] = np.ascontiguousarray(xs[c][:S])
        m["p"] = np.ascontiguousarray(p[:, c, :S, :])
        maps.append(m)
    return maps


def kernel(**inputs):
    x = np.asarray(inputs["x"], dtype=np.float32)
    B = x.shape[0]
    nc = _get_nc((0, 1, 2, 3))
    maps = make_in_maps(inputs, [x[c] for c in range(B)])
    res = run_bass_kernel_spmd(nc, maps, core_ids=list(range(B)))
    return np.stack([np.asarray(r["out"], dtype=np.float32) for r in res.results], axis=0)
```

```python
from contextlib import ExitStack

import ml_dtypes
import numpy as np

import concourse.bass as bass
import concourse.mybir as mybir
from concourse.bass_utils import run_bass_kernel_spmd

F32 = mybir.dt.float32
BF16 = mybir.dt.bfloat16
AF = mybir.ActivationFunctionType
ALU = mybir.AluOpType

D = 1024
DFF = 4096
PLE = 256
NH = 16
HD = 64
EPS = 1e-6
NEG = -30000.0

ENGS = ("sync", "scalar", "vector", "gpsimd", "tensor")


class Buf:
    __slots__ = ("w", "r")

    def __init__(self):
        self.w = {}
        self.r = {}


class Sched:
    def __init__(self, nc, ctx):
        self.nc = nc
        self.ctx = ctx
        self.ops = {e: [] for e in ENGS}
        self.cur = {}
        self.nsem = 0
        self.dma_pool = {}
        self.dma_rr = {}
        self.n_ops = 0
        self.fence_toks = []

    def fence(self):
        toks = {}
        for s in self.cur.values():
            if s[1] > 0:
                toks[id(s[0])] = (s[0], s[1])
        for pool in self.dma_pool.values():
            for slot in pool:
                if slot[2] is not None:
                    toks[id(slot[0])] = slot[2]
        for t in self.fence_toks:
            k = id(t[0])
            if k not in toks:
                toks[k] = t
        self.fence_toks = list(toks.values())

    def _newsem(self):
        self.nsem += 1
        return self.ctx.enter_context(self.nc.semaphore(f"s{self.nsem}"))

    def _dma_slot(self, eng):
        if eng not in self.dma_pool:
            n = 20 if eng == "sync" else 8
            self.dma_pool[eng] = [[self._newsem(), 0, None] for _ in range(n)]
            self.dma_rr[eng] = 0
        pool = self.dma_pool[eng]
        i = self.dma_rr[eng]
        self.dma_rr[eng] = (i + 1) % len(pool)
        slot = pool[i]
        if slot[1] >= 30000:
            slot[0] = self._newsem()
            slot[1] = 0
            slot[2] = None
        return slot

    def emit(self, eng, fn, reads=(), writes=(), dma=False, disjoint=False):
        waits = {}

        def add(tok):
            if tok is None:
                return
            k = id(tok[0])
            if k not in waits or waits[k][1] < tok[1]:
                waits[k] = tok

        for t in self.fence_toks:
            add(t)
        for b in reads:
            for t in b.w.values():
                add(t)
        for b in writes:
            if not disjoint:
                for t in b.w.values():
                    add(t)
            for t in b.r.values():
                add(t)
        if dma:
            slot = self._dma_slot(eng)
            add(slot[2])
            slot[1] += 16
            tok = (slot[0], slot[1])
            slot[2] = tok
            amt = 16
        else:
            s = self.cur.get(eng)
            if s is None or s[1] >= 30000:
                s = [self._newsem(), 0]
                self.cur[eng] = s
            s[1] += 1
            tok = (s[0], s[1])
            amt = 1
        self.ops[eng].append((fn, list(waits.values()), tok, amt))
        for b in reads:
            k = id(tok[0])
            if k not in b.r or b.r[k][1] < tok[1]:
                b.r[k] = tok
        for b in writes:
            if disjoint and not b.r:
                k = id(tok[0])
                if k not in b.w or b.w[k][1] < tok[1]:
                    b.w[k] = tok
            else:
                b.w = {id(tok[0]): tok}
            b.r = {}
        self.n_ops += 1
        return tok

    def run(self):
        nc = self.nc
        with nc.Block() as block:
            def replay(name):
                def body(e):
                    waited = {}
                    for fn, waits, tok, amt in self.ops[name]:
                        for (sem, val) in waits:
                            k = id(sem)
                            if waited.get(k, 0) >= val:
                                continue
                            e.wait_ge(sem, val)
                            waited[k] = val
                        ins = fn(e)
                        ins.then_inc(tok[0], amt)
                    if self.ops[name]:
                        seen = {}
                        for fn, waits, tok, amt in self.ops[name]:
                            seen[id(tok[0])] = tok
                        for (sem, val) in seen.values():
                            e.wait_ge(sem, val)
                return body

            block.sync(replay("sync"))
            block.scalar(replay("scalar"))
            block.vector(replay("vector"))
            block.gpsimd(replay("gpsimd"))
            block.tensor(replay("tensor"))


class Arena:
    LO = 16640
    HI = 229376 - 256

    def __init__(self, nc):
        self.nc = nc
        self.top = Arena.LO
        self.n = 0

    def mark(self):
        return self.top

    def reset(self, m):
        self.top = m

    def alloc(self, shape, dtype, name="t"):
        esz = 4 if dtype == F32 else 2
        free = 1
        for s in shape[1:]:
            free *= s
        nbytes = (free * esz + 63) // 64 * 64
        off = self.top
        assert off + nbytes <= Arena.HI, f"SBUF overflow allocating {name} {shape}: {off}+{nbytes}"
        self.top += nbytes
        self.n += 1
        return self.nc.alloc_sbuf_tensor_at(f"{name}_{self.n}", list(shape), dtype, offset=off)


class T:
    def __init__(self, t):
        self.t = t
        self.b = Buf()


def host_consts():
    bf = ml_dtypes.bfloat16
    c = {}
    c["ident"] = np.eye(128, dtype=np.float32).astype(bf)
    j = np.arange(128)[:, None]
    s = np.arange(128)[None, :]
    c["tri"] = (-(j >= s).astype(np.float32)).astype(bf)
    e2 = np.zeros((128, 32, 128), np.float32)
    for kb in range(32):
        e2[:, kb, 64 + kb] = 1.0
    c["e2"] = e2.astype(bf)
    ksel = np.zeros((32, 32, 128), np.float32)
    for kp in range(32):
        for kb in range(32):
            if kp > kb:
                ksel[kp, kb, :] = -1.0
    c["ksel"] = ksel.astype(bf)
    sel = np.zeros((128, 32, 128), np.float32)
    for kp in range(64):
        for kb in range(32):
            if (kp % 32) > kb:
                sel[kp, kb, :] = -1.0
    c["selall"] = sel.astype(bf)
    sp = np.arange(128)[:, None, None] + 128 * np.arange(4)[None, :, None]
    tt = np.arange(512)[None, None, :]
    c["mask_lt"] = (sp < tt).astype(np.float32).astype(bf)
    c["mask_le"] = (sp <= tt).astype(np.float32).astype(bf)
    L = 384
    selb = np.zeros((32, L), np.float32)
    maskv = np.zeros((1, L), np.float32)
    for i in range(L):
        dist = i - 127
        if 0 <= dist < 128:
            if dist < 16:
                bkt = dist
            else:
                dd = np.float32(max(dist, 1))
                val = np.log(dd / np.float32(16)) / np.float32(np.log(128 / 16)) * np.float32(16)
                bkt = min(16 + int(np.float32(val)), 31)
            selb[bkt, i] = 1.0
        else:
            maskv[0, i] = NEG
    c["selb"] = selb
    c["maskv"] = maskv
    return c


CONST_SPECS = {
    "ident": ([128, 128], BF16), "tri": ([128, 128], BF16), "e2": ([128, 32, 128], BF16),
    "selall": ([128, 32, 128], BF16), "ksel": ([32, 32, 128], BF16), "mask_lt": ([128, 4, 512], BF16), "mask_le": ([128, 4, 512], BF16),
    "selb": ([32, 384], F32), "maskv": ([1, 384], F32),
}

WEIGHT_SPECS = {
    "attn_norm": [4, D], "mlp_norm": [4, D], "ple_norm": [4, D], "final_norm": [1, D],
    "w_in_sb": [2, D, 3072], "w_out_sb": [2, D, D], "w_in_fox": [1, D, 3088], "b_forget": [16, 1],
    "w_out_fox": [1, D, D], "w_in_swa": [1, D, 1536], "sinks": [1, 16], "w_out_swa": [1, D, D],
    "rel_bias": [32, 16], "w_up": [4, D, DFF], "w_down": [4, DFF, D], "w_ple": [4, PLE, D],
    "w_ple_gate": [4, D, D],
}


def build(layers, S=4096, dbg=False):
    nc = bass.Bass("TRN2", target_bir_lowering=False)
    NG = S // 512
    NKB = S // 128
    last_is_final = (layers[-1] == 3)

    x_in = nc.dram_tensor("x", [S, D], F32, kind="ExternalInput").ap()
    p_in = nc.dram_tensor("p", [4, S, PLE], F32, kind="ExternalInput").ap()
    W = {k: nc.dram_tensor(k, shp, F32, kind="ExternalInput").ap() for k, shp in WEIGHT_SPECS.items()}
    C = {k: nc.dram_tensor(k, shp, dt, kind="ExternalInput").ap() for k, (shp, dt) in CONST_SPECS.items()}
    out = nc.dram_tensor("out", [S, D], F32, kind="ExternalOutput").ap()
    skind = "ExternalOutput" if dbg else "Internal"
    hbuf = nc.dram_tensor("hbuf", [S, D], F32, kind=skind).ap()
    QT = nc.dram_tensor("QT", [D, S], BF16, kind=skind).ap()
    KT = nc.dram_tensor("KT", [D, S], BF16, kind=skind).ap()
    Vd = nc.dram_tensor("Vd", [S, D], BF16, kind=skind).ap()
    OT = nc.dram_tensor("OT", [D, S], BF16, kind=skind).ap()
    AT = nc.dram_tensor("AT", [32, 128, S], BF16, kind=skind).ap()
    A3 = nc.dram_tensor("A3", [16, 3, S], BF16, kind=skind).ap()
    NA3 = nc.dram_tensor("NA3", [16, 3, S], BF16, kind=skind).ap()
    RPD = nc.dram_tensor("RPD", [16, 128, 384], F32, kind=skind).ap()
    dQT, dKT, dV, dOT, dAT, dA3, dRP, dOUT = (Buf() for _ in range(8))
    dH = [Buf() for _ in range(S // 128)]
    dX = [Buf() for _ in range(S // 128)]

    with ExitStack() as ctx:
        ctx.enter_context(nc.allow_low_precision("bf16 matmul operands, fp32 accumulation"))
        sc = Sched(nc, ctx)
        ar = Arena(nc)
        em = sc.emit

        class TV:
            def __init__(self, ap):
                self.t = ap
                self.b = Buf()

        PSW = [T(nc.alloc_psum_tensor(f"psw{i}", [128, 1024], F32)) for i in range(2)]
        PS = [TV(PSW[i // 2].t[:, (i % 2) * 512:(i % 2 + 1) * 512]) for i in range(4)]
        PS += [T(nc.alloc_psum_tensor(f"ps{i}", [128, 512], F32)) for i in range(4, 7)]
        PST = T(nc.alloc_psum_tensor("pst", [128, 1024], BF16))

        def cload(name, shape, dt):
            t = T(ar.alloc(shape, dt, name))
            em("sync", lambda e, t=t, name=name: e.dma_start(out=t.t[:], in_=C[name]), writes=[t.b], dma=True)
            return t

        ident = cload("ident", [128, 128], BF16)
        tri = cload("tri", [128, 128], BF16)
        e2 = cload("e2", [128, 32, 128], BF16)
        mask_lt = cload("mask_lt", [128, 4, 512], BF16)
        mask_le = cload("mask_le", [128, 4, 512], BF16)
        ones_bf = T(ar.alloc([128, 128], BF16, "ones"))
        em("gpsimd", lambda e: e.memset(ones_bf.t[:], 1.0), writes=[ones_bf.b])
        neghalf = T(ar.alloc([128, 4], F32, "neghalf"))
        em("gpsimd", lambda e: e.memset(neghalf.t[:], -0.5), writes=[neghalf.b])
        persist_mark = ar.mark()

        def bcast_rows(ap_row, n):
            return bass.AP(ap_row.tensor, ap_row.offset, [[0, n]] + [list(x) for x in ap_row.ap[1:]])

        class WT:
            def __init__(self, t, ncols, cw):
                self.t = t
                self.cw = cw
                self.bufs = [Buf() for _ in range((ncols + cw - 1) // cw)]

            def rb(self, c0, c1):
                return self.bufs[c0 // self.cw:(c1 - 1) // self.cw + 1]

        def load_w_bf16(wap, ncols, name, k_chunks, cw=512):
            t = WT(ar.alloc([128, k_chunks, ncols], BF16, name), ncols, cw)
            src = wap.rearrange("(kc p) n -> p kc n", p=128)
            kstep = max(1, min(k_chunks, 4096 // cw))
            for ci, c0 in enumerate(range(0, ncols, cw)):
                c1 = min(ncols, c0 + cw)
                for k0 in range(0, k_chunks, kstep):
                    k1 = min(k_chunks, k0 + kstep)
                    em("gpsimd", lambda e, k0=k0, k1=k1, c0=c0, c1=c1: e.dma_start(
                        out=t.t[:, k0:k1, c0:c1], in_=src[:, k0:k1, c0:c1]),
                       writes=[t.bufs[ci]], dma=True, disjoint=True)
            return t

        def rmsnorm_tile(h_ap, hbufs, gam, ubf, ss, ms, rstd, junk, col):
            em("scalar", lambda e: e.activation(out=junk.t[:], in_=h_ap, func=AF.Square,
                                                 accum_out=ss.t[:, col:col + 1]),
               reads=hbufs, writes=[junk.b, ss.b])
            em("gpsimd", lambda e: e.tensor_scalar(out=ms.t[:, col:col + 1], in0=ss.t[:, col:col + 1],
                                                    scalar1=1.0 / D, scalar2=EPS, op0=ALU.mult, op1=ALU.add),
               reads=[ss.b], writes=[ms.b])
            em("gpsimd", lambda e: e.tensor_tensor(out=rstd.t[:, col:col + 1], in0=ms.t[:, col:col + 1],
                                                    in1=neghalf.t[:, 0:1], op=ALU.pow),
               reads=[ms.b, neghalf.b], writes=[rstd.b])
            em("vector", lambda e: e.scalar_tensor_tensor(out=ubf.t[:], in0=h_ap, scalar=rstd.t[:, col:col + 1],
                                                          in1=gam.t[:], op0=ALU.mult, op1=ALU.mult),
               reads=hbufs + [rstd.b, gam.b], writes=[ubf.b])

        def transpose_to(ubf, nchunk, dstT, dst_cols):
            def pe(e):
                ins = None
                for c in range(nchunk):
                    ins = e.transpose(out=PST.t[:, c * 128:(c + 1) * 128], in_=ubf.t[:, c * 128:(c + 1) * 128],
                                      identity=ident.t[:])
                return ins
            em("tensor", pe, reads=[ubf.b, ident.b], writes=[PST.b])
            em("scalar", lambda e: e.activation(
                out=dstT.t[:, 0:nchunk, dst_cols],
                in_=PST.t[:, 0:nchunk * 128].rearrange("p (c t) -> p c t", t=128), func=AF.Copy),
               reads=[PST.b], writes=[dstT.b])

        evac_rr = [0]

        def evac(dst_ap, dst_bufs, ps, scale=None, eng=None):
            if eng is None:
                eng = "scalar" if evac_rr[0] % 2 == 0 else "vector"
                evac_rr[0] += 1
            if eng == "scalar":
                if scale is None:
                    em("scalar", lambda e: e.activation(out=dst_ap, in_=ps.t[:], func=AF.Copy),
                       reads=[ps.b], writes=dst_bufs)
                else:
                    em("scalar", lambda e: e.activation(out=dst_ap, in_=ps.t[:], func=AF.Copy, scale=scale),
                       reads=[ps.b], writes=dst_bufs)
            else:
                if scale is None:
                    em("vector", lambda e: e.tensor_copy(out=dst_ap, in_=ps.t[:]), reads=[ps.b], writes=dst_bufs)
                else:
                    em("vector", lambda e: e.tensor_scalar(out=dst_ap, in0=ps.t[:], scalar1=scale, scalar2=None,
                                                            op0=ALU.mult),
                       reads=[ps.b], writes=dst_bufs)

        def stage_A(li, hsrc, hsrc_bufs):
            kind = li % 3
            j = li // 3
            ar.reset(persist_mark)
            sc.fence()
            if kind == 0:
                w_ap, ncols, nq, nk, vcols = W["w_in_sb"][j], 3072, 8, 8, 1024
            elif kind == 1:
                w_ap, ncols, nq, nk, vcols = W["w_in_fox"][j], 3088, 8, 8, 1024
            else:
                w_ap, ncols, nq, nk, vcols = W["w_in_swa"][j], 1536, 8, 2, 256
            voff = (nq + nk) * 128
            if kind == 1:
                negb = T(ar.alloc([16, 1], F32, "negb"))
                em("sync", lambda e: e.dma_start(out=negb.t[:], in_=W["b_forget"]), writes=[negb.b], dma=True)
                em("gpsimd", lambda e: e.tensor_scalar(out=negb.t[:], in0=negb.t[:], scalar1=-1.0, scalar2=None,
                                                        op0=ALU.mult), reads=[negb.b], writes=[negb.b])
                aT = T(ar.alloc([16, S], F32, "aT"))
                spt = [T(ar.alloc([16, 512], F32, "spt")) for _ in range(2)]
                ones16 = T(ar.alloc([16, 512], F32, "ones16"))
                em("gpsimd", lambda e: e.memset(ones16.t[:], 1.0), writes=[ones16.b])
                tail_mark = ar.mark()
            wsb = load_w_bf16(w_ap, ncols, "w_in", 8)
            gam = T(ar.alloc([128, D], F32, "gam"))
            em("sync", lambda e: e.dma_start(out=gam.t[:], in_=bcast_rows(W["attn_norm"][li:li + 1, :], 128)),
               writes=[gam.b], dma=True)
            ht = [T(ar.alloc([128, 4, D], F32, "ht")) for _ in range(2)]
            ubf = [T(ar.alloc([128, D], BF16, "ubf")) for _ in range(2)]
            junk = T(ar.alloc([128, D], BF16, "junk"))
            uT = [T(ar.alloc([128, 8, 512], BF16, "uT")) for _ in range(2)]
            qko = [T(ar.alloc([128, 512], BF16, "qko")) for _ in range(3)]
            vo = [T(ar.alloc([128, vcols], BF16, "vo")) for _ in range(2)]
            ss = [T(ar.alloc([128, 4], F32, "ss")) for _ in range(2)]
            ms = [T(ar.alloc([128, 4], F32, "ms")) for _ in range(2)]
            rstd = [T(ar.alloc([128, 4], F32, "rstd")) for _ in range(2)]
            st = {"nqo": 0, "pi": 0}

            def load(g):
                hb = ht[g % 2]
                em("sync", lambda e: e.dma_start(
                    out=hb.t[:], in_=hsrc[g * 512:(g + 1) * 512, :].rearrange("(t p) d -> p t d", p=128)),
                   reads=hsrc_bufs[4 * g:4 * g + 4], writes=[hb.b], dma=True)

            def norm(g, t):
                hb = ht[g % 2]
                rmsnorm_tile(hb.t[:, t, :], [hb.b], gam, ubf[t % 2], ss[g % 2], ms[g % 2], rstd[g % 2], junk, t)

            def trans(g, t):
                transpose_to(ubf[t % 2], 8, uT[g % 2], slice(t * 128, (t + 1) * 128))

            def qk_block(g, fb):
                u_t = uT[g % 2]
                ps = PS[st["pi"] % 4]
                st["pi"] += 1

                def pe(e):
                    ins = None
                    for kc in range(8):
                        ins = e.matmul(ps.t[:], wsb.t[:, kc, fb * 128:(fb + 1) * 128], u_t.t[:, kc, :],
                                       start=(kc == 0), stop=(kc == 7))
                    return ins
                em("tensor", pe, reads=wsb.rb(fb * 128, (fb + 1) * 128) + [u_t.b], writes=[ps.b])
                qo = qko[st["nqo"] % 3]
                st["nqo"] += 1
                evac(qo.t[:], [qo.b], ps, scale=(0.125 if fb < nq else None))
                if fb < nq:
                    dst, db = QT[fb * 128:(fb + 1) * 128, g * 512:(g + 1) * 512], dQT
                else:
                    dst, db = KT[(fb - nq) * 128:(fb - nq + 1) * 128, g * 512:(g + 1) * 512], dKT
                em("sync", lambda e: e.dma_start(out=dst, in_=qo.t[:]), reads=[qo.b], writes=[db], dma=True, disjoint=True)

            def v_block(g, t):
                u_t = uT[g % 2]
                vt = vo[t % 2]
                for c0 in range(0, vcols, 512):
                    cw = min(512, vcols - c0)
                    ps = PS[st["pi"] % 4]
                    st["pi"] += 1

                    def pe(e, c0=c0, cw=cw, ps=ps):
                        ins = None
                        for kc in range(8):
                            ins = e.matmul(ps.t[:, 0:cw], u_t.t[:, kc, t * 128:(t + 1) * 128],
                                           wsb.t[:, kc, voff + c0:voff + c0 + cw], start=(kc == 0), stop=(kc == 7))
                        return ins
                    em("tensor", pe, reads=wsb.rb(voff + c0, voff + c0 + cw) + [u_t.b], writes=[ps.b])
                    eng = "scalar" if evac_rr[0] % 2 == 0 else "vector"
                    evac_rr[0] += 1
                    if eng == "scalar":
                        em("scalar", lambda e, ps=ps, c0=c0, cw=cw: e.activation(
                            out=vt.t[:, c0:c0 + cw], in_=ps.t[:, 0:cw], func=AF.Copy), reads=[ps.b], writes=[vt.b])
                    else:
                        em("vector", lambda e, ps=ps, c0=c0, cw=cw: e.tensor_copy(
                            out=vt.t[:, c0:c0 + cw], in_=ps.t[:, 0:cw]), reads=[ps.b], writes=[vt.b])
                r0 = g * 512 + t * 128
                em("sync", lambda e: e.dma_start(out=Vd[r0:r0 + 128, 0:vcols], in_=vt.t[:]),
                   reads=[vt.b], writes=[dV], dma=True, disjoint=True)

            def fox_block(g):
                u_t = uT[g % 2]
                ps = PS[4]

                def pe(e):
                    ins = None
                    for kc in range(8):
                        ins = e.matmul(ps.t[0:16, :], wsb.t[:, kc, 3072:3088], u_t.t[:, kc, :],
                                       start=(kc == 0), stop=(kc == 7))
                    return ins
                em("tensor", pe, reads=wsb.rb(3072, 3088) + [u_t.b], writes=[ps.b])
                sp_ = spt[g % 2]
                em("scalar", lambda e: e.activation(out=sp_.t[:], in_=ps.t[0:16, :], func=AF.Softplus,
                                                     bias=negb.t[:], scale=-1.0),
                   reads=[ps.b, negb.b], writes=[sp_.b])
                init = 0.0 if g == 0 else aT.t[:, g * 512 - 1:g * 512]
                em("vector", lambda e: e.tensor_tensor_scan(
                    out=aT.t[:, g * 512:(g + 1) * 512], data0=ones16.t[:], data1=sp_.t[:], initial=init,
                    op0=ALU.mult, op1=ALU.add), reads=[sp_.b, ones16.b, aT.b], writes=[aT.b])

            load(0)
            for t in range(4):
                norm(0, t)
                trans(0, t)
            for g in range(NG):
                work = [("qk", fb) for fb in range(nq + nk)] + [("v", t) for t in range(4)]
                if kind == 1:
                    work.append(("f", 0))
                nxt = g + 1 < NG
                if nxt:
                    load(g + 1)
                per = (len(work) + 3) // 4
                for part in range(4):
                    if nxt:
                        norm(g + 1, part)
                    for kind_, a in work[part * per:(part + 1) * per]:
                        if kind_ == "qk":
                            qk_block(g, a)
                        elif kind_ == "v":
                            v_block(g, a)
                        else:
                            fox_block(g)
                    if nxt:
                        trans(g + 1, part)
            if kind == 1:
                sc.fence()
                ar.reset(tail_mark)
                a3 = T(ar.alloc([16, 3, S], BF16, "a3"))
                r1 = T(ar.alloc([16, S], F32, "r1"))
                v = "vector"
                em(v, lambda e: e.tensor_copy(out=a3.t[:, 0, :], in_=aT.t[:]), reads=[aT.b], writes=[a3.b])
                em(v, lambda e: e.tensor_tensor(out=r1.t[:], in0=aT.t[:], in1=a3.t[:, 0, :], op=ALU.subtract),
                   reads=[aT.b, a3.b], writes=[r1.b])
                em(v, lambda e: e.tensor_copy(out=a3.t[:, 1, :], in_=r1.t[:]), reads=[r1.b], writes=[a3.b])
                em(v, lambda e: e.tensor_tensor(out=r1.t[:], in0=r1.t[:], in1=a3.t[:, 1, :], op=ALU.subtract),
                   reads=[r1.b, a3.b], writes=[r1.b])
                em(v, lambda e: e.tensor_copy(out=a3.t[:, 2, :], in_=r1.t[:]), reads=[r1.b], writes=[a3.b])
                em("sync", lambda e: e.dma_start(out=A3, in_=a3.t[:]), reads=[a3.b], writes=[dA3], dma=True, disjoint=True)
                na3 = T(ar.alloc([16, 3, S], BF16, "na3"))
                em(v, lambda e: e.tensor_scalar(out=na3.t[:], in0=a3.t[:], scalar1=-1.0, scalar2=None, op0=ALU.mult),
                   reads=[a3.b], writes=[na3.b])
                em("sync", lambda e: e.dma_start(out=NA3, in_=na3.t[:]), reads=[na3.b], writes=[dA3], dma=True, disjoint=True)

        def stage_B_full(li):
            kind = li % 3
            fox = (kind == 1)
            ar.reset(persist_mark)
            sc.fence()
            kta = [[T(ar.alloc([128, S], BF16, "kta")) for _ in range(2)] for _ in range(2)]
            qta = [[T(ar.alloc([128, S], BF16, "qta")) for _ in range(2)] for _ in range(2)]
            vp = [T(ar.alloc([128, NKB, 128], BF16, "vp")) for _ in range(2)]
            for s_ in range(2):
                for hh in range(2):
                    for tt_ in (kta[s_][hh], qta[s_][hh]):
                        em("vector", lambda e, tt_=tt_: e.memset(tt_.t[64:128, :], 0.0), writes=[tt_.b])
                        if fox:
                            em("vector", lambda e, tt_=tt_: e.memset(tt_.t[64:70, :], 1.0), writes=[tt_.b])
            if not fox:
                Lt = ar.alloc([128, NKB, 512], BF16, "Lt")
                Ltb = [Buf() for _ in range(NKB)]
                T2 = T(ar.alloc([128, 512], BF16, "T2"))
                em("vector", lambda e: e.memset(T2.t[:], 0.0), writes=[T2.b])
            wt = [T(ar.alloc([128, 512], BF16, "wt")) for _ in range(6)]
            otile = [T(ar.alloc([128, 512], BF16, "otile")) for _ in range(3)]
            rec = [T(ar.alloc([128, 512], F32, "rec")) for _ in range(2)]
            pvq = []
            mask = mask_le if fox else mask_lt
            wti = [0]
            if fox:
                ps_s = [PS[0], PS[1], PS[2]]
                ps_o = [PS[3], PS[4]]
                ps_d = [PS[5], PS[6]]
            else:
                ps_z = [PS[0], PS[1]]
                ps_T = PS[2]
                ps_c = [PS[3], PS[4]]
                ps_o = [PS[5], PS[6]]
            cnt = {"z": 0, "c": 0, "o": 0, "s": 0}

            def load_pair(hp, sset):
                for hh in range(2):
                    h = 2 * hp + hh
                    k_t, q_t = kta[sset][hh], qta[sset][hh]
                    em("sync", lambda e, k_t=k_t, h=h: e.dma_start(out=k_t.t[0:64, :], in_=KT[h * 64:(h + 1) * 64, :]),
                       reads=[dKT], writes=[k_t.b], dma=True)
                    em("sync", lambda e, q_t=q_t, h=h: e.dma_start(out=q_t.t[0:64, :], in_=QT[h * 64:(h + 1) * 64, :]),
                       reads=[dQT], writes=[q_t.b], dma=True)
                    if fox:
                        em("sync", lambda e, k_t=k_t, h=h: e.dma_start(out=k_t.t[64:67, :], in_=A3[h]),
                           reads=[dA3], writes=[k_t.b], dma=True)
                        em("sync", lambda e, q_t=q_t, h=h: e.dma_start(out=q_t.t[67:70, :], in_=NA3[h]),
                           reads=[dA3], writes=[q_t.b], dma=True)
                v_t = vp[sset]
                em("sync", lambda e, v_t=v_t, hp=hp: e.dma_start(
                    out=v_t.t[:], in_=Vd[:, hp * 128:(hp + 1) * 128].rearrange("(kb p) c -> p kb c", p=128)),
                   reads=[dV], writes=[v_t.b], dma=True)

            def do_tile(hp, i, sset, v_t):
                if True:
                    nkb = 4 * i + 4
                    qs = slice(i * 512, (i + 1) * 512)
                    ot = otile[(hp * NG + i) % 3]
                    for hh in range(2):
                        do_head(hp, i, sset, v_t, nkb, qs, ot, hh)
                    pvq.append(lambda: em("sync", lambda e: e.dma_start(out=OT[hp * 128:(hp + 1) * 128, qs], in_=ot.t[:]),
                                          reads=[ot.b], writes=[dOT], dma=True, disjoint=True))
                    if not fox:
                        while pvq:
                            pvq.pop(0)()

            def do_head(hp, i, sset, v_t, nkb, qs, ot, hh):
                if True:
                    if True:
                        k_t, q_t = kta[sset][hh], qta[sset][hh]
                        rows = slice(hh * 64, (hh + 1) * 64)
                        pso = ps_o[cnt["o"] % 2]
                        cnt["o"] += 1
                        if fox:
                            psd = ps_d[(cnt["o"] - 1) % 2]
                            q0 = i * 512
                            for kb in range(nkb):
                                ks = slice(kb * 128, (kb + 1) * 128)
                                pss = ps_s[cnt["s"] % 3]
                                cnt["s"] += 1
                                jj = kb - 4 * i
                                c0 = max(jj, 0) * 128
                                em("tensor", lambda e, pss=pss, ks=ks, c0=c0: e.matmul(
                                    pss.t[:, c0:], k_t.t[:, ks], q_t.t[:, q0 + c0:q0 + 512], start=True, stop=True),
                                   reads=[k_t.b, q_t.b], writes=[pss.b])
                                w_ = wt[wti[0] % len(wt)]
                                wti[0] += 1
                                em("scalar", lambda e, w_=w_, pss=pss, c0=c0: e.activation(
                                    out=w_.t[:, c0:], in_=pss.t[:, c0:], func=AF.Exp),
                                   reads=[pss.b], writes=[w_.b])
                                if jj >= 0:
                                    em("vector", lambda e, w_=w_, jj=jj, c0=c0: e.scalar_tensor_tensor(
                                        out=w_.t[:, c0:], in0=w_.t[:, c0:], scalar=3.0e38, in1=mask.t[:, jj, c0:],
                                        op0=ALU.min, op1=ALU.mult),
                                       reads=[w_.b, mask.b], writes=[w_.b])

                                def pv(pk=kb, pw=w_, pc0=c0):
                                    def pe(e):
                                        e.matmul(pso.t[:, pc0:], v_t.t[:, pk, :], pw.t[:, pc0:], start=(pk == 0), stop=(pk == nkb - 1))
                                        return e.matmul(psd.t[:, pc0:], ones_bf.t[:], pw.t[:, pc0:], start=(pk == 0), stop=(pk == nkb - 1))
                                    em("tensor", pe, reads=[v_t.b, pw.b, ones_bf.b], writes=[pso.b, psd.b])
                                pvq.append(pv)
                                while len(pvq) > 2:
                                    pvq.pop(0)()
                            rc = rec[hh]

                            def epi():
                                em("vector", lambda e: e.reciprocal(out=rc.t[rows, :], in_=psd.t[rows, :]),
                                   reads=[psd.b], writes=[rc.b])
                                em("vector", lambda e: e.tensor_tensor(
                                    out=ot.t[rows, :], in0=pso.t[rows, :], in1=rc.t[rows, :], op=ALU.mult),
                                   reads=[pso.b, rc.b], writes=[ot.b])
                            pvq.append(epi)
                        else:
                            prev = None
                            for kb in range(nkb + 1):
                                cur = None
                                if kb < nkb:
                                    ks = slice(kb * 128, (kb + 1) * 128)
                                    psz = ps_z[cnt["z"] % 2]
                                    cnt["z"] += 1
                                    em("tensor", lambda e, psz=psz, k_t=k_t, q_t=q_t, ks=ks: e.matmul(
                                        psz.t[:], k_t.t[:, ks], q_t.t[:, qs], start=True, stop=True),
                                       reads=[k_t.b, q_t.b], writes=[psz.b])
                                    em("scalar", lambda e, psz=psz, kb=kb: e.activation(
                                        out=Lt[:, kb, :], in_=psz.t[:], func=AF.Softplus),
                                       reads=[psz.b], writes=[Ltb[kb]])
                                    jj = kb - 4 * i
                                    if jj >= 0:
                                        em("vector", lambda e, kb=kb, jj=jj: e.tensor_tensor(
                                            out=Lt[:, kb, :], in0=Lt[:, kb, :], in1=mask.t[:, jj, :], op=ALU.mult),
                                           reads=[Ltb[kb], mask.b], writes=[Ltb[kb]])
                                    cur = kb
                                if prev is not None:
                                    pk = prev
                                    em("tensor", lambda e, pk=pk: e.matmul(
                                        ps_T.t[:], e2.t[:, pk, :], Lt[:, pk, :], start=(pk == 0), stop=(pk == nkb - 1)),
                                       reads=[e2.b, Ltb[pk]], writes=[ps_T.b])
                                prev = cur
                            em("vector", lambda e: e.tensor_copy(out=T2.t[0:64, :], in_=ps_T.t[0:64, :]), reads=[ps_T.b], writes=[T2.b])
                            em("vector", lambda e: e.tensor_tensor(out=T2.t[32:64, :], in0=ps_T.t[32:64, :],
                                                                   in1=T2.t[32:64, :], op=ALU.subtract),
                               reads=[ps_T.b, T2.b], writes=[T2.b])
                            prev = None
                            for kb in range(nkb + 1):
                                cur = None
                                if kb < nkb:
                                    ks = slice(kb * 128, (kb + 1) * 128)
                                    psc = ps_c[cnt["c"] % 2]
                                    cnt["c"] += 1

                                    def pe(e, psc=psc, kb=kb, ks=ks, k_t=k_t, q_t=q_t):
                                        e.matmul(psc.t[:], tri.t[:], Lt[:, kb, :], start=True, stop=False)
                                        e.matmul(psc.t[:], selall.t[:, kb, :], T2.t[:], start=False, stop=False)
                                        return e.matmul(psc.t[:], k_t.t[:, ks], q_t.t[:, qs], start=False, stop=True)
                                    em("tensor", pe, reads=[tri.b, Ltb[kb], selall.b, T2.b, k_t.b, q_t.b], writes=[psc.b])
                                    w_ = wt[wti[0] % 3]
                                    wti[0] += 1
                                    em("scalar", lambda e, w_=w_, psc=psc: e.activation(
                                        out=w_.t[:], in_=psc.t[:], func=AF.Exp),
                                       reads=[psc.b], writes=[w_.b])
                                    jj = kb - 4 * i
                                    if jj >= 0:
                                        em("vector", lambda e, w_=w_, jj=jj: e.scalar_tensor_tensor(
                                            out=w_.t[:], in0=w_.t[:], scalar=3.0e38, in1=mask.t[:, jj, :],
                                            op0=ALU.min, op1=ALU.mult),
                                           reads=[w_.b, mask.b], writes=[w_.b])
                                    cur = (kb, w_)
                                if prev is not None:
                                    pk, pw = prev
                                    em("tensor", lambda e, pk=pk, pw=pw, pso=pso, v_t=v_t: e.matmul(
                                        pso.t[:], v_t.t[:, pk, :], pw.t[:], start=(pk == 0), stop=(pk == nkb - 1)),
                                       reads=[v_t.b, pw.b], writes=[pso.b])
                                prev = cur
                            em("vector", lambda e, pso=pso, ot=ot, rows=rows: e.tensor_copy(out=ot.t[rows, :], in_=pso.t[rows, :]),
                               reads=[pso.b], writes=[ot.b])

            load_pair(0, 0)
            for hp in range(NH // 2):
                sset = hp % 2
                if hp + 1 < NH // 2:
                    load_pair(hp + 1, 1 - sset)
                v_t = vp[sset]
                for i in range(NG):
                    do_tile(hp, i, sset, v_t)
                while pvq:
                    pvq.pop(0)()


        def stage_B_sb(li):
            ar.reset(persist_mark)
            sc.fence()
            kta = [[T(ar.alloc([128, S], BF16, "kta")) for _ in range(2)] for _ in range(2)]
            qta = [[T(ar.alloc([128, S], BF16, "qta")) for _ in range(2)] for _ in range(2)]
            vp = [T(ar.alloc([128, NKB, 128], BF16, "vp")) for _ in range(2)]
            for s_ in range(2):
                for hh in range(2):
                    k_ = kta[s_][hh]
                    em("vector", lambda e, k_=k_: e.memset(k_.t[64:128, :], 0.0), writes=[k_.b])
                    em("sync", lambda e, k_=k_: e.dma_start(
                        out=k_.t[64:96, :].rearrange("p (kb s) -> p kb s", s=128), in_=C["ksel"][:, 0:NKB, :]),
                       writes=[k_.b], dma=True)
            Lt = [ar.alloc([128, NKB, 512], BF16, "Lt") for _ in range(2)]
            Ltb = [[Buf() for _ in range(NKB)] for _ in range(2)]
            wt = [T(ar.alloc([128, 2, 512], BF16, "wt")) for _ in range(4)]
            otile = [T(ar.alloc([128, 512], BF16, "otile")) for _ in range(2)]
            ps_T = [PS[4], PS[4]]
            ps_o = [PS[5], PS[6]]
            cnt = {"zc": 0, "w": 0}

            def load_pair(hp, sset):
                for hh in range(2):
                    h = 2 * hp + hh
                    k_t, q_t = kta[sset][hh], qta[sset][hh]
                    em("sync", lambda e, k_t=k_t, h=h: e.dma_start(out=k_t.t[0:64, :], in_=KT[h * 64:(h + 1) * 64, :]),
                       reads=[dKT], writes=[k_t.b], dma=True)
                    em("vector", lambda e, q_t=q_t: e.memset(q_t.t[64:128, :], 0.0), writes=[q_t.b])
                    em("sync", lambda e, q_t=q_t, h=h: e.dma_start(out=q_t.t[0:64, :], in_=QT[h * 64:(h + 1) * 64, :]),
                       reads=[dQT], writes=[q_t.b], dma=True)
                v_t = vp[sset]
                em("sync", lambda e, v_t=v_t, hp=hp: e.dma_start(
                    out=v_t.t[:], in_=Vd[:, hp * 128:(hp + 1) * 128].rearrange("(kb p) c -> p kb c", p=128)),
                   reads=[dV], writes=[v_t.b], dma=True)


            def units(i):
                return [(kb, kb + 1) for kb in range(0, 4 * i, 2)] + [(kb,) for kb in range(4 * i, 4 * i + 4)]

            def phase_a(i, hh, k_t, q_t):
                nkb = 4 * i + 4
                q0 = i * 512
                L, Lb, pT = Lt[hh], Ltb[hh], ps_T[hh]
                prev = None
                for u in units(i) + [None]:
                    if u is not None:
                        W_ = PSW[cnt["zc"] % 2]
                        cnt["zc"] += 1
                        jj = u[0] - 4 * i
                        c0 = max(jj, 0) * 128

                        def pe(e, u=u, W_=W_, c0=c0):
                            ins = None
                            for n_, kb in enumerate(u):
                                ins = e.matmul(W_.t[:, n_ * 512 + c0:(n_ + 1) * 512], k_t.t[:, kb * 128:(kb + 1) * 128],
                                               q_t.t[:, q0 + c0:q0 + 512], start=True, stop=True)
                            return ins
                        em("tensor", pe, reads=[k_t.b, q_t.b], writes=[W_.b])
                        if len(u) == 2:
                            em("scalar", lambda e, u=u, W_=W_: e.activation(
                                out=L[:, u[0]:u[0] + 2, :], in_=W_.t[:].rearrange("p (a c) -> p a c", c=512), func=AF.Softplus),
                               reads=[W_.b], writes=[Lb[u[0]], Lb[u[1]]])
                        else:
                            kb = u[0]
                            em("scalar", lambda e, kb=kb, W_=W_, c0=c0: e.activation(
                                out=L[:, kb, c0:], in_=W_.t[:, c0:512], func=AF.Softplus), reads=[W_.b], writes=[Lb[kb]])
                            em("vector", lambda e, kb=kb, jj=jj, c0=c0: e.tensor_tensor(
                                out=L[:, kb, c0:], in0=L[:, kb, c0:], in1=mask_lt.t[:, jj, c0:], op=ALU.mult),
                               reads=[Lb[kb], mask_lt.b], writes=[Lb[kb]])
                    if prev is not None:
                        pu, pc0 = prev

                        def peT(e, pu=pu, pc0=pc0):
                            ins = None
                            for pk in pu:
                                ins = e.matmul(pT.t[:, pc0:], e2.t[:, pk, :], L[:, pk, pc0:], start=(pk == 0), stop=(pk == nkb - 1))
                            return ins
                        em("tensor", peT, reads=[e2.b] + [Lb[pk] for pk in pu], writes=[pT.b])
                    prev = (u, c0) if u is not None else None
                em("vector", lambda e: e.tensor_copy(out=q_t.t[64:96, q0:q0 + 512], in_=pT.t[64:96, :]),
                   reads=[pT.b], writes=[q_t.b])

            def phase_b(i, hh, k_t, q_t, v_t, ot):
                nkb = 4 * i + 4
                q0 = i * 512
                L, Lb, pso = Lt[hh], Ltb[hh], ps_o[hh]
                rows = slice(hh * 64, (hh + 1) * 64)
                prev = None
                for u in units(i) + [None]:
                    if u is not None:
                        W_ = PSW[cnt["zc"] % 2]
                        cnt["zc"] += 1
                        jj = u[0] - 4 * i
                        c0 = max(jj, 0) * 128

                        def pe(e, u=u, W_=W_, c0=c0):
                            ins = None
                            for n_, kb in enumerate(u):
                                o_ = W_.t[:, n_ * 512 + c0:(n_ + 1) * 512]
                                e.matmul(o_, tri.t[:], L[:, kb, c0:], start=True, stop=False)
                                ins = e.matmul(o_, k_t.t[:, kb * 128:(kb + 1) * 128], q_t.t[:, q0 + c0:q0 + 512],
                                               start=False, stop=True)
                            return ins
                        em("tensor", pe, reads=[tri.b, k_t.b, q_t.b] + [Lb[kb] for kb in u], writes=[W_.b])
                        w_ = wt[cnt["w"] % 4]
                        cnt["w"] += 1
                        if len(u) == 2:
                            em("scalar", lambda e, w_=w_, W_=W_: e.activation(
                                out=w_.t[:], in_=W_.t[:].rearrange("p (a c) -> p a c", c=512), func=AF.Exp),
                               reads=[W_.b], writes=[w_.b])
                        else:
                            em("scalar", lambda e, w_=w_, W_=W_, c0=c0: e.activation(
                                out=w_.t[:, 0, c0:], in_=W_.t[:, c0:512], func=AF.Exp), reads=[W_.b], writes=[w_.b])
                            em("vector", lambda e, w_=w_, jj=jj, c0=c0: e.scalar_tensor_tensor(
                                out=w_.t[:, 0, c0:], in0=w_.t[:, 0, c0:], scalar=3.0e38, in1=mask_lt.t[:, jj, c0:],
                                op0=ALU.min, op1=ALU.mult), reads=[w_.b, mask_lt.b], writes=[w_.b])
                    if prev is not None:
                        pu, pw, pc0 = prev

                        def pv(e, pu=pu, pw=pw, pc0=pc0):
                            ins = None
                            for n_, pk in enumerate(pu):
                                ins = e.matmul(pso.t[:, pc0:], v_t.t[:, pk, :], pw.t[:, n_, pc0:],
                                               start=(pk == 0), stop=(pk == nkb - 1))
                            return ins
                        em("tensor", pv, reads=[v_t.b, pw.b], writes=[pso.b])
                    prev = (u, w_, c0) if u is not None else None
                em("vector", lambda e: e.tensor_copy(out=ot.t[rows, :], in_=pso.t[rows, :]), reads=[pso.b], writes=[ot.b])

            def do_tile(hp, i, sset):
                v_t = vp[sset]
                ot = otile[(hp * NG + i) % 2]
                qs = slice(i * 512, (i + 1) * 512)
                for hh in range(2):
                    phase_a(i, hh, kta[sset][hh], qta[sset][hh])
                for hh in range(2):
                    phase_b(i, hh, kta[sset][hh], qta[sset][hh], v_t, ot)
                em("sync", lambda e: e.dma_start(out=OT[hp * 128:(hp + 1) * 128, qs], in_=ot.t[:]),
                   reads=[ot.b], writes=[dOT], dma=True, disjoint=True)

            load_pair(0, 0)
            for hp in range(NH // 2):
                sset = hp % 2
                if hp + 1 < NH // 2:
                    load_pair(hp + 1, 1 - sset)
                for i in range(NG):
                    do_tile(hp, i, sset)

        def stage_B_fox(li):
            ar.reset(persist_mark)
            sc.fence()
            kta = [[T(ar.alloc([128, S], BF16, "kta")) for _ in range(2)] for _ in range(2)]
            qta = [[T(ar.alloc([128, S], BF16, "qta")) for _ in range(2)] for _ in range(2)]
            vp = [[T(ar.alloc([128, NKB, 128], BF16, "vp")) for _ in range(2)] for _ in range(2)]
            for s_ in range(2):
                for hh in range(2):
                    for tt_ in (kta[s_][hh], qta[s_][hh]):
                        em("vector", lambda e, tt_=tt_: e.memset(tt_.t[64:128, :], 0.0), writes=[tt_.b])
                        em("vector", lambda e, tt_=tt_: e.memset(tt_.t[64:70, :], 1.0), writes=[tt_.b])
            wt = [T(ar.alloc([128, 2, 512], BF16, "wt")) for _ in range(5)]
            otile = [T(ar.alloc([128, 512], BF16, "otile")) for _ in range(3)]
            rec = [T(ar.alloc([128, 512], F32, "rec")) for _ in range(2)]
            rec2 = [T(ar.alloc([128, 512], F32, "rec2")) for _ in range(2)]
            pvq = []
            ps_o = [PS[4], PS[5], PS[6]]
            cnt = {"w": 0, "s": 0, "o": 0}

            def load_pair(hp, sset):
                for hh in range(2):
                    h = 2 * hp + hh
                    k_t, q_t, v_t = kta[sset][hh], qta[sset][hh], vp[sset][hh]
                    em("sync", lambda e, k_t=k_t, h=h: e.dma_start(out=k_t.t[0:64, :], in_=KT[h * 64:(h + 1) * 64, :]),
                       reads=[dKT], writes=[k_t.b], dma=True)
                    em("sync", lambda e, q_t=q_t, h=h: e.dma_start(out=q_t.t[0:64, :], in_=QT[h * 64:(h + 1) * 64, :]),
                       reads=[dQT], writes=[q_t.b], dma=True)
                    em("sync", lambda e, k_t=k_t, h=h: e.dma_start(out=k_t.t[64:67, :], in_=A3[h]),
                       reads=[dA3], writes=[k_t.b], dma=True)
                    em("sync", lambda e, q_t=q_t, h=h: e.dma_start(out=q_t.t[67:70, :], in_=NA3[h]),
                       reads=[dA3], writes=[q_t.b], dma=True)
                    em("sync", lambda e, v_t=v_t, hp=hp: e.dma_start(
                        out=v_t.t[:], in_=Vd[:, hp * 128:(hp + 1) * 128].rearrange("(kb p) c -> p kb c", p=128)),
                       reads=[dV], writes=[v_t.b], dma=True)
                    oc = slice(64, 128) if hh == 0 else slice(0, 64)
                    em("vector", lambda e, v_t=v_t, oc=oc: e.memset(v_t.t[:, :, oc], 1.0), writes=[v_t.b])

            def units(i):
                return [(kb, kb + 1) for kb in range(0, 4 * i, 2)] + [(kb,) for kb in range(4 * i, 4 * i + 4)]

            def do_head(hp, i, sset, hh, ot):
                nkb = 4 * i + 4
                q0 = i * 512
                k_t, q_t, v_t = kta[sset][hh], qta[sset][hh], vp[sset][hh]
                orow = slice(hh * 64, (hh + 1) * 64)
                drow = slice(64, 128) if hh == 0 else slice(0, 64)
                pso = ps_o[cnt["o"] % 3]
                cnt["o"] += 1
                for u in units(i):
                    W_ = PSW[cnt["s"] % 2]
                    cnt["s"] += 1
                    jj = u[0] - 4 * i
                    c0 = max(jj, 0) * 128

                    def pe(e, u=u, W_=W_, c0=c0):
                        ins = None
                        for n_, kb in enumerate(u):
                            ins = e.matmul(W_.t[:, n_ * 512 + c0:(n_ + 1) * 512], k_t.t[:, kb * 128:(kb + 1) * 128],
                                           q_t.t[:, q0 + c0:q0 + 512], start=True, stop=True)
                        return ins
                    em("tensor", pe, reads=[k_t.b, q_t.b], writes=[W_.b])
                    w_ = wt[cnt["w"] % len(wt)]
                    cnt["w"] += 1
                    if len(u) == 2:
                        em("scalar", lambda e, w_=w_, W_=W_: e.activation(
                            out=w_.t[:], in_=W_.t[:].rearrange("p (a c) -> p a c", c=512), func=AF.Exp),
                           reads=[W_.b], writes=[w_.b])
                    else:
                        em("scalar", lambda e, w_=w_, W_=W_, c0=c0: e.activation(
                            out=w_.t[:, 0, c0:], in_=W_.t[:, c0:512], func=AF.Exp), reads=[W_.b], writes=[w_.b])
                        em("vector", lambda e, w_=w_, jj=jj, c0=c0: e.scalar_tensor_tensor(
                            out=w_.t[:, 0, c0:], in0=w_.t[:, 0, c0:], scalar=3.0e38, in1=mask_le.t[:, jj, c0:],
                            op0=ALU.min, op1=ALU.mult), reads=[w_.b, mask_le.b], writes=[w_.b])

                    def pv(pu=u, pw=w_, pc0=c0):
                        def pe2(e):
                            ins = None
                            for n_, pk in enumerate(pu):
                                ins = e.matmul(pso.t[:, pc0:], v_t.t[:, pk, :], pw.t[:, n_, pc0:],
                                               start=(pk == 0), stop=(pk == nkb - 1))
                            return ins
                        em("tensor", pe2, reads=[v_t.b, pw.b], writes=[pso.b])
                    pvq.append(pv)
                    while len(pvq) > 2:
                        pvq.pop(0)()
                rc, rc2 = rec[hh], rec2[hh]

                def epi():
                    em("vector", lambda e: e.reciprocal(out=rc.t[drow, :], in_=pso.t[drow, :]), reads=[pso.b], writes=[rc.b])
                    em("vector", lambda e: e.tensor_copy(out=rc2.t[orow, :], in_=rc.t[drow, :]), reads=[rc.b], writes=[rc2.b])
                    em("vector", lambda e: e.tensor_tensor(out=ot.t[orow, :], in0=pso.t[orow, :], in1=rc2.t[orow, :],
                                                           op=ALU.mult), reads=[pso.b, rc2.b], writes=[ot.b])
                pvq.append(epi)

            load_pair(0, 0)
            for hp in range(NH // 2):
                sset = hp % 2
                if hp + 1 < NH // 2:
                    load_pair(hp + 1, 1 - sset)
                for i in range(NG):
                    ot = otile[(hp * NG + i) % 3]
                    qs = slice(i * 512, (i + 1) * 512)
                    for hh in range(2):
                        do_head(hp, i, sset, hh, ot)
                    pvq.append(lambda hp=hp, qs=qs, ot=ot: em(
                        "sync", lambda e: e.dma_start(out=OT[hp * 128:(hp + 1) * 128, qs], in_=ot.t[:]),
                        reads=[ot.b], writes=[dOT], dma=True, disjoint=True))
                while pvq:
                    pvq.pop(0)()

        def stage_B_swa(li):
            j = li // 3
            ar.reset(persist_mark)
            sc.fence()
            relb = T(ar.alloc([32, 16], F32, "relb"))
            selb = T(ar.alloc([32, 384], F32, "selb"))
            maskv = T(ar.alloc([1, 384], F32, "maskv"))
            ones1 = T(ar.alloc([1, 16], F32, "ones1"))
            rp = T(ar.alloc([16, 384], F32, "rp"))
            em("sync", lambda e: e.dma_start(out=relb.t[:], in_=W["rel_bias"]), writes=[relb.b], dma=True)
            em("sync", lambda e: e.dma_start(out=selb.t[:], in_=C["selb"]), writes=[selb.b], dma=True)
            em("sync", lambda e: e.dma_start(out=maskv.t[:], in_=C["maskv"]), writes=[maskv.b], dma=True)
            em("gpsimd", lambda e: e.memset(ones1.t[:], 1.0), writes=[ones1.b])

            def pe(e):
                e.matmul(PS[0].t[0:16, 0:384], relb.t[:], selb.t[:], start=True, stop=False)
                return e.matmul(PS[0].t[0:16, 0:384], ones1.t[:], maskv.t[:], start=False, stop=True)
            em("tensor", pe, reads=[relb.b, selb.b, ones1.b, maskv.b], writes=[PS[0].b])
            em("vector", lambda e: e.tensor_copy(out=rp.t[:], in_=PS[0].t[0:16, 0:384]), reads=[PS[0].b], writes=[rp.b])
            for r0 in range(0, 128, 32):
                src = bass.AP(rp.t[:].tensor, rp.t[:].offset, [list(rp.t[:].ap[0]), [0, 32], [1, 384]])
                em("sync", lambda e, r0=r0, src=src: e.dma_start(out=RPD[:, r0:r0 + 32, :], in_=src),
                   reads=[rp.b], writes=[dRP], dma=True, disjoint=True)
            biasT = T(ar.alloc([128, 16, 2, 128], F32, "biasT"))
            for h in range(16):
                src = bass.AP(RPD.tensor, RPD.offset + h * 128 * 384 + 127, [[383, 128], [128, 2], [1, 128]])
                em("sync", lambda e, h=h, src=src: e.dma_start(out=biasT.t[:, h, :, :], in_=src),
                   reads=[dRP], writes=[biasT.b], dma=True)
            bias_hi = T(ar.alloc([128, 16, 2, 128], BF16, "bias_hi"))
            bias_lo = T(ar.alloc([128, 16, 2, 128], BF16, "bias_lo"))
            em("vector", lambda e: e.tensor_copy(out=bias_hi.t[:], in_=biasT.t[:]), reads=[biasT.b], writes=[bias_hi.b])
            em("vector", lambda e: e.tensor_tensor(out=biasT.t[:], in0=biasT.t[:], in1=bias_hi.t[:], op=ALU.subtract),
               reads=[biasT.b, bias_hi.b], writes=[biasT.b])
            em("vector", lambda e: e.tensor_copy(out=bias_lo.t[:], in_=biasT.t[:]), reads=[biasT.b], writes=[bias_lo.b])
            esk = T(ar.alloc([64, 16], F32, "esk"))
            em("sync", lambda e: e.dma_start(out=esk.t[:], in_=bcast_rows(W["sinks"][j:j + 1, :], 64)),
               writes=[esk.b], dma=True)
            em("scalar", lambda e: e.activation(out=esk.t[:], in_=esk.t[:], func=AF.Exp), reads=[esk.b], writes=[esk.b])
            q4 = [T(ar.alloc([128, 4, S], BF16, "q4")) for _ in range(2)]
            kg = [T(ar.alloc([128, S], BF16, "kg")) for _ in range(2)]
            vg = [T(ar.alloc([128, NKB, 128], BF16, "vg")) for _ in range(2)]
            for s_ in range(2):
                em("vector", lambda e, s_=s_: e.memset(q4[s_].t[64:128, :, :], 0.0), writes=[q4[s_].b])
                em("vector", lambda e, s_=s_: e.memset(kg[s_].t[64:128, :], 0.0), writes=[kg[s_].b])
                em("vector", lambda e, s_=s_: e.memset(vg[s_].t[:], 0.0), writes=[vg[s_].b])
            ssb = [T(ar.alloc([128, 512], F32, "ssb")) for _ in range(4)]
            wt = [T(ar.alloc([128, 512], BF16, "wt")) for _ in range(5)]
            den = [T(ar.alloc([64, 512], F32, "den")) for _ in range(2)]
            ot4 = [T(ar.alloc([64, 4, 512], BF16, "ot4")) for _ in range(2)]
            ps_s = [PS[0], PS[1], PS[2]]
            ps_o = [PS[3], PS[4]]
            ps_d = [PS[5], PS[6]]
            ci = {"s": 0, "w": 0, "o": 0, "sb": 0}

            def load_g(g, sset):
                em("sync", lambda e: e.dma_start(
                    out=q4[sset].t[0:64, :, :], in_=QT[g * 256:(g + 1) * 256, :].rearrange("(j d) t -> d j t", d=64)),
                   reads=[dQT], writes=[q4[sset].b], dma=True)
                em("sync", lambda e: e.dma_start(out=kg[sset].t[0:64, :], in_=KT[g * 64:(g + 1) * 64, :]),
                   reads=[dKT], writes=[kg[sset].b], dma=True)
                em("sync", lambda e: e.dma_start(
                    out=vg[sset].t[:, :, 0:64], in_=Vd[:, g * 64:(g + 1) * 64].rearrange("(kb p) c -> p kb c", p=128)),
                   reads=[dV], writes=[vg[sset].b], dma=True)

            do_qblock = make_do_qblock(ot4, ps_o, ps_d, ps_s, ci, ssb, wt, den, (bias_hi, bias_lo), esk)
            load_g(0, 0)
            for g in range(4):
                sset = g % 2
                if g + 1 < 4:
                    load_g(g + 1, 1 - sset)
                q_t, k_t, v_t = q4[sset], kg[sset], vg[sset]
                do_qblock(g, q_t, k_t, v_t, NKB)

        def _unused():
            pass

        def make_do_qblock(ot4, ps_o, ps_d, ps_s, ci, ssb, wt, den, biasT, esk):
            def front(g, n, q_t, k_t, v_t):
                qcols = slice(n * 128, (n + 1) * 128)
                blocks = [(n, 0)] + ([(n - 1, 1)] if n > 0 else [])
                ws = []
                for bi, (kb, c) in enumerate(blocks):
                    ks = slice(kb * 128, (kb + 1) * 128)
                    pss = ps_s[ci["s"] % 3]
                    ci["s"] += 1
                    b_hi, b_lo = biasT

                    def pe(e, pss=pss, ks=ks, c=c):
                        o3 = pss.t[:].rearrange("p (j t) -> p j t", t=128)
                        e.matmul(o3, k_t.t[:, ks], q_t.t[:, :, qcols], start=True, stop=False)
                        e.matmul(o3, ident.t[:], b_hi.t[:, 4 * g:4 * g + 4, c, :], start=False, stop=False)
                        return e.matmul(o3, ident.t[:], b_lo.t[:, 4 * g:4 * g + 4, c, :], start=False, stop=True)
                    em("tensor", pe, reads=[k_t.b, q_t.b, ident.b, b_hi.b, b_lo.b], writes=[pss.b])
                    w_ = wt[ci["w"] % len(wt)]
                    ci["w"] += 1
                    em("scalar", lambda e, w_=w_, pss=pss: e.activation(out=w_.t[:], in_=pss.t[:], func=AF.Exp),
                       reads=[pss.b], writes=[w_.b])
                    ws.append((kb, w_))
                return ws

            def back(g, n, v_t, ws):
                o4 = ot4[(n // 4) % 2]
                pso = ps_o[ci["o"] % 2]
                psd = ps_d[ci["o"] % 2]
                ci["o"] += 1
                for bi, (kb, w_) in enumerate(ws):
                    first, lastb = (bi == 0), (bi == len(ws) - 1)

                    def pe(e, w_=w_, kb=kb, first=first, lastb=lastb):
                        e.matmul(pso.t[:], v_t.t[:, kb, :], w_.t[:], start=first, stop=lastb)
                        return e.matmul(psd.t[:], ones_bf.t[:], w_.t[:], start=first, stop=lastb)
                    em("tensor", pe, reads=[v_t.b, w_.b, ones_bf.b], writes=[pso.b, psd.b])
                dn = den[n % 2]
                esb = bass.AP(esk.t[:].tensor, esk.t[:, 4 * g:4 * g + 4].offset,
                              [list(esk.t[:].ap[0]), [1, 4], [0, 128]])
                em("vector", lambda e: e.tensor_tensor(
                    out=dn.t[:].rearrange("p (j t) -> p j t", t=128),
                    in0=psd.t[0:64, :].rearrange("p (j t) -> p j t", t=128), in1=esb, op=ALU.add),
                   reads=[psd.b, esk.b], writes=[dn.b])
                em("vector", lambda e: e.reciprocal(out=dn.t[:], in_=dn.t[:]), reads=[dn.b], writes=[dn.b])
                c4 = slice((n % 4) * 128, (n % 4 + 1) * 128)
                em("vector", lambda e: e.tensor_tensor(
                    out=o4.t[:, :, c4], in0=pso.t[0:64, :].rearrange("p (j t) -> p j t", t=128),
                    in1=dn.t[:].rearrange("p (j t) -> p j t", t=128), op=ALU.mult),
                   reads=[pso.b, dn.b], writes=[o4.b])
                if n % 4 == 3:
                    n0 = (n - 3) * 128
                    em("sync", lambda e: e.dma_start(
                        out=OT[g * 256:(g + 1) * 256, n0:n0 + 512].rearrange("(j d) t -> d j t", d=64),
                        in_=o4.t[:]), reads=[o4.b], writes=[dOT], dma=True, disjoint=True)

            def do_group(g, q_t, k_t, v_t, nkb):
                ws = front(g, 0, q_t, k_t, v_t)
                for n in range(nkb):
                    ws_next = front(g, n + 1, q_t, k_t, v_t) if n + 1 < nkb else None
                    back(g, n, v_t, ws)
                    ws = ws_next
            return do_group

        def stage_CU(li, hsrc, hsrc_bufs):
            kind = li % 3
            j = li // 3
            ar.reset(persist_mark)
            sc.fence()
            wo_ap = {0: W["w_out_sb"], 1: W["w_out_fox"], 2: W["w_out_swa"]}[kind][j]
            wout = load_w_bf16(wo_ap, D, "wout", 8)
            wup = load_w_bf16(W["w_up"][li], DFF, "wup", 8)
            gam = T(ar.alloc([128, D], F32, "gam"))
            em("sync", lambda e: e.dma_start(out=gam.t[:], in_=bcast_rows(W["mlp_norm"][li:li + 1, :], 128)),
               writes=[gam.b], dma=True)
            ott = [T(ar.alloc([128, 8, 512], BF16, "ott")) for _ in range(2)]
            ht = [T(ar.alloc([128, 4, D], F32, "ht")) for _ in range(2)]
            ubf = [T(ar.alloc([128, D], BF16, "ubf")) for _ in range(2)]
            junk = T(ar.alloc([128, D], BF16, "junk"))
            uT = [T(ar.alloc([128, 8, 512], BF16, "uT")) for _ in range(2)]
            rt = [T(ar.alloc([128, 512], F32, "rt")) for _ in range(2)]
            ao = [T(ar.alloc([128, 512], BF16, "ao")) for _ in range(3)]
            ss = [T(ar.alloc([128, 4], F32, "ss")) for _ in range(2)]
            ms = [T(ar.alloc([128, 4], F32, "ms")) for _ in range(2)]
            rstd = [T(ar.alloc([128, 4], F32, "rstd")) for _ in range(2)]
            st = {"pi": 0, "ai": 0}

            def load(g):
                hb, ob = ht[g % 2], ott[g % 2]
                em("sync", lambda e: e.dma_start(
                    out=hb.t[:], in_=hsrc[g * 512:(g + 1) * 512, :].rearrange("(t p) d -> p t d", p=128)),
                   reads=hsrc_bufs[4 * g:4 * g + 4], writes=[hb.b], dma=True)
                em("sync", lambda e: e.dma_start(
                    out=ob.t[:], in_=OT[:, g * 512:(g + 1) * 512].rearrange("(c p) t -> p c t", p=128)),
                   reads=[dOT], writes=[ob.b], dma=True)

            def head(g, t):
                hb, ob = ht[g % 2], ott[g % 2]
                for half in range(2):
                    ps = PS[st["pi"] % 4]
                    st["pi"] += 1

                    def pe(e, ps=ps, half=half):
                        ins = None
                        for c in range(8):
                            ins = e.matmul(ps.t[:], ob.t[:, c, t * 128:(t + 1) * 128],
                                           wout.t[:, c, half * 512:(half + 1) * 512], start=(c == 0), stop=(c == 7))
                        return ins
                    em("tensor", pe, reads=[ob.b] + wout.rb(half * 512, (half + 1) * 512), writes=[ps.b])
                    em("vector", lambda e, ps=ps, half=half: e.tensor_tensor(
                        out=hb.t[:, t, half * 512:(half + 1) * 512], in0=ps.t[:],
                        in1=hb.t[:, t, half * 512:(half + 1) * 512], op=ALU.add),
                       reads=[ps.b, hb.b], writes=[hb.b])
                rmsnorm_tile(hb.t[:, t, :], [hb.b], gam, ubf[t % 2], ss[g % 2], ms[g % 2], rstd[g % 2], junk, t)

            def trans(g, t):
                transpose_to(ubf[t % 2], 8, uT[g % 2], slice(t * 128, (t + 1) * 128))

            def store(g):
                hb = ht[g % 2]
                em("sync", lambda e: e.dma_start(
                    out=hbuf[g * 512:(g + 1) * 512, :].rearrange("(t p) d -> p t d", p=128), in_=hb.t[:]),
                   reads=[hb.b], writes=dH[4 * g:4 * g + 4], dma=True)

            def up(g, fc):
                u_t = uT[g % 2]
                ps = PS[4 + (st["pi"] % 3)]
                st["pi"] += 1

                def pe(e):
                    ins = None
                    for kc in range(8):
                        ins = e.matmul(ps.t[:], wup.t[:, kc, fc * 128:(fc + 1) * 128], u_t.t[:, kc, :],
                                       start=(kc == 0), stop=(kc == 7))
                    return ins
                em("tensor", pe, reads=wup.rb(fc * 128, (fc + 1) * 128) + [u_t.b], writes=[ps.b])
                r_ = rt[fc % 2]
                em("scalar", lambda e: e.activation(out=r_.t[:], in_=ps.t[:], func=AF.Relu), reads=[ps.b], writes=[r_.b])
                a_ = ao[st["ai"] % 3]
                st["ai"] += 1
                em("vector", lambda e: e.tensor_tensor(out=a_.t[:], in0=r_.t[:], in1=r_.t[:], op=ALU.mult),
                   reads=[r_.b], writes=[a_.b])
                em("sync", lambda e: e.dma_start(out=AT[fc, :, g * 512:(g + 1) * 512], in_=a_.t[:]),
                   reads=[a_.b], writes=[dAT], dma=True, disjoint=True)

            load(0)
            for t in range(4):
                head(0, t)
                trans(0, t)
            store(0)
            for g in range(NG):
                nxt = g + 1 < NG
                if nxt:
                    load(g + 1)
                for part in range(4):
                    if nxt:
                        head(g + 1, part)
                    for fc in range(part * 8, part * 8 + 8):
                        up(g, fc)
                    if nxt:
                        trans(g + 1, part)
                if nxt:
                    store(g + 1)

        def stage_CD(li, final):
            ar.reset(persist_mark)
            sc.fence()
            wdn = load_w_bf16(W["w_down"][li], D, "wdn", 32)
            wgt = load_w_bf16(W["w_ple_gate"][li], D, "wgt", 8)
            wpl = load_w_bf16(W["w_ple"][li], D, "wpl", 2)
            gam = T(ar.alloc([128, D], F32, "gam"))
            em("sync", lambda e: e.dma_start(out=gam.t[:], in_=bcast_rows(W["ple_norm"][li:li + 1, :], 128)),
               writes=[gam.b], dma=True)
            if final:
                gamf = T(ar.alloc([128, D], F32, "gamf"))
                em("sync", lambda e: e.dma_start(out=gamf.t[:], in_=bcast_rows(W["final_norm"][0:1, :], 128)),
                   writes=[gamf.b], dma=True)
            TG = 256
            TPG = TG // 128
            ntile = S // 128
            ain = [T(ar.alloc([128, 32, TG], BF16, "ain")) for _ in range(2)]
            ht = [T(ar.alloc([128, D], F32, "ht")) for _ in range(3)]
            pt = [T(ar.alloc([128, PLE], F32, "pt")) for _ in range(2)]
            pbf = [T(ar.alloc([128, PLE], BF16, "pbf")) for _ in range(2)]
            pT = [T(ar.alloc([128, 2, 128], BF16, "pT")) for _ in range(2)]
            ubf = [T(ar.alloc([128, D], BF16, "ubf")) for _ in range(2)]
            junk = T(ar.alloc([128, D], BF16, "junk"))
            uT = [T(ar.alloc([128, 8, 128], BF16, "uT")) for _ in range(2)]
            gate = [T(ar.alloc([128, D], F32, "gate")) for _ in range(2)]
            ofin = [T(ar.alloc([128, D], F32, "ofin")) for _ in range(2)]
            ss = [T(ar.alloc([128, 4], F32, "ss")) for _ in range(2)]
            ms = [T(ar.alloc([128, 4], F32, "ms")) for _ in range(2)]
            rstd = [T(ar.alloc([128, 4], F32, "rstd")) for _ in range(2)]
            st = {"pi": 0}
            head_banks = {}

            def load_ain(g):
                a_in = ain[g % 2]
                em("sync", lambda e: e.dma_start(
                    out=a_in.t[:], in_=AT[:, :, g * TG:(g + 1) * TG].rearrange("c p t -> p c t")),
                   reads=[dAT], writes=[a_in.b], dma=True)

            def head(ti):
                g, t = ti // TPG, ti % TPG
                a_in = ain[g % 2]
                if t == 0 and (g + 1) * TG < S:
                    load_ain(g + 1)
                r0 = ti * 128
                hb, p_ = ht[ti % 3], pt[ti % 2]
                em("sync", lambda e: e.dma_start(out=hb.t[:], in_=hbuf[r0:r0 + 128, :]),
                   reads=[dH[ti]], writes=[hb.b], dma=True)
                em("sync", lambda e: e.dma_start(out=p_.t[:], in_=p_in[li, r0:r0 + 128, :]), writes=[p_.b], dma=True)
                banks = []
                for half in range(2):
                    ps = PS[st["pi"] % 4]
                    st["pi"] += 1
                    banks.append(ps)

                    def pe(e, ps=ps, half=half):
                        ins = None
                        for fc in range(32):
                            ins = e.matmul(ps.t[:], a_in.t[:, fc, t * 128:(t + 1) * 128],
                                           wdn.t[:, fc, half * 512:(half + 1) * 512], start=(fc == 0), stop=(fc == 31))
                        return ins
                    em("tensor", pe, reads=[a_in.b] + wdn.rb(half * 512, (half + 1) * 512), writes=[ps.b])
                pb = pbf[ti % 2]
                em("gpsimd", lambda e: e.tensor_copy(out=pb.t[:], in_=p_.t[:]), reads=[p_.b], writes=[pb.b])
                head_banks[ti] = banks

            def head_post(ti):
                hb = ht[ti % 3]
                for half, ps in enumerate(head_banks.pop(ti)):
                    em("vector", lambda e, ps=ps, half=half: e.tensor_tensor(
                        out=hb.t[:, half * 512:(half + 1) * 512], in0=ps.t[:],
                        in1=hb.t[:, half * 512:(half + 1) * 512], op=ALU.add),
                       reads=[ps.b, hb.b], writes=[hb.b])

            def tail_norm(ti):
                hb = ht[ti % 3]
                sidx = ti % 2
                rmsnorm_tile(hb.t[:], [hb.b], gam, ubf[ti % 2], ss[sidx], ms[sidx], rstd[sidx], junk, 0)

            def tail(ti):
                r0 = ti * 128
                hb = ht[ti % 3]
                ub = ubf[ti % 2]
                sidx = ti % 2
                u_t = uT[ti % 2]
                transpose_to(ub, 8, u_t, slice(0, 128))
                p_T = pT[ti % 2]
                transpose_to(pbf[ti % 2], 2, p_T, slice(0, 128))
                gt = gate[ti % 2]
                for half in range(2):
                    hs = slice(half * 512, (half + 1) * 512)
                    ps = PS[4 + (st["pi"] % 3)]
                    st["pi"] += 1

                    def pe(e, ps=ps, hs=hs):
                        ins = None
                        for kc in range(8):
                            ins = e.matmul(ps.t[:], u_t.t[:, kc, :], wgt.t[:, kc, hs], start=(kc == 0), stop=(kc == 7))
                        return ins
                    em("tensor", pe, reads=[u_t.b] + wgt.rb(hs.start, hs.stop), writes=[ps.b])
                    em("scalar", lambda e, ps=ps, hs=hs: e.activation(out=gt.t[:, hs], in_=ps.t[:], func=AF.Sigmoid),
                       reads=[ps.b], writes=[gt.b])
                    ps2 = PS[4 + (st["pi"] % 3)]
                    st["pi"] += 1

                    def pe2(e, ps2=ps2, hs=hs):
                        e.matmul(ps2.t[:], p_T.t[:, 0, :], wpl.t[:, 0, hs], start=True, stop=False)
                        return e.matmul(ps2.t[:], p_T.t[:, 1, :], wpl.t[:, 1, hs], start=False, stop=True)
                    em("tensor", pe2, reads=[p_T.b] + wpl.rb(hs.start, hs.stop), writes=[ps2.b])
                    em("vector", lambda e, ps2=ps2, hs=hs: e.tensor_tensor(
                        out=gt.t[:, hs], in0=ps2.t[:], in1=gt.t[:, hs], op=ALU.mult),
                       reads=[ps2.b, gt.b], writes=[gt.b])
                em("gpsimd", lambda e: e.tensor_tensor(out=hb.t[:], in0=hb.t[:], in1=gt.t[:], op=ALU.add),
                   reads=[hb.b, gt.b], writes=[hb.b])
                if final:
                    of = ofin[ti % 2]
                    em("scalar", lambda e: e.activation(out=junk.t[:], in_=hb.t[:], func=AF.Square,
                                                         accum_out=ss[sidx].t[:, 1:2]),
                       reads=[hb.b], writes=[junk.b, ss[sidx].b])
                    em("gpsimd", lambda e: e.tensor_scalar(
                        out=ms[sidx].t[:, 1:2], in0=ss[sidx].t[:, 1:2], scalar1=1.0 / D, scalar2=EPS,
                        op0=ALU.mult, op1=ALU.add), reads=[ss[sidx].b], writes=[ms[sidx].b])
                    em("gpsimd", lambda e: e.tensor_tensor(
                        out=rstd[sidx].t[:, 1:2], in0=ms[sidx].t[:, 1:2], in1=neghalf.t[:, 0:1], op=ALU.pow),
                       reads=[ms[sidx].b, neghalf.b], writes=[rstd[sidx].b])
                    em("vector", lambda e: e.scalar_tensor_tensor(
                        out=of.t[:], in0=hb.t[:], scalar=rstd[sidx].t[:, 1:2], in1=gamf.t[:], op0=ALU.mult, op1=ALU.mult),
                       reads=[hb.b, rstd[sidx].b, gamf.b], writes=[of.b])
                    em("sync", lambda e: e.dma_start(out=out[r0:r0 + 128, :], in_=of.t[:]),
                       reads=[of.b], writes=[dOUT], dma=True, disjoint=True)
                else:
                    em("sync", lambda e: e.dma_start(out=hbuf[r0:r0 + 128, :], in_=hb.t[:]),
                       reads=[hb.b], writes=[dH[ti]], dma=True)

            load_ain(0)
            head(0)
            head_post(0)
            for ti in range(ntile):
                if ti + 1 < ntile:
                    head(ti + 1)
                tail_norm(ti)
                if ti + 1 < ntile:
                    head_post(ti + 1)
                tail(ti)

        hsrc, hsrc_bufs = x_in, dX
        for li in layers:
            stage_A(li, hsrc, hsrc_bufs)
            if li % 3 == 2:
                stage_B_swa(li)
            elif li % 3 == 0:
                stage_B_sb(li)
            else:
                stage_B_fox(li)
            stage_CU(li, hsrc, hsrc_bufs)
            final = (li == 3)
            stage_CD(li, final)
            hsrc, hsrc_bufs = hbuf, dH
        if not last_is_final:
            em("sync", lambda e: e.dma_start(out=out, in_=hbuf), reads=dH, writes=[dOUT], dma=True, disjoint=True)
        sc.run()
    return nc


_CONSTS = None
_NC_CACHE = {}


def _get_nc(layers, S=4096, dbg=False):
    key = (tuple(layers), S, dbg)
    if key not in _NC_CACHE:
        _NC_CACHE[key] = build(list(layers), S=S, dbg=dbg)
    return _NC_CACHE[key]


def make_in_maps(inputs, xs, S=4096):
    global _CONSTS
    if _CONSTS is None:
        _CONSTS = host_consts()
    shared = {}
    for k, shp in WEIGHT_SPECS.items():
        shared[k] = np.ascontiguousarray(np.asarray(inputs[k], dtype=np.float32).reshape(shp))
    shared.update(_CONSTS)
    p = np.asarray(inputs["p"], dtype=np.float32)
    maps = []
    for c in range(len(xs)):
        m = dict(shared)
        m["x"] = np.ascontiguousarray(xs[c][:S])
        m["p"] = np.ascontiguousarray(p[:, c, :S, :])
        maps.append(m)
    return maps


def kernel(**inputs):
    x = np.asarray(inputs["x"], dtype=np.float32)
    B = x.shape[0]
    nc = _get_nc((0, 1, 2, 3))
    maps = make_in_maps(inputs, [x[c] for c in range(B)])
    res = run_bass_kernel_spmd(nc, maps, core_ids=list(range(B)))
    return np.stack([np.asarray(r["out"], dtype=np.float32) for r in res.results], axis=0)
```

```python
from contextlib import ExitStack

import ml_dtypes
import numpy as np

import concourse.bass as bass
import concourse.mybir as mybir
from concourse.bass_utils import run_bass_kernel_spmd

F32 = mybir.dt.float32
BF16 = mybir.dt.bfloat16
AF = mybir.ActivationFunctionType
ALU = mybir.AluOpType

D = 1024
DFF = 4096
PLE = 256
NH = 16
HD = 64
EPS = 1e-6
NEG = -30000.0

ENGS = ("sync", "scalar", "vector", "gpsimd", "tensor")


class Buf:
    __slots__ = ("w", "r")

    def __init__(self):
        self.w = {}
        self.r = {}


class Sched:
    def __init__(self, nc, ctx):
        self.nc = nc
        self.ctx = ctx
        self.ops = {e: [] for e in ENGS}
        self.cur = {}
        self.nsem = 0
        self.dma_pool = {}
        self.dma_rr = {}
        self.n_ops = 0
        self.fence_toks = []

    def fence(self):
        toks = {}
        for s in self.cur.values():
            if s[1] > 0:
                toks[id(s[0])] = (s[0], s[1])
        for pool in self.dma_pool.values():
            for slot in pool:
                if slot[2] is not None:
                    toks[id(slot[0])] = slot[2]
        for t in self.fence_toks:
            k = id(t[0])
            if k not in toks:
                toks[k] = t
        self.fence_toks = list(toks.values())

    def _newsem(self):
        self.nsem += 1
        return self.ctx.enter_context(self.nc.semaphore(f"s{self.nsem}"))

    def _dma_slot(self, eng):
        if eng not in self.dma_pool:
            n = 20 if eng == "sync" else 8
            self.dma_pool[eng] = [[self._newsem(), 0, None] for _ in range(n)]
            self.dma_rr[eng] = 0
        pool = self.dma_pool[eng]
        i = self.dma_rr[eng]
        self.dma_rr[eng] = (i + 1) % len(pool)
        slot = pool[i]
        if slot[1] >= 30000:
            slot[0] = self._newsem()
            slot[1] = 0
            slot[2] = None
        return slot

    def emit(self, eng, fn, reads=(), writes=(), dma=False, disjoint=False):
        waits = {}

        def add(tok):
            if tok is None:
                return
            k = id(tok[0])
            if k not in waits or waits[k][1] < tok[1]:
                waits[k] = tok

        for t in self.fence_toks:
            add(t)
        for b in reads:
            for t in b.w.values():
                add(t)
        for b in writes:
            if not disjoint:
                for t in b.w.values():
                    add(t)
            for t in b.r.values():
                add(t)
        if dma:
            slot = self._dma_slot(eng)
            add(slot[2])
            slot[1] += 16
            tok = (slot[0], slot[1])
            slot[2] = tok
            amt = 16
        else:
            s = self.cur.get(eng)
            if s is None or s[1] >= 30000:
                s = [self._newsem(), 0]
                self.cur[eng] = s
            s[1] += 1
            tok = (s[0], s[1])
            amt = 1
        self.ops[eng].append((fn, list(waits.values()), tok, amt))
        for b in reads:
            k = id(tok[0])
            if k not in b.r or b.r[k][1] < tok[1]:
                b.r[k] = tok
        for b in writes:
            if disjoint and not b.r:
                k = id(tok[0])
                if k not in b.w or b.w[k][1] < tok[1]:
                    b.w[k] = tok
            else:
                b.w = {id(tok[0]): tok}
            b.r = {}
        self.n_ops += 1
        return tok

    def run(self):
        nc = self.nc
        with nc.Block() as block:
            def replay(name):
                def body(e):
                    waited = {}
                    for fn, waits, tok, amt in self.ops[name]:
                        for (sem, val) in waits:
                            k = id(sem)
                            if waited.get(k, 0) >= val:
                                continue
                            e.wait_ge(sem, val)
                            waited[k] = val
                        ins = fn(e)
                        ins.then_inc(tok[0], amt)
                    if self.ops[name]:
                        seen = {}
                        for fn, waits, tok, amt in self.ops[name]:
                            seen[id(tok[0])] = tok
                        for (sem, val) in seen.values():
                            e.wait_ge(sem, val)
                return body

            block.sync(replay("sync"))
            block.scalar(replay("scalar"))
            block.vector(replay("vector"))
            block.gpsimd(replay("gpsimd"))
            block.tensor(replay("tensor"))


class Arena:
    LO = 16640
    HI = 229376 - 256

    def __init__(self, nc):
        self.nc = nc
        self.top = Arena.LO
        self.n = 0

    def mark(self):
        return self.top

    def reset(self, m):
        self.top = m

    def alloc(self, shape, dtype, name="t"):
        esz = 4 if dtype == F32 else 2
        free = 1
        for s in shape[1:]:
            free *= s
        nbytes = (free * esz + 63) // 64 * 64
        off = self.top
        assert off + nbytes <= Arena.HI, f"SBUF overflow allocating {name} {shape}: {off}+{nbytes}"
        self.top += nbytes
        self.n += 1
        return self.nc.alloc_sbuf_tensor_at(f"{name}_{self.n}", list(shape), dtype, offset=off)


class T:
    def __init__(self, t):
        self.t = t
        self.b = Buf()


def host_consts():
    bf = ml_dtypes.bfloat16
    c = {}
    c["ident"] = np.eye(128, dtype=np.float32).astype(bf)
    j = np.arange(128)[:, None]
    s = np.arange(128)[None, :]
    c["tri"] = (-(j >= s).astype(np.float32)).astype(bf)
    e2 = np.zeros((128, 32, 128), np.float32)
    for kb in range(32):
        e2[:, kb, 64 + kb] = 1.0
    c["e2"] = e2.astype(bf)
    ksel = np.zeros((32, 32, 128), np.float32)
    for kp in range(32):
        for kb in range(32):
            if kp > kb:
                ksel[kp, kb, :] = -1.0
    c["ksel"] = ksel.astype(bf)
    sel = np.zeros((128, 32, 128), np.float32)
    for kp in range(64):
        for kb in range(32):
            if (kp % 32) > kb:
                sel[kp, kb, :] = -1.0
    c["selall"] = sel.astype(bf)
    sp = np.arange(128)[:, None, None] + 128 * np.arange(4)[None, :, None]
    tt = np.arange(512)[None, None, :]
    c["mask_lt"] = (sp < tt).astype(np.float32).astype(bf)
    c["mask_le"] = (sp <= tt).astype(np.float32).astype(bf)
    L = 384
    selb = np.zeros((32, L), np.float32)
    maskv = np.zeros((1, L), np.float32)
    for i in range(L):
        dist = i - 127
        if 0 <= dist < 128:
            if dist < 16:
                bkt = dist
            else:
                dd = np.float32(max(dist, 1))
                val = np.log(dd / np.float32(16)) / np.float32(np.log(128 / 16)) * np.float32(16)
                bkt = min(16 + int(np.float32(val)), 31)
            selb[bkt, i] = 1.0
        else:
            maskv[0, i] = NEG
    c["selb"] = selb
    c["maskv"] = maskv
    return c


CONST_SPECS = {
    "ident": ([128, 128], BF16), "tri": ([128, 128], BF16), "e2": ([128, 32, 128], BF16),
    "selall": ([128, 32, 128], BF16), "ksel": ([32, 32, 128], BF16), "mask_lt": ([128, 4, 512], BF16), "mask_le": ([128, 4, 512], BF16),
    "selb": ([32, 384], F32), "maskv": ([1, 384], F32),
}

WEIGHT_SPECS = {
    "attn_norm": [4, D], "mlp_norm": [4, D], "ple_norm": [4, D], "final_norm": [1, D],
    "w_in_sb": [2, D, 3072], "w_out_sb": [2, D, D], "w_in_fox": [1, D, 3088], "b_forget": [16, 1],
    "w_out_fox": [1, D, D], "w_in_swa": [1, D, 1536], "sinks": [1, 16], "w_out_swa": [1, D, D],
    "rel_bias": [32, 16], "w_up": [4, D, DFF], "w_down": [4, DFF, D], "w_ple": [4, PLE, D],
    "w_ple_gate": [4, D, D],
}


def build(layers, S=4096, dbg=False):
    nc = bass.Bass("TRN2", target_bir_lowering=False)
    NG = S // 512
    NKB = S // 128
    last_is_final = (layers[-1] == 3)

    x_in = nc.dram_tensor("x", [S, D], F32, kind="ExternalInput").ap()
    p_in = nc.dram_tensor("p", [4, S, PLE], F32, kind="ExternalInput").ap()
    W = {k: nc.dram_tensor(k, shp, F32, kind="ExternalInput").ap() for k, shp in WEIGHT_SPECS.items()}
    C = {k: nc.dram_tensor(k, shp, dt, kind="ExternalInput").ap() for k, (shp, dt) in CONST_SPECS.items()}
    out = nc.dram_tensor("out", [S, D], F32, kind="ExternalOutput").ap()
    skind = "ExternalOutput" if dbg else "Internal"
    hbuf = nc.dram_tensor("hbuf", [S, D], F32, kind=skind).ap()
    QT = nc.dram_tensor("QT", [D, S], BF16, kind=skind).ap()
    KT = nc.dram_tensor("KT", [D, S], BF16, kind=skind).ap()
    Vd = nc.dram_tensor("Vd", [S, D], BF16, kind=skind).ap()
    OT = nc.dram_tensor("OT", [D, S], BF16, kind=skind).ap()
    AT = nc.dram_tensor("AT", [32, 128, S], BF16, kind=skind).ap()
    A3 = nc.dram_tensor("A3", [16, 3, S], BF16, kind=skind).ap()
    NA3 = nc.dram_tensor("NA3", [16, 3, S], BF16, kind=skind).ap()
    RPD = nc.dram_tensor("RPD", [16, 128, 384], F32, kind=skind).ap()
    dQT, dKT, dV, dOT, dAT, dA3, dRP, dOUT = (Buf() for _ in range(8))
    dH = [Buf() for _ in range(S // 128)]
    dX = [Buf() for _ in range(S // 128)]

    with ExitStack() as ctx:
        ctx.enter_context(nc.allow_low_precision("bf16 matmul operands, fp32 accumulation"))
        sc = Sched(nc, ctx)
        ar = Arena(nc)
        em = sc.emit

        class TV:
            def __init__(self, ap):
                self.t = ap
                self.b = Buf()

        PSW = [T(nc.alloc_psum_tensor(f"psw{i}", [128, 1024], F32)) for i in range(2)]
        PS = [TV(PSW[i // 2].t[:, (i % 2) * 512:(i % 2 + 1) * 512]) for i in range(4)]
        PS += [T(nc.alloc_psum_tensor(f"ps{i}", [128, 512], F32)) for i in range(4, 7)]
        PST = T(nc.alloc_psum_tensor("pst", [128, 1024], BF16))

        def cload(name, shape, dt):
            t = T(ar.alloc(shape, dt, name))
            em("sync", lambda e, t=t, name=name: e.dma_start(out=t.t[:], in_=C[name]), writes=[t.b], dma=True)
            return t

        ident = cload("ident", [128, 128], BF16)
        tri = cload("tri", [128, 128], BF16)
        e2 = cload("e2", [128, 32, 128], BF16)
        mask_lt = cload("mask_lt", [128, 4, 512], BF16)
        mask_le = cload("mask_le", [128, 4, 512], BF16)
        ones_bf = T(ar.alloc([128, 128], BF16, "ones"))
        em("gpsimd", lambda e: e.memset(ones_bf.t[:], 1.0), writes=[ones_bf.b])
        neghalf = T(ar.alloc([128, 4], F32, "neghalf"))
        em("gpsimd", lambda e: e.memset(neghalf.t[:], -0.5), writes=[neghalf.b])
        persist_mark = ar.mark()

        def bcast_rows(ap_row, n):
            return bass.AP(ap_row.tensor, ap_row.offset, [[0, n]] + [list(x) for x in ap_row.ap[1:]])

        class WT:
            def __init__(self, t, ncols, cw):
                self.t = t
                self.cw = cw
                self.bufs = [Buf() for _ in range((ncols + cw - 1) // cw)]

            def rb(self, c0, c1):
                return self.bufs[c0 // self.cw:(c1 - 1) // self.cw + 1]

        def load_w_bf16(wap, ncols, name, k_chunks, cw=512):
            t = WT(ar.alloc([128, k_chunks, ncols], BF16, name), ncols, cw)
            src = wap.rearrange("(kc p) n -> p kc n", p=128)
            kstep = max(1, min(k_chunks, 4096 // cw))
            for ci, c0 in enumerate(range(0, ncols, cw)):
                c1 = min(ncols, c0 + cw)
                for k0 in range(0, k_chunks, kstep):
                    k1 = min(k_chunks, k0 + kstep)
                    em("gpsimd", lambda e, k0=k0, k1=k1, c0=c0, c1=c1: e.dma_start(
                        out=t.t[:, k0:k1, c0:c1], in_=src[:, k0:k1, c0:c1]),
                       writes=[t.bufs[ci]], dma=True, disjoint=True)
            return t

        def rmsnorm_tile(h_ap, hbufs, gam, ubf, ss, ms, rstd, junk, col):
            em("scalar", lambda e: e.activation(out=junk.t[:], in_=h_ap, func=AF.Square,
                                                 accum_out=ss.t[:, col:col + 1]),
               reads=hbufs, writes=[junk.b, ss.b])
            em("gpsimd", lambda e: e.tensor_scalar(out=ms.t[:, col:col + 1], in0=ss.t[:, col:col + 1],
                                                    scalar1=1.0 / D, scalar2=EPS, op0=ALU.mult, op1=ALU.add),
               reads=[ss.b], writes=[ms.b])
            em("gpsimd", lambda e: e.tensor_tensor(out=rstd.t[:, col:col + 1], in0=ms.t[:, col:col + 1],
                                                    in1=neghalf.t[:, 0:1], op=ALU.pow),
               reads=[ms.b, neghalf.b], writes=[rstd.b])
            em("vector", lambda e: e.scalar_tensor_tensor(out=ubf.t[:], in0=h_ap, scalar=rstd.t[:, col:col + 1],
                                                          in1=gam.t[:], op0=ALU.mult, op1=ALU.mult),
               reads=hbufs + [rstd.b, gam.b], writes=[ubf.b])

        def transpose_to(ubf, nchunk, dstT, dst_cols):
            def pe(e):
                ins = None
                for c in range(nchunk):
                    ins = e.transpose(out=PST.t[:, c * 128:(c + 1) * 128], in_=ubf.t[:, c * 128:(c + 1) * 128],
                                      identity=ident.t[:])
                return ins
            em("tensor", pe, reads=[ubf.b, ident.b], writes=[PST.b])
            em("scalar", lambda e: e.activation(
                out=dstT.t[:, 0:nchunk, dst_cols],
                in_=PST.t[:, 0:nchunk * 128].rearrange("p (c t) -> p c t", t=128), func=AF.Copy),
               reads=[PST.b], writes=[dstT.b])

        evac_rr = [0]

        def evac(dst_ap, dst_bufs, ps, scale=None, eng=None):
            if eng is None:
                eng = "scalar" if evac_rr[0] % 2 == 0 else "vector"
                evac_rr[0] += 1
            if eng == "scalar":
                if scale is None:
                    em("scalar", lambda e: e.activation(out=dst_ap, in_=ps.t[:], func=AF.Copy),
                       reads=[ps.b], writes=dst_bufs)
                else:
                    em("scalar", lambda e: e.activation(out=dst_ap, in_=ps.t[:], func=AF.Copy, scale=scale),
                       reads=[ps.b], writes=dst_bufs)
            else:
                if scale is None:
                    em("vector", lambda e: e.tensor_copy(out=dst_ap, in_=ps.t[:]), reads=[ps.b], writes=dst_bufs)
                else:
                    em("vector", lambda e: e.tensor_scalar(out=dst_ap, in0=ps.t[:], scalar1=scale, scalar2=None,
                                                            op0=ALU.mult),
                       reads=[ps.b], writes=dst_bufs)

        def stage_A(li, hsrc, hsrc_bufs):
            kind = li % 3
            j = li // 3
            ar.reset(persist_mark)
            sc.fence()
            if kind == 0:
                w_ap, ncols, nq, nk, vcols = W["w_in_sb"][j], 3072, 8, 8, 1024
            elif kind == 1:
                w_ap, ncols, nq, nk, vcols = W["w_in_fox"][j], 3088, 8, 8, 1024
            else:
                w_ap, ncols, nq, nk, vcols = W["w_in_swa"][j], 1536, 8, 2, 256
            voff = (nq + nk) * 128
            if kind == 1:
                negb = T(ar.alloc([16, 1], F32, "negb"))
                em("sync", lambda e: e.dma_start(out=negb.t[:], in_=W["b_forget"]), writes=[negb.b], dma=True)
                em("gpsimd", lambda e: e.tensor_scalar(out=negb.t[:], in0=negb.t[:], scalar1=-1.0, scalar2=None,
                                                        op0=ALU.mult), reads=[negb.b], writes=[negb.b])
                aT = T(ar.alloc([16, S], F32, "aT"))
                spt = [T(ar.alloc([16, 512], F32, "spt")) for _ in range(2)]
                ones16 = T(ar.alloc([16, 512], F32, "ones16"))
                em("gpsimd", lambda e: e.memset(ones16.t[:], 1.0), writes=[ones16.b])
                tail_mark = ar.mark()
            wsb = load_w_bf16(w_ap, ncols, "w_in", 8)
            gam = T(ar.alloc([128, D], F32, "gam"))
            em("sync", lambda e: e.dma_start(out=gam.t[:], in_=bcast_rows(W["attn_norm"][li:li + 1, :], 128)),
               writes=[gam.b], dma=True)
            ht = [T(ar.alloc([128, 4, D], F32, "ht")) for _ in range(2)]
            ubf = [T(ar.alloc([128, D], BF16, "ubf")) for _ in range(2)]
            junk = T(ar.alloc([128, D], BF16, "junk"))
            uT = [T(ar.alloc([128, 8, 512], BF16, "uT")) for _ in range(2)]
            qko = [T(ar.alloc([128, 512], BF16, "qko")) for _ in range(3)]
            vo = [T(ar.alloc([128, vcols], BF16, "vo")) for _ in range(2)]
            ss = [T(ar.alloc([128, 4], F32, "ss")) for _ in range(2)]
            ms = [T(ar.alloc([128, 4], F32, "ms")) for _ in range(2)]
            rstd = [T(ar.alloc([128, 4], F32, "rstd")) for _ in range(2)]
            st = {"nqo": 0, "pi": 0}

            def load(g):
                hb = ht[g % 2]
                em("sync", lambda e: e.dma_start(
                    out=hb.t[:], in_=hsrc[g * 512:(g + 1) * 512, :].rearrange("(t p) d -> p t d", p=128)),
                   reads=hsrc_bufs[4 * g:4 * g + 4], writes=[hb.b], dma=True)

            def norm(g, t):
                hb = ht[g % 2]
                rmsnorm_tile(hb.t[:, t, :], [hb.b], gam, ubf[t % 2], ss[g % 2], ms[g % 2], rstd[g % 2], junk, t)

            def trans(g, t):
                transpose_to(ubf[t % 2], 8, uT[g % 2], slice(t * 128, (t + 1) * 128))

            def qk_block(g, fb):
                u_t = uT[g % 2]
                ps = PS[st["pi"] % 4]
                st["pi"] += 1

                def pe(e):
                    ins = None
                    for kc in range(8):
                        ins = e.matmul(ps.t[:], wsb.t[:, kc, fb * 128:(fb + 1) * 128], u_t.t[:, kc, :],
                                       start=(kc == 0), stop=(kc == 7))
                    return ins
                em("tensor", pe, reads=wsb.rb(fb * 128, (fb + 1) * 128) + [u_t.b], writes=[ps.b])
                qo = qko[st["nqo"] % 3]
                st["nqo"] += 1
                evac(qo.t[:], [qo.b], ps, scale=(0.125 if fb < nq else None))
                if fb < nq:
                    dst, db = QT[fb * 128:(fb + 1) * 128, g * 512:(g + 1) * 512], dQT
                else:
                    dst, db = KT[(fb - nq) * 128:(fb - nq + 1) * 128, g * 512:(g + 1) * 512], dKT
                em("sync", lambda e: e.dma_start(out=dst, in_=qo.t[:]), reads=[qo.b], writes=[db], dma=True, disjoint=True)

            def v_block(g, t):
                u_t = uT[g % 2]
                vt = vo[t % 2]
                for c0 in range(0, vcols, 512):
                    cw = min(512, vcols - c0)
                    ps = PS[st["pi"] % 4]
                    st["pi"] += 1

                    def pe(e, c0=c0, cw=cw, ps=ps):
                        ins = None
                        for kc in range(8):
                            ins = e.matmul(ps.t[:, 0:cw], u_t.t[:, kc, t * 128:(t + 1) * 128],
                                           wsb.t[:, kc, voff + c0:voff + c0 + cw], start=(kc == 0), stop=(kc == 7))
                        return ins
                    em("tensor", pe, reads=wsb.rb(voff + c0, voff + c0 + cw) + [u_t.b], writes=[ps.b])
                    eng = "scalar" if evac_rr[0] % 2 == 0 else "vector"
                    evac_rr[0] += 1
                    if eng == "scalar":
                        em("scalar", lambda e, ps=ps, c0=c0, cw=cw: e.activation(
                            out=vt.t[:, c0:c0 + cw], in_=ps.t[:, 0:cw], func=AF.Copy), reads=[ps.b], writes=[vt.b])
                    else:
                        em("vector", lambda e, ps=ps, c0=c0, cw=cw: e.tensor_copy(
                            out=vt.t[:, c0:c0 + cw], in_=ps.t[:, 0:cw]), reads=[ps.b], writes=[vt.b])
                r0 = g * 512 + t * 128
                em("sync", lambda e: e.dma_start(out=Vd[r0:r0 + 128, 0:vcols], in_=vt.t[:]),
                   reads=[vt.b], writes=[dV], dma=True, disjoint=True)

            def fox_block(g):
                u_t = uT[g % 2]
                ps = PS[4]

                def pe(e):
                    ins = None
                    for kc in range(8):
                        ins = e.matmul(ps.t[0:16, :], wsb.t[:, kc, 3072:3088], u_t.t[:, kc, :],
                                       start=(kc == 0), stop=(kc == 7))
                    return ins
                em("tensor", pe, reads=wsb.rb(3072, 3088) + [u_t.b], writes=[ps.b])
                sp_ = spt[g % 2]
                em("scalar", lambda e: e.activation(out=sp_.t[:], in_=ps.t[0:16, :], func=AF.Softplus,
                                                     bias=negb.t[:], scale=-1.0),
                   reads=[ps.b, negb.b], writes=[sp_.b])
                init = 0.0 if g == 0 else aT.t[:, g * 512 - 1:g * 512]
                em("vector", lambda e: e.tensor_tensor_scan(
                    out=aT.t[:, g * 512:(g + 1) * 512], data0=ones16.t[:], data1=sp_.t[:], initial=init,
                    op0=ALU.mult, op1=ALU.add), reads=[sp_.b, ones16.b, aT.b], writes=[aT.b])

            load(0)
            for t in range(4):
                norm(0, t)
                trans(0, t)
            for g in range(NG):
                work = [("qk", fb) for fb in range(nq + nk)] + [("v", t) for t in range(4)]
                if kind == 1:
                    work.append(("f", 0))
                nxt = g + 1 < NG
                if nxt:
                    load(g + 1)
                per = (len(work) + 3) // 4
                for part in range(4):
                    if nxt:
                        norm(g + 1, part)
                    for kind_, a in work[part * per:(part + 1) * per]:
                        if kind_ == "qk":
                            qk_block(g, a)
                        elif kind_ == "v":
                            v_block(g, a)
                        else:
                            fox_block(g)
                    if nxt:
                        trans(g + 1, part)
            if kind == 1:
                sc.fence()
                ar.reset(tail_mark)
                a3 = T(ar.alloc([16, 3, S], BF16, "a3"))
                r1 = T(ar.alloc([16, S], F32, "r1"))
                v = "vector"
                em(v, lambda e: e.tensor_copy(out=a3.t[:, 0, :], in_=aT.t[:]), reads=[aT.b], writes=[a3.b])
                em(v, lambda e: e.tensor_tensor(out=r1.t[:], in0=aT.t[:], in1=a3.t[:, 0, :], op=ALU.subtract),
                   reads=[aT.b, a3.b], writes=[r1.b])
                em(v, lambda e: e.tensor_copy(out=a3.t[:, 1, :], in_=r1.t[:]), reads=[r1.b], writes=[a3.b])
                em(v, lambda e: e.tensor_tensor(out=r1.t[:], in0=r1.t[:], in1=a3.t[:, 1, :], op=ALU.subtract),
                   reads=[r1.b, a3.b], writes=[r1.b])
                em(v, lambda e: e.tensor_copy(out=a3.t[:, 2, :], in_=r1.t[:]), reads=[r1.b], writes=[a3.b])
                em("sync", lambda e: e.dma_start(out=A3, in_=a3.t[:]), reads=[a3.b], writes=[dA3], dma=True, disjoint=True)
                na3 = T(ar.alloc([16, 3, S], BF16, "na3"))
                em(v, lambda e: e.tensor_scalar(out=na3.t[:], in0=a3.t[:], scalar1=-1.0, scalar2=None, op0=ALU.mult),
                   reads=[a3.b], writes=[na3.b])
                em("sync", lambda e: e.dma_start(out=NA3, in_=na3.t[:]), reads=[na3.b], writes=[dA3], dma=True, disjoint=True)

        def stage_B_full(li):
            kind = li % 3
            fox = (kind == 1)
            ar.reset(persist_mark)
            sc.fence()
            kta = [[T(ar.alloc([128, S], BF16, "kta")) for _ in range(2)] for _ in range(2)]
            qta = [[T(ar.alloc([128, S], BF16, "qta")) for _ in range(2)] for _ in range(2)]
            vp = [T(ar.alloc([128, NKB, 128], BF16, "vp")) for _ in range(2)]
            for s_ in range(2):
                for hh in range(2):
                    for tt_ in (kta[s_][hh], qta[s_][hh]):
                        em("vector", lambda e, tt_=tt_: e.memset(tt_.t[64:128, :], 0.0), writes=[tt_.b])
                        if fox:
                            em("vector", lambda e, tt_=tt_: e.memset(tt_.t[64:70, :], 1.0), writes=[tt_.b])
            if not fox:
                Lt = ar.alloc([128, NKB, 512], BF16, "Lt")
                Ltb = [Buf() for _ in range(NKB)]
                T2 = T(ar.alloc([128, 512], BF16, "T2"))
                em("vector", lambda e: e.memset(T2.t[:], 0.0), writes=[T2.b])
            wt = [T(ar.alloc([128, 512], BF16, "wt")) for _ in range(6)]
            otile = [T(ar.alloc([128, 512], BF16, "otile")) for _ in range(3)]
            rec = [T(ar.alloc([128, 512], F32, "rec")) for _ in range(2)]
            pvq = []
            mask = mask_le if fox else mask_lt
            wti = [0]
            if fox:
                ps_s = [PS[0], PS[1], PS[2]]
                ps_o = [PS[3], PS[4]]
                ps_d = [PS[5], PS[6]]
            else:
                ps_z = [PS[0], PS[1]]
                ps_T = PS[2]
                ps_c = [PS[3], PS[4]]
                ps_o = [PS[5], PS[6]]
            cnt = {"z": 0, "c": 0, "o": 0, "s": 0}

            def load_pair(hp, sset):
                for hh in range(2):
                    h = 2 * hp + hh
                    k_t, q_t = kta[sset][hh], qta[sset][hh]
                    em("sync", lambda e, k_t=k_t, h=h: e.dma_start(out=k_t.t[0:64, :], in_=KT[h * 64:(h + 1) * 64, :]),
                       reads=[dKT], writes=[k_t.b], dma=True)
                    em("sync", lambda e, q_t=q_t, h=h: e.dma_start(out=q_t.t[0:64, :], in_=QT[h * 64:(h + 1) * 64, :]),
                       reads=[dQT], writes=[q_t.b], dma=True)
                    if fox:
                        em("sync", lambda e, k_t=k_t, h=h: e.dma_start(out=k_t.t[64:67, :], in_=A3[h]),
                           reads=[dA3], writes=[k_t.b], dma=True)
                        em("sync", lambda e, q_t=q_t, h=h: e.dma_start(out=q_t.t[67:70, :], in_=NA3[h]),
                           reads=[dA3], writes=[q_t.b], dma=True)
                v_t = vp[sset]
                em("sync", lambda e, v_t=v_t, hp=hp: e.dma_start(
                    out=v_t.t[:], in_=Vd[:, hp * 128:(hp + 1) * 128].rearrange("(kb p) c -> p kb c", p=128)),
                   reads=[dV], writes=[v_t.b], dma=True)

            def do_tile(hp, i, sset, v_t):
                if True:
                    nkb = 4 * i + 4
                    qs = slice(i * 512, (i + 1) * 512)
                    ot = otile[(hp * NG + i) % 3]
                    for hh in range(2):
                        do_head(hp, i, sset, v_t, nkb, qs, ot, hh)
                    pvq.append(lambda: em("sync", lambda e: e.dma_start(out=OT[hp * 128:(hp + 1) * 128, qs], in_=ot.t[:]),
                                          reads=[ot.b], writes=[dOT], dma=True, disjoint=True))
                    if not fox:
                        while pvq:
                            pvq.pop(0)()

            def do_head(hp, i, sset, v_t, nkb, qs, ot, hh):
                if True:
                    if True:
                        k_t, q_t = kta[sset][hh], qta[sset][hh]
                        rows = slice(hh * 64, (hh + 1) * 64)
                        pso = ps_o[cnt["o"] % 2]
                        cnt["o"] += 1
                        if fox:
                            psd = ps_d[(cnt["o"] - 1) % 2]
                            q0 = i * 512
                            for kb in range(nkb):
                                ks = slice(kb * 128, (kb + 1) * 128)
                                pss = ps_s[cnt["s"] % 3]
                                cnt["s"] += 1
                                jj = kb - 4 * i
                                c0 = max(jj, 0) * 128
                                em("tensor", lambda e, pss=pss, ks=ks, c0=c0: e.matmul(
                                    pss.t[:, c0:], k_t.t[:, ks], q_t.t[:, q0 + c0:q0 + 512], start=True, stop=True),
                                   reads=[k_t.b, q_t.b], writes=[pss.b])
                                w_ = wt[wti[0] % len(wt)]
                                wti[0] += 1
                                em("scalar", lambda e, w_=w_, pss=pss, c0=c0: e.activation(
                                    out=w_.t[:, c0:], in_=pss.t[:, c0:], func=AF.Exp),
                                   reads=[pss.b], writes=[w_.b])
                                if jj >= 0:
                                    em("vector", lambda e, w_=w_, jj=jj, c0=c0: e.scalar_tensor_tensor(
                                        out=w_.t[:, c0:], in0=w_.t[:, c0:], scalar=3.0e38, in1=mask.t[:, jj, c0:],
                                        op0=ALU.min, op1=ALU.mult),
                                       reads=[w_.b, mask.b], writes=[w_.b])

                                def pv(pk=kb, pw=w_, pc0=c0):
                                    def pe(e):
                                        e.matmul(pso.t[:, pc0:], v_t.t[:, pk, :], pw.t[:, pc0:], start=(pk == 0), stop=(pk == nkb - 1))
                                        return e.matmul(psd.t[:, pc0:], ones_bf.t[:], pw.t[:, pc0:], start=(pk == 0), stop=(pk == nkb - 1))
                                    em("tensor", pe, reads=[v_t.b, pw.b, ones_bf.b], writes=[pso.b, psd.b])
                                pvq.append(pv)
                                while len(pvq) > 2:
                                    pvq.pop(0)()
                            rc = rec[hh]

                            def epi():
                                em("vector", lambda e: e.reciprocal(out=rc.t[rows, :], in_=psd.t[rows, :]),
                                   reads=[psd.b], writes=[rc.b])
                                em("vector", lambda e: e.tensor_tensor(
                                    out=ot.t[rows, :], in0=pso.t[rows, :], in1=rc.t[rows, :], op=ALU.mult),
                                   reads=[pso.b, rc.b], writes=[ot.b])
                            pvq.append(epi)
                        else:
                            prev = None
                            for kb in range(nkb + 1):
                                cur = None
                                if kb < nkb:
                                    ks = slice(kb * 128, (kb + 1) * 128)
                                    psz = ps_z[cnt["z"] % 2]
                                    cnt["z"] += 1
                                    em("tensor", lambda e, psz=psz, k_t=k_t, q_t=q_t, ks=ks: e.matmul(
                                        psz.t[:], k_t.t[:, ks], q_t.t[:, qs], start=True, stop=True),
                                       reads=[k_t.b, q_t.b], writes=[psz.b])
                                    em("scalar", lambda e, psz=psz, kb=kb: e.activation(
                                        out=Lt[:, kb, :], in_=psz.t[:], func=AF.Softplus),
                                       reads=[psz.b], writes=[Ltb[kb]])
                                    jj = kb - 4 * i
                                    if jj >= 0:
                                        em("vector", lambda e, kb=kb, jj=jj: e.tensor_tensor(
                                            out=Lt[:, kb, :], in0=Lt[:, kb, :], in1=mask.t[:, jj, :], op=ALU.mult),
                                           reads=[Ltb[kb], mask.b], writes=[Ltb[kb]])
                                    cur = kb
                                if prev is not None:
                                    pk = prev
                                    em("tensor", lambda e, pk=pk: e.matmul(
                                        ps_T.t[:], e2.t[:, pk, :], Lt[:, pk, :], start=(pk == 0), stop=(pk == nkb - 1)),
                                       reads=[e2.b, Ltb[pk]], writes=[ps_T.b])
                                prev = cur
                            em("vector", lambda e: e.tensor_copy(out=T2.t[0:64, :], in_=ps_T.t[0:64, :]), reads=[ps_T.b], writes=[T2.b])
                            em("vector", lambda e: e.tensor_tensor(out=T2.t[32:64, :], in0=ps_T.t[32:64, :],
                                                                   in1=T2.t[32:64, :], op=ALU.subtract),
                               reads=[ps_T.b, T2.b], writes=[T2.b])
                            prev = None
                            for kb in range(nkb + 1):
                                cur = None
                                if kb < nkb:
                                    ks = slice(kb * 128, (kb + 1) * 128)
                                    psc = ps_c[cnt["c"] % 2]
                                    cnt["c"] += 1

                                    def pe(e, psc=psc, kb=kb, ks=ks, k_t=k_t, q_t=q_t):
                                        e.matmul(psc.t[:], tri.t[:], Lt[:, kb, :], start=True, stop=False)
                                        e.matmul(psc.t[:], selall.t[:, kb, :], T2.t[:], start=False, stop=False)
                                        return e.matmul(psc.t[:], k_t.t[:, ks], q_t.t[:, qs], start=False, stop=True)
                                    em("tensor", pe, reads=[tri.b, Ltb[kb], selall.b, T2.b, k_t.b, q_t.b], writes=[psc.b])
                                    w_ = wt[wti[0] % 3]
                                    wti[0] += 1
                                    em("scalar", lambda e, w_=w_, psc=psc: e.activation(
                                        out=w_.t[:], in_=psc.t[:], func=AF.Exp),
                                       reads=[psc.b], writes=[w_.b])
                                    jj = kb - 4 * i
                                    if jj >= 0:
                                        em("vector", lambda e, w_=w_, jj=jj: e.scalar_tensor_tensor(
                                            out=w_.t[:], in0=w_.t[:], scalar=3.0e38, in1=mask.t[:, jj, :],
                                            op0=ALU.min, op1=ALU.mult),
                                           reads=[w_.b, mask.b], writes=[w_.b])
                                    cur = (kb, w_)
                                if prev is not None:
                                    pk, pw = prev
                                    em("tensor", lambda e, pk=pk, pw=pw, pso=pso, v_t=v_t: e.matmul(
                                        pso.t[:], v_t.t[:, pk, :], pw.t[:], start=(pk == 0), stop=(pk == nkb - 1)),
                                       reads=[v_t.b, pw.b], writes=[pso.b])
                                prev = cur
                            em("vector", lambda e, pso=pso, ot=ot, rows=rows: e.tensor_copy(out=ot.t[rows, :], in_=pso.t[rows, :]),
                               reads=[pso.b], writes=[ot.b])

            load_pair(0, 0)
            for hp in range(NH // 2):
                sset = hp % 2
                if hp + 1 < NH // 2:
                    load_pair(hp + 1, 1 - sset)
                v_t = vp[sset]
                for i in range(NG):
                    do_tile(hp, i, sset, v_t)
                while pvq:
                    pvq.pop(0)()


        def stage_B_sb(li):
            ar.reset(persist_mark)
            sc.fence()
            kta = [[T(ar.alloc([128, S], BF16, "kta")) for _ in range(2)] for _ in range(2)]
            qta = [[T(ar.alloc([128, S], BF16, "qta")) for _ in range(2)] for _ in range(2)]
            vp = [T(ar.alloc([128, NKB, 128], BF16, "vp")) for _ in range(2)]
            for s_ in range(2):
                for hh in range(2):
                    k_ = kta[s_][hh]
                    em("vector", lambda e, k_=k_: e.memset(k_.t[64:128, :], 0.0), writes=[k_.b])
                    em("sync", lambda e, k_=k_: e.dma_start(
                        out=k_.t[64:96, :].rearrange("p (kb s) -> p kb s", s=128), in_=C["ksel"][:, 0:NKB, :]),
                       writes=[k_.b], dma=True)
            Lt = [ar.alloc([128, NKB, 512], BF16, "Lt") for _ in range(2)]
            Ltb = [[Buf() for _ in range(NKB)] for _ in range(2)]
            wt = [T(ar.alloc([128, 2, 512], BF16, "wt")) for _ in range(4)]
            otile = [T(ar.alloc([128, 512], BF16, "otile")) for _ in range(2)]
            ps_T = [PS[4], PS[4]]
            ps_o = [PS[5], PS[6]]
            cnt = {"zc": 0, "w": 0}

            def load_pair(hp, sset):
                for hh in range(2):
                    h = 2 * hp + hh
                    k_t, q_t = kta[sset][hh], qta[sset][hh]
                    em("sync", lambda e, k_t=k_t, h=h: e.dma_start(out=k_t.t[0:64, :], in_=KT[h * 64:(h + 1) * 64, :]),
                       reads=[dKT], writes=[k_t.b], dma=True)
                    em("vector", lambda e, q_t=q_t: e.memset(q_t.t[64:128, :], 0.0), writes=[q_t.b])
                    em("sync", lambda e, q_t=q_t, h=h: e.dma_start(out=q_t.t[0:64, :], in_=QT[h * 64:(h + 1) * 64, :]),
                       reads=[dQT], writes=[q_t.b], dma=True)
                v_t = vp[sset]
                em("sync", lambda e, v_t=v_t, hp=hp: e.dma_start(
                    out=v_t.t[:], in_=Vd[:, hp * 128:(hp + 1) * 128].rearrange("(kb p) c -> p kb c", p=128)),
                   reads=[dV], writes=[v_t.b], dma=True)


            def units(i):
                return [(kb, kb + 1) for kb in range(0, 4 * i, 2)] + [(kb,) for kb in range(4 * i, 4 * i + 4)]

            def phase_a(i, hh, k_t, q_t):
                nkb = 4 * i + 4
                q0 = i * 512
                L, Lb, pT = Lt[hh], Ltb[hh], ps_T[hh]
                prev = None
                for u in units(i) + [None]:
                    if u is not None:
                        W_ = PSW[cnt["zc"] % 2]
                        cnt["zc"] += 1
                        jj = u[0] - 4 * i
                        c0 = max(jj, 0) * 128

                        def pe(e, u=u, W_=W_, c0=c0):
                            ins = None
                            for n_, kb in enumerate(u):
                                ins = e.matmul(W_.t[:, n_ * 512 + c0:(n_ + 1) * 512], k_t.t[:, kb * 128:(kb + 1) * 128],
                                               q_t.t[:, q0 + c0:q0 + 512], start=True, stop=True)
                            return ins
                        em("tensor", pe, reads=[k_t.b, q_t.b], writes=[W_.b])
                        if len(u) == 2:
                            em("scalar", lambda e, u=u, W_=W_: e.activation(
                                out=L[:, u[0]:u[0] + 2, :], in_=W_.t[:].rearrange("p (a c) -> p a c", c=512), func=AF.Softplus),
                               reads=[W_.b], writes=[Lb[u[0]], Lb[u[1]]])
                        else:
                            kb = u[0]
                            em("scalar", lambda e, kb=kb, W_=W_, c0=c0: e.activation(
                                out=L[:, kb, c0:], in_=W_.t[:, c0:512], func=AF.Softplus), reads=[W_.b], writes=[Lb[kb]])
                            em("vector", lambda e, kb=kb, jj=jj, c0=c0: e.tensor_tensor(
                                out=L[:, kb, c0:], in0=L[:, kb, c0:], in1=mask_lt.t[:, jj, c0:], op=ALU.mult),
                               reads=[Lb[kb], mask_lt.b], writes=[Lb[kb]])
                    if prev is not None:
                        pu, pc0 = prev

                        def peT(e, pu=pu, pc0=pc0):
                            ins = None
                            for pk in pu:
                                ins = e.matmul(pT.t[:, pc0:], e2.t[:, pk, :], L[:, pk, pc0:], start=(pk == 0), stop=(pk == nkb - 1))
                            return ins
                        em("tensor", peT, reads=[e2.b] + [Lb[pk] for pk in pu], writes=[pT.b])
                    prev = (u, c0) if u is not None else None
                em("vector", lambda e: e.tensor_copy(out=q_t.t[64:96, q0:q0 + 512], in_=pT.t[64:96, :]),
                   reads=[pT.b], writes=[q_t.b])

            def phase_b(i, hh, k_t, q_t, v_t, ot):
                nkb = 4 * i + 4
                q0 = i * 512
                L, Lb, pso = Lt[hh], Ltb[hh], ps_o[hh]
                rows = slice(hh * 64, (hh + 1) * 64)
                prev = None
                for u in units(i) + [None]:
                    if u is not None:
                        W_ = PSW[cnt["zc"] % 2]
                        cnt["zc"] += 1
                        jj = u[0] - 4 * i
                        c0 = max(jj, 0) * 128

                        def pe(e, u=u, W_=W_, c0=c0):
                            ins = None
                            for n_, kb in enumerate(u):
                                o_ = W_.t[:, n_ * 512 + c0:(n_ + 1) * 512]
                                e.matmul(o_, tri.t[:], L[:, kb, c0:], start=True, stop=False)
                                ins = e.matmul(o_, k_t.t[:, kb * 128:(kb + 1) * 128], q_t.t[:, q0 + c0:q0 + 512],
                                               start=False, stop=True)
                            return ins
                        em("tensor", pe, reads=[tri.b, k_t.b, q_t.b] + [Lb[kb] for kb in u], writes=[W_.b])
                        w_ = wt[cnt["w"] % 4]
                        cnt["w"] += 1
                        if len(u) == 2:
                            em("scalar", lambda e, w_=w_, W_=W_: e.activation(
                                out=w_.t[:], in_=W_.t[:].rearrange("p (a c) -> p a c", c=512), func=AF.Exp),
                               reads=[W_.b], writes=[w_.b])
                        else:
                            em("scalar", lambda e, w_=w_, W_=W_, c0=c0: e.activation(
                                out=w_.t[:, 0, c0:], in_=W_.t[:, c0:512], func=AF.Exp), reads=[W_.b], writes=[w_.b])
                            em("vector", lambda e, w_=w_, jj=jj, c0=c0: e.scalar_tensor_tensor(
                                out=w_.t[:, 0, c0:], in0=w_.t[:, 0, c0:], scalar=3.0e38, in1=mask_lt.t[:, jj, c0:],
                                op0=ALU.min, op1=ALU.mult), reads=[w_.b, mask_lt.b], writes=[w_.b])
                    if prev is not None:
                        pu, pw, pc0 = prev

                        def pv(e, pu=pu, pw=pw, pc0=pc0):
                            ins = None
                            for n_, pk in enumerate(pu):
                                ins = e.matmul(pso.t[:, pc0:], v_t.t[:, pk, :], pw.t[:, n_, pc0:],
                                               start=(pk == 0), stop=(pk == nkb - 1))
                            return ins
                        em("tensor", pv, reads=[v_t.b, pw.b], writes=[pso.b])
                    prev = (u, w_, c0) if u is not None else None
                em("vector", lambda e: e.tensor_copy(out=ot.t[rows, :], in_=pso.t[rows, :]), reads=[pso.b], writes=[ot.b])

            def do_tile(hp, i, sset):
                v_t = vp[sset]
                ot = otile[(hp * NG + i) % 2]
                qs = slice(i * 512, (i + 1) * 512)
                for hh in range(2):
                    phase_a(i, hh, kta[sset][hh], qta[sset][hh])
                for hh in range(2):
                    phase_b(i, hh, kta[sset][hh], qta[sset][hh], v_t, ot)
                em("sync", lambda e: e.dma_start(out=OT[hp * 128:(hp + 1) * 128, qs], in_=ot.t[:]),
                   reads=[ot.b], writes=[dOT], dma=True, disjoint=True)

            load_pair(0, 0)
            for hp in range(NH // 2):
                sset = hp % 2
                if hp + 1 < NH // 2:
                    load_pair(hp + 1, 1 - sset)
                for i in range(NG):
                    do_tile(hp, i, sset)

        def stage_B_fox(li):
            ar.reset(persist_mark)
            sc.fence()
            kta = [[T(ar.alloc([128, S], BF16, "kta")) for _ in range(2)] for _ in range(2)]
            qta = [[T(ar.alloc([128, S], BF16, "qta")) for _ in range(2)] for _ in range(2)]
            vp = [[T(ar.alloc([128, NKB, 128], BF16, "vp")) for _ in range(2)] for _ in range(2)]
            for s_ in range(2):
                for hh in range(2):
                    for tt_ in (kta[s_][hh], qta[s_][hh]):
                        em("vector", lambda e, tt_=tt_: e.memset(tt_.t[64:128, :], 0.0), writes=[tt_.b])
                        em("vector", lambda e, tt_=tt_: e.memset(tt_.t[64:70, :], 1.0), writes=[tt_.b])
            wt = [T(ar.alloc([128, 2, 512], BF16, "wt")) for _ in range(5)]
            otile = [T(ar.alloc([128, 512], BF16, "otile")) for _ in range(3)]
            rec = [T(ar.alloc([128, 512], F32, "rec")) for _ in range(2)]
            rec2 = [T(ar.alloc([128, 512], F32, "rec2")) for _ in range(2)]
            pvq = []
            ps_o = [PS[4], PS[5], PS[6]]
            cnt = {"w": 0, "s": 0, "o": 0}

            def load_pair(hp, sset):
                for hh in range(2):
                    h = 2 * hp + hh
                    k_t, q_t, v_t = kta[sset][hh], qta[sset][hh], vp[sset][hh]
                    em("sync", lambda e, k_t=k_t, h=h: e.dma_start(out=k_t.t[0:64, :], in_=KT[h * 64:(h + 1) * 64, :]),
                       reads=[dKT], writes=[k_t.b], dma=True)
                    em("sync", lambda e, q_t=q_t, h=h: e.dma_start(out=q_t.t[0:64, :], in_=QT[h * 64:(h + 1) * 64, :]),
                       reads=[dQT], writes=[q_t.b], dma=True)
                    em("sync", lambda e, k_t=k_t, h=h: e.dma_start(out=k_t.t[64:67, :], in_=A3[h]),
                       reads=[dA3], writes=[k_t.b], dma=True)
                    em("sync", lambda e, q_t=q_t, h=h: e.dma_start(out=q_t.t[67:70, :], in_=NA3[h]),
                       reads=[dA3], writes=[q_t.b], dma=True)
                    em("sync", lambda e, v_t=v_t, hp=hp: e.dma_start(
                        out=v_t.t[:], in_=Vd[:, hp * 128:(hp + 1) * 128].rearrange("(kb p) c -> p kb c", p=128)),
                       reads=[dV], writes=[v_t.b], dma=True)
                    oc = slice(64, 128) if hh == 0 else slice(0, 64)
                    em("vector", lambda e, v_t=v_t, oc=oc: e.memset(v_t.t[:, :, oc], 1.0), writes=[v_t.b])

            def units(i):
                return [(kb, kb + 1) for kb in range(0, 4 * i, 2)] + [(kb,) for kb in range(4 * i, 4 * i + 4)]

            def do_head(hp, i, sset, hh, ot):
                nkb = 4 * i + 4
                q0 = i * 512
                k_t, q_t, v_t = kta[sset][hh], qta[sset][hh], vp[sset][hh]
                orow = slice(hh * 64, (hh + 1) * 64)
                drow = slice(64, 128) if hh == 0 else slice(0, 64)
                pso = ps_o[cnt["o"] % 3]
                cnt["o"] += 1
                for u in units(i):
                    W_ = PSW[cnt["s"] % 2]
                    cnt["s"] += 1
                    jj = u[0] - 4 * i
                    c0 = max(jj, 0) * 128

                    def pe(e, u=u, W_=W_, c0=c0):
                        ins = None
                        for n_, kb in enumerate(u):
                            ins = e.matmul(W_.t[:, n_ * 512 + c0:(n_ + 1) * 512], k_t.t[:, kb * 128:(kb + 1) * 128],
                                           q_t.t[:, q0 + c0:q0 + 512], start=True, stop=True)
                        return ins
                    em("tensor", pe, reads=[k_t.b, q_t.b], writes=[W_.b])
                    w_ = wt[cnt["w"] % len(wt)]
                    cnt["w"] += 1
                    if len(u) == 2:
                        em("scalar", lambda e, w_=w_, W_=W_: e.activation(
                            out=w_.t[:], in_=W_.t[:].rearrange("p (a c) -> p a c", c=512), func=AF.Exp),
                           reads=[W_.b], writes=[w_.b])
                    else:
                        em("scalar", lambda e, w_=w_, W_=W_, c0=c0: e.activation(
                            out=w_.t[:, 0, c0:], in_=W_.t[:, c0:512], func=AF.Exp), reads=[W_.b], writes=[w_.b])
                        em("vector", lambda e, w_=w_, jj=jj, c0=c0: e.scalar_tensor_tensor(
                            out=w_.t[:, 0, c0:], in0=w_.t[:, 0, c0:], scalar=3.0e38, in1=mask_le.t[:, jj, c0:],
                            op0=ALU.min, op1=ALU.mult), reads=[w_.b, mask_le.b], writes=[w_.b])

                    def pv(pu=u, pw=w_, pc0=c0):
                        def pe2(e):
                            ins = None
                            for n_, pk in enumerate(pu):
                                ins = e.matmul(pso.t[:, pc0:], v_t.t[:, pk, :], pw.t[:, n_, pc0:],
                                               start=(pk == 0), stop=(pk == nkb - 1))
                            return ins
                        em("tensor", pe2, reads=[v_t.b, pw.b], writes=[pso.b])
                    pvq.append(pv)
                    while len(pvq) > 2:
                        pvq.pop(0)()
                rc, rc2 = rec[hh], rec2[hh]

                def epi():
                    em("vector", lambda e: e.reciprocal(out=rc.t[drow, :], in_=pso.t[drow, :]), reads=[pso.b], writes=[rc.b])
                    em("vector", lambda e: e.tensor_copy(out=rc2.t[orow, :], in_=rc.t[drow, :]), reads=[rc.b], writes=[rc2.b])
                    em("vector", lambda e: e.tensor_tensor(out=ot.t[orow, :], in0=pso.t[orow, :], in1=rc2.t[orow, :],
                                                           op=ALU.mult), reads=[pso.b, rc2.b], writes=[ot.b])
                pvq.append(epi)

            load_pair(0, 0)
            for hp in range(NH // 2):
                sset = hp % 2
                if hp + 1 < NH // 2:
                    load_pair(hp + 1, 1 - sset)
                for i in range(NG):
                    ot = otile[(hp * NG + i) % 3]
                    qs = slice(i * 512, (i + 1) * 512)
                    for hh in range(2):
                        do_head(hp, i, sset, hh, ot)
                    pvq.append(lambda hp=hp, qs=qs, ot=ot: em(
                        "sync", lambda e: e.dma_start(out=OT[hp * 128:(hp + 1) * 128, qs], in_=ot.t[:]),
                        reads=[ot.b], writes=[dOT], dma=True, disjoint=True))
                while pvq:
                    pvq.pop(0)()

        def stage_B_swa(li):
            j = li // 3
            ar.reset(persist_mark)
            sc.fence()
            relb = T(ar.alloc([32, 16], F32, "relb"))
            selb = T(ar.alloc([32, 384], F32, "selb"))
            maskv = T(ar.alloc([1, 384], F32, "maskv"))
            ones1 = T(ar.alloc([1, 16], F32, "ones1"))
            rp = T(ar.alloc([16, 384], F32, "rp"))
            em("sync", lambda e: e.dma_start(out=relb.t[:], in_=W["rel_bias"]), writes=[relb.b], dma=True)
            em("sync", lambda e: e.dma_start(out=selb.t[:], in_=C["selb"]), writes=[selb.b], dma=True)
            em("sync", lambda e: e.dma_start(out=maskv.t[:], in_=C["maskv"]), writes=[maskv.b], dma=True)
            em("gpsimd", lambda e: e.memset(ones1.t[:], 1.0), writes=[ones1.b])

            def pe(e):
                e.matmul(PS[0].t[0:16, 0:384], relb.t[:], selb.t[:], start=True, stop=False)
                return e.matmul(PS[0].t[0:16, 0:384], ones1.t[:], maskv.t[:], start=False, stop=True)
            em("tensor", pe, reads=[relb.b, selb.b, ones1.b, maskv.b], writes=[PS[0].b])
            em("vector", lambda e: e.tensor_copy(out=rp.t[:], in_=PS[0].t[0:16, 0:384]), reads=[PS[0].b], writes=[rp.b])
            for r0 in range(0, 128, 32):
                src = bass.AP(rp.t[:].tensor, rp.t[:].offset, [list(rp.t[:].ap[0]), [0, 32], [1, 384]])
                em("sync", lambda e, r0=r0, src=src: e.dma_start(out=RPD[:, r0:r0 + 32, :], in_=src),
                   reads=[rp.b], writes=[dRP], dma=True, disjoint=True)
            biasT = T(ar.alloc([128, 16, 2, 128], F32, "biasT"))
            for h in range(16):
                src = bass.AP(RPD.tensor, RPD.offset + h * 128 * 384 + 127, [[383, 128], [128, 2], [1, 128]])
                em("sync", lambda e, h=h, src=src: e.dma_start(out=biasT.t[:, h, :, :], in_=src),
                   reads=[dRP], writes=[biasT.b], dma=True)
            bias_hi = T(ar.alloc([128, 16, 2, 128], BF16, "bias_hi"))
            bias_lo = T(ar.alloc([128, 16, 2, 128], BF16, "bias_lo"))
            em("vector", lambda e: e.tensor_copy(out=bias_hi.t[:], in_=biasT.t[:]), reads=[biasT.b], writes=[bias_hi.b])
            em("vector", lambda e: e.tensor_tensor(out=biasT.t[:], in0=biasT.t[:], in1=bias_hi.t[:], op=ALU.subtract),
               reads=[biasT.b, bias_hi.b], writes=[biasT.b])
            em("vector", lambda e: e.tensor_copy(out=bias_lo.t[:], in_=biasT.t[:]), reads=[biasT.b], writes=[bias_lo.b])
            esk = T(ar.alloc([64, 16], F32, "esk"))
            em("sync", lambda e: e.dma_start(out=esk.t[:], in_=bcast_rows(W["sinks"][j:j + 1, :], 64)),
               writes=[esk.b], dma=True)
            em("scalar", lambda e: e.activation(out=esk.t[:], in_=esk.t[:], func=AF.Exp), reads=[esk.b], writes=[esk.b])
            q4 = [T(ar.alloc([128, 4, S], BF16, "q4")) for _ in range(2)]
            kg = [T(ar.alloc([128, S], BF16, "kg")) for _ in range(2)]
            vg = [T(ar.alloc([128, NKB, 128], BF16, "vg")) for _ in range(2)]
            for s_ in range(2):
                em("vector", lambda e, s_=s_: e.memset(q4[s_].t[64:128, :, :], 0.0), writes=[q4[s_].b])
                em("vector", lambda e, s_=s_: e.memset(kg[s_].t[64:128, :], 0.0), writes=[kg[s_].b])
                em("vector", lambda e, s_=s_: e.memset(vg[s_].t[:], 0.0), writes=[vg[s_].b])
            ssb = [T(ar.alloc([128, 512], F32, "ssb")) for _ in range(4)]
            wt = [T(ar.alloc([128, 512], BF16, "wt")) for _ in range(5)]
            den = [T(ar.alloc([64, 512], F32, "den")) for _ in range(2)]
            ot4 = [T(ar.alloc([64, 4, 512], BF16, "ot4")) for _ in range(2)]
            ps_s = [PS[0], PS[1], PS[2]]
            ps_o = [PS[3], PS[4]]
            ps_d = [PS[5], PS[6]]
            ci = {"s": 0, "w": 0, "o": 0, "sb": 0}

            def load_g(g, sset):
                em("sync", lambda e: e.dma_start(
                    out=q4[sset].t[0:64, :, :], in_=QT[g * 256:(g + 1) * 256, :].rearrange("(j d) t -> d j t", d=64)),
                   reads=[dQT], writes=[q4[sset].b], dma=True)
                em("sync", lambda e: e.dma_start(out=kg[sset].t[0:64, :], in_=KT[g * 64:(g + 1) * 64, :]),
                   reads=[dKT], writes=[kg[sset].b], dma=True)
                em("sync", lambda e: e.dma_start(
                    out=vg[sset].t[:, :, 0:64], in_=Vd[:, g * 64:(g + 1) * 64].rearrange("(kb p) c -> p kb c", p=128)),
                   reads=[dV], writes=[vg[sset].b], dma=True)

            do_qblock = make_do_qblock(ot4, ps_o, ps_d, ps_s, ci, ssb, wt, den, (bias_hi, bias_lo), esk)
            load_g(0, 0)
            for g in range(4):
                sset = g % 2
                if g + 1 < 4:
                    load_g(g + 1, 1 - sset)
                q_t, k_t, v_t = q4[sset], kg[sset], vg[sset]
                do_qblock(g, q_t, k_t, v_t, NKB)

        def _unused():
            pass

        def make_do_qblock(ot4, ps_o, ps_d, ps_s, ci, ssb, wt, den, biasT, esk):
            def front(g, n, q_t, k_t, v_t):
                qcols = slice(n * 128, (n + 1) * 128)
                blocks = [(n, 0)] + ([(n - 1, 1)] if n > 0 else [])
                ws = []
                for bi, (kb, c) in enumerate(blocks):
                    ks = slice(kb * 128, (kb + 1) * 128)
                    pss = ps_s[ci["s"] % 3]
                    ci["s"] += 1
                    b_hi, b_lo = biasT

                    def pe(e, pss=pss, ks=ks, c=c):
                        o3 = pss.t[:].rearrange("p (j t) -> p j t", t=128)
                        e.matmul(o3, k_t.t[:, ks], q_t.t[:, :, qcols], start=True, stop=False)
                        e.matmul(o3, ident.t[:], b_hi.t[:, 4 * g:4 * g + 4, c, :], start=False, stop=False)
                        return e.matmul(o3, ident.t[:], b_lo.t[:, 4 * g:4 * g + 4, c, :], start=False, stop=True)
                    em("tensor", pe, reads=[k_t.b, q_t.b, ident.b, b_hi.b, b_lo.b], writes=[pss.b])
                    w_ = wt[ci["w"] % len(wt)]
                    ci["w"] += 1
                    em("scalar", lambda e, w_=w_, pss=pss: e.activation(out=w_.t[:], in_=pss.t[:], func=AF.Exp),
                       reads=[pss.b], writes=[w_.b])
                    ws.append((kb, w_))
                return ws

            def back(g, n, v_t, ws):
                o4 = ot4[(n // 4) % 2]
                pso = ps_o[ci["o"] % 2]
                psd = ps_d[ci["o"] % 2]
                ci["o"] += 1
                for bi, (kb, w_) in enumerate(ws):
                    first, lastb = (bi == 0), (bi == len(ws) - 1)

                    def pe(e, w_=w_, kb=kb, first=first, lastb=lastb):
                        e.matmul(pso.t[:], v_t.t[:, kb, :], w_.t[:], start=first, stop=lastb)
                        return e.matmul(psd.t[:], ones_bf.t[:], w_.t[:], start=first, stop=lastb)
                    em("tensor", pe, reads=[v_t.b, w_.b, ones_bf.b], writes=[pso.b, psd.b])
                dn = den[n % 2]
                esb = bass.AP(esk.t[:].tensor, esk.t[:, 4 * g:4 * g + 4].offset,
                              [list(esk.t[:].ap[0]), [1, 4], [0, 128]])
                em("vector", lambda e: e.tensor_tensor(
                    out=dn.t[:].rearrange("p (j t) -> p j t", t=128),
                    in0=psd.t[0:64, :].rearrange("p (j t) -> p j t", t=128), in1=esb, op=ALU.add),
                   reads=[psd.b, esk.b], writes=[dn.b])
                em("vector", lambda e: e.reciprocal(out=dn.t[:], in_=dn.t[:]), reads=[dn.b], writes=[dn.b])
                c4 = slice((n % 4) * 128, (n % 4 + 1) * 128)
                em("vector", lambda e: e.tensor_tensor(
                    out=o4.t[:, :, c4], in0=pso.t[0:64, :].rearrange("p (j t) -> p j t", t=128),
                    in1=dn.t[:].rearrange("p (j t) -> p j t", t=128), op=ALU.mult),
                   reads=[pso.b, dn.b], writes=[o4.b])
                if n % 4 == 3:
                    n0 = (n - 3) * 128
                    em("sync", lambda e: e.dma_start(
                        out=OT[g * 256:(g + 1) * 256, n0:n0 + 512].rearrange("(j d) t -> d j t", d=64),
                        in_=o4.t[:]), reads=[o4.b], writes=[dOT], dma=True, disjoint=True)

            def do_group(g, q_t, k_t, v_t, nkb):
                ws = front(g, 0, q_t, k_t, v_t)
                for n in range(nkb):
                    ws_next = front(g, n + 1, q_t, k_t, v_t) if n + 1 < nkb else None
                    back(g, n, v_t, ws)
                    ws = ws_next
            return do_group

        def stage_CU(li, hsrc, hsrc_bufs):
            kind = li % 3
            j = li // 3
            ar.reset(persist_mark)
            sc.fence()
            wo_ap = {0: W["w_out_sb"], 1: W["w_out_fox"], 2: W["w_out_swa"]}[kind][j]
            wout = load_w_bf16(wo_ap, D, "wout", 8)
            wup = load_w_bf16(W["w_up"][li], DFF, "wup", 8)
            gam = T(ar.alloc([128, D], F32, "gam"))
            em("sync", lambda e: e.dma_start(out=gam.t[:], in_=bcast_rows(W["mlp_norm"][li:li + 1, :], 128)),
               writes=[gam.b], dma=True)
            ott = [T(ar.alloc([128, 8, 512], BF16, "ott")) for _ in range(2)]
            ht = [T(ar.alloc([128, 4, D], F32, "ht")) for _ in range(2)]
            ubf = [T(ar.alloc([128, D], BF16, "ubf")) for _ in range(2)]
            junk = T(ar.alloc([128, D], BF16, "junk"))
            uT = [T(ar.alloc([128, 8, 512], BF16, "uT")) for _ in range(2)]
            rt = [T(ar.alloc([128, 512], F32, "rt")) for _ in range(2)]
            ao = [T(ar.alloc([128, 512], BF16, "ao")) for _ in range(3)]
            ss = [T(ar.alloc([128, 4], F32, "ss")) for _ in range(2)]
            ms = [T(ar.alloc([128, 4], F32, "ms")) for _ in range(2)]
            rstd = [T(ar.alloc([128, 4], F32, "rstd")) for _ in range(2)]
            st = {"pi": 0, "ai": 0}

            def load(g):
                hb, ob = ht[g % 2], ott[g % 2]
                em("sync", lambda e: e.dma_start(
                    out=hb.t[:], in_=hsrc[g * 512:(g + 1) * 512, :].rearrange("(t p) d -> p t d", p=128)),
                   reads=hsrc_bufs[4 * g:4 * g + 4], writes=[hb.b], dma=True)
                em("sync", lambda e: e.dma_start(
                    out=ob.t[:], in_=OT[:, g * 512:(g + 1) * 512].rearrange("(c p) t -> p c t", p=128)),
                   reads=[dOT], writes=[ob.b], dma=True)

            def head(g, t):
                hb, ob = ht[g % 2], ott[g % 2]
                for half in range(2):
                    ps = PS[st["pi"] % 4]
                    st["pi"] += 1

                    def pe(e, ps=ps, half=half):
                        ins = None
                        for c in range(8):
                            ins = e.matmul(ps.t[:], ob.t[:, c, t * 128:(t + 1) * 128],
                                           wout.t[:, c, half * 512:(half + 1) * 512], start=(c == 0), stop=(c == 7))
                        return ins
                    em("tensor", pe, reads=[ob.b] + wout.rb(half * 512, (half + 1) * 512), writes=[ps.b])
                    em("vector", lambda e, ps=ps, half=half: e.tensor_tensor(
                        out=hb.t[:, t, half * 512:(half + 1) * 512], in0=ps.t[:],
                        in1=hb.t[:, t, half * 512:(half + 1) * 512], op=ALU.add),
                       reads=[ps.b, hb.b], writes=[hb.b])
                rmsnorm_tile(hb.t[:, t, :], [hb.b], gam, ubf[t % 2], ss[g % 2], ms[g % 2], rstd[g % 2], junk, t)

            def trans(g, t):
                transpose_to(ubf[t % 2], 8, uT[g % 2], slice(t * 128, (t + 1) * 128))

            def store(g):
                hb = ht[g % 2]
                em("sync", lambda e: e.dma_start(
                    out=hbuf[g * 512:(g + 1) * 512, :].rearrange("(t p) d -> p t d", p=128), in_=hb.t[:]),
                   reads=[hb.b], writes=dH[4 * g:4 * g + 4], dma=True)

            def up(g, fc):
                u_t = uT[g % 2]
                ps = PS[4 + (st["pi"] % 3)]
                st["pi"] += 1

                def pe(e):
                    ins = None
                    for kc in range(8):
                        ins = e.matmul(ps.t[:], wup.t[:, kc, fc * 128:(fc + 1) * 128], u_t.t[:, kc, :],
                                       start=(kc == 0), stop=(kc == 7))
                    return ins
                em("tensor", pe, reads=wup.rb(fc * 128, (fc + 1) * 128) + [u_t.b], writes=[ps.b])
                r_ = rt[fc % 2]
                em("scalar", lambda e: e.activation(out=r_.t[:], in_=ps.t[:], func=AF.Relu), reads=[ps.b], writes=[r_.b])
                a_ = ao[st["ai"] % 3]
                st["ai"] += 1
                em("vector", lambda e: e.tensor_tensor(out=a_.t[:], in0=r_.t[:], in1=r_.t[:], op=ALU.mult),
                   reads=[r_.b], writes=[a_.b])
                em("sync", lambda e: e.dma_start(out=AT[fc, :, g * 512:(g + 1) * 512], in_=a_.t[:]),
                   reads=[a_.b], writes=[dAT], dma=True, disjoint=True)

            load(0)
            for t in range(4):
                head(0, t)
                trans(0, t)
            store(0)
            for g in range(NG):
                nxt = g + 1 < NG
                if nxt:
                    load(g + 1)
                for part in range(4):
                    if nxt:
                        head(g + 1, part)
                    for fc in range(part * 8, part * 8 + 8):
                        up(g, fc)
                    if nxt:
                        trans(g + 1, part)
                if nxt:
                    store(g + 1)

        def stage_CD(li, final):
            ar.reset(persist_mark)
            sc.fence()
            wdn = load_w_bf16(W["w_down"][li], D, "wdn", 32)
            wgt = load_w_bf16(W["w_ple_gate"][li], D, "wgt", 8)
            wpl = load_w_bf16(W["w_ple"][li], D, "wpl", 2)
            gam = T(ar.alloc([128, D], F32, "gam"))
            em("sync", lambda e: e.dma_start(out=gam.t[:], in_=bcast_rows(W["ple_norm"][li:li + 1, :], 128)),
               writes=[gam.b], dma=True)
            if final:
                gamf = T(ar.alloc([128, D], F32, "gamf"))
                em("sync", lambda e: e.dma_start(out=gamf.t[:], in_=bcast_rows(W["final_norm"][0:1, :], 128)),
                   writes=[gamf.b], dma=True)
            TG = 256
            TPG = TG // 128
            ntile = S // 128
            ain = [T(ar.alloc([128, 32, TG], BF16, "ain")) for _ in range(2)]
            ht = [T(ar.alloc([128, D], F32, "ht")) for _ in range(3)]
            pt = [T(ar.alloc([128, PLE], F32, "pt")) for _ in range(2)]
            pbf = [T(ar.alloc([128, PLE], BF16, "pbf")) for _ in range(2)]
            pT = [T(ar.alloc([128, 2, 128], BF16, "pT")) for _ in range(2)]
            ubf = [T(ar.alloc([128, D], BF16, "ubf")) for _ in range(2)]
            junk = T(ar.alloc([128, D], BF16, "junk"))
            uT = [T(ar.alloc([128, 8, 128], BF16, "uT")) for _ in range(2)]
            gate = [T(ar.alloc([128, D], F32, "gate")) for _ in range(2)]
            ofin = [T(ar.alloc([128, D], F32, "ofin")) for _ in range(2)]
            ss = [T(ar.alloc([128, 4], F32, "ss")) for _ in range(2)]
            ms = [T(ar.alloc([128, 4], F32, "ms")) for _ in range(2)]
            rstd = [T(ar.alloc([128, 4], F32, "rstd")) for _ in range(2)]
            st = {"pi": 0}
            head_banks = {}

            def load_ain(g):
                a_in = ain[g % 2]
                em("sync", lambda e: e.dma_start(
                    out=a_in.t[:], in_=AT[:, :, g * TG:(g + 1) * TG].rearrange("c p t -> p c t")),
                   reads=[dAT], writes=[a_in.b], dma=True)

            def head(ti):
                g, t = ti // TPG, ti % TPG
                a_in = ain[g % 2]
                if t == 0 and (g + 1) * TG < S:
                    load_ain(g + 1)
                r0 = ti * 128
                hb, p_ = ht[ti % 3], pt[ti % 2]
                em("sync", lambda e: e.dma_start(out=hb.t[:], in_=hbuf[r0:r0 + 128, :]),
                   reads=[dH[ti]], writes=[hb.b], dma=True)
                em("sync", lambda e: e.dma_start(out=p_.t[:], in_=p_in[li, r0:r0 + 128, :]), writes=[p_.b], dma=True)
                banks = []
                for half in range(2):
                    ps = PS[st["pi"] % 4]
                    st["pi"] += 1
                    banks.append(ps)

                    def pe(e, ps=ps, half=half):
                        ins = None
                        for fc in range(32):
                            ins = e.matmul(ps.t[:], a_in.t[:, fc, t * 128:(t + 1) * 128],
                                           wdn.t[:, fc, half * 512:(half + 1) * 512], start=(fc == 0), stop=(fc == 31))
                        return ins
                    em("tensor", pe, reads=[a_in.b] + wdn.rb(half * 512, (half + 1) * 512), writes=[ps.b])
                head_banks[ti] = banks

            def head_post(ti):
                hb = ht[ti % 3]
                pb, p_ = pbf[ti % 2], pt[ti % 2]
                em("gpsimd", lambda e: e.tensor_copy(out=pb.t[:], in_=p_.t[:]), reads=[p_.b], writes=[pb.b])
                for half, ps in enumerate(head_banks.pop(ti)):
                    em("vector", lambda e, ps=ps, half=half: e.tensor_tensor(
                        out=hb.t[:, half * 512:(half + 1) * 512], in0=ps.t[:],
                        in1=hb.t[:, half * 512:(half + 1) * 512], op=ALU.add),
                       reads=[ps.b, hb.b], writes=[hb.b])

            def tail_norm(ti):
                hb = ht[ti % 3]
                sidx = ti % 2
                rmsnorm_tile(hb.t[:], [hb.b], gam, ubf[ti % 2], ss[sidx], ms[sidx], rstd[sidx], junk, 0)

            def tail(ti):
                r0 = ti * 128
                hb = ht[ti % 3]
                ub = ubf[ti % 2]
                sidx = ti % 2
                u_t = uT[ti % 2]
                transpose_to(ub, 8, u_t, slice(0, 128))
                p_T = pT[ti % 2]
                transpose_to(pbf[ti % 2], 2, p_T, slice(0, 128))
                gt = gate[ti % 2]
                for half in range(2):
                    hs = slice(half * 512, (half + 1) * 512)
                    ps = PS[4 + (st["pi"] % 3)]
                    st["pi"] += 1

                    def pe(e, ps=ps, hs=hs):
                        ins = None
                        for kc in range(8):
                            ins = e.matmul(ps.t[:], u_t.t[:, kc, :], wgt.t[:, kc, hs], start=(kc == 0), stop=(kc == 7))
                        return ins
                    em("tensor", pe, reads=[u_t.b] + wgt.rb(hs.start, hs.stop), writes=[ps.b])
                    em("scalar", lambda e, ps=ps, hs=hs: e.activation(out=gt.t[:, hs], in_=ps.t[:], func=AF.Sigmoid),
                       reads=[ps.b], writes=[gt.b])
                    ps2 = PS[4 + (st["pi"] % 3)]
                    st["pi"] += 1

                    def pe2(e, ps2=ps2, hs=hs):
                        e.matmul(ps2.t[:], p_T.t[:, 0, :], wpl.t[:, 0, hs], start=True, stop=False)
                        return e.matmul(ps2.t[:], p_T.t[:, 1, :], wpl.t[:, 1, hs], start=False, stop=True)
                    em("tensor", pe2, reads=[p_T.b] + wpl.rb(hs.start, hs.stop), writes=[ps2.b])
                    em("vector", lambda e, ps2=ps2, hs=hs: e.tensor_tensor(
                        out=gt.t[:, hs], in0=ps2.t[:], in1=gt.t[:, hs], op=ALU.mult),
                       reads=[ps2.b, gt.b], writes=[gt.b])
                em("gpsimd", lambda e: e.tensor_tensor(out=hb.t[:], in0=hb.t[:], in1=gt.t[:], op=ALU.add),
                   reads=[hb.b, gt.b], writes=[hb.b])
                if final:
                    of = ofin[ti % 2]
                    em("scalar", lambda e: e.activation(out=junk.t[:], in_=hb.t[:], func=AF.Square,
                                                         accum_out=ss[sidx].t[:, 1:2]),
                       reads=[hb.b], writes=[junk.b, ss[sidx].b])
                    em("gpsimd", lambda e: e.tensor_scalar(
                        out=ms[sidx].t[:, 1:2], in0=ss[sidx].t[:, 1:2], scalar1=1.0 / D, scalar2=EPS,
                        op0=ALU.mult, op1=ALU.add), reads=[ss[sidx].b], writes=[ms[sidx].b])
                    em("gpsimd", lambda e: e.tensor_tensor(
                        out=rstd[sidx].t[:, 1:2], in0=ms[sidx].t[:, 1:2], in1=neghalf.t[:, 0:1], op=ALU.pow),
                       reads=[ms[sidx].b, neghalf.b], writes=[rstd[sidx].b])
                    em("vector", lambda e: e.scalar_tensor_tensor(
                        out=of.t[:], in0=hb.t[:], scalar=rstd[sidx].t[:, 1:2], in1=gamf.t[:], op0=ALU.mult, op1=ALU.mult),
                       reads=[hb.b, rstd[sidx].b, gamf.b], writes=[of.b])
                    em("sync", lambda e: e.dma_start(out=out[r0:r0 + 128, :], in_=of.t[:]),
                       reads=[of.b], writes=[dOUT], dma=True, disjoint=True)
                else:
                    em("sync", lambda e: e.dma_start(out=hbuf[r0:r0 + 128, :], in_=hb.t[:]),
                       reads=[hb.b], writes=[dH[ti]], dma=True)

            load_ain(0)
            head(0)
            head_post(0)
            for ti in range(ntile):
                if ti + 1 < ntile:
                    head(ti + 1)
                tail_norm(ti)
                if ti + 1 < ntile:
                    head_post(ti + 1)
                tail(ti)

        hsrc, hsrc_bufs = x_in, dX
        for li in layers:
            stage_A(li, hsrc, hsrc_bufs)
            if li % 3 == 2:
                stage_B_swa(li)
            elif li % 3 == 0:
                stage_B_sb(li)
            else:
                stage_B_fox(li)
            stage_CU(li, hsrc, hsrc_bufs)
            final = (li == 3)
            stage_CD(li, final)
            hsrc, hsrc_bufs = hbuf, dH
        if not last_is_final:
            em("sync", lambda e: e.dma_start(out=out, in_=hbuf), reads=dH, writes=[dOUT], dma=True, disjoint=True)
        sc.run()
    return nc


_CONSTS = None
_NC_CACHE = {}


def _get_nc(layers, S=4096, dbg=False):
    key = (tuple(layers), S, dbg)
    if key not in _NC_CACHE:
        _NC_CACHE[key] = build(list(layers), S=S, dbg=dbg)
    return _NC_CACHE[key]


def make_in_maps(inputs, xs, S=4096):
    global _CONSTS
    if _CONSTS is None:
        _CONSTS = host_consts()
    shared = {}
    for k, shp in WEIGHT_SPECS.items():
        shared[k] = np.ascontiguousarray(np.asarray(inputs[k], dtype=np.float32).reshape(shp))
    shared.update(_CONSTS)
    p = np.asarray(inputs["p"], dtype=np.float32)
    maps = []
    for c in range(len(xs)):
        m = dict(shared)
        m["x"] = np.ascontiguousarray(xs[c][:S])
        m["p"] = np.ascontiguousarray(p[:, c, :S, :])
        maps.append(m)
    return maps


def kernel(**inputs):
    x = np.asarray(inputs["x"], dtype=np.float32)
    B = x.shape[0]
    nc = _get_nc((0, 1, 2, 3))
    maps = make_in_maps(inputs, [x[c] for c in range(B)])
    res = run_bass_kernel_spmd(nc, maps, core_ids=list(range(B)))
    return np.stack([np.asarray(r["out"], dtype=np.float32) for r in res.results], axis=0)
```
